# Optimizing a Trainium2 kernel written in Bass

```python
import math
import jax, jax.numpy as jnp
from jax import lax
import numpy as np

D_MODEL = 1024
BATCH = 4
SEQ = 8192
DEPTH = 1

GLA_HEADS = 4
GLA_DV = D_MODEL // GLA_HEADS
GLA_DK = GLA_DV // 2
GLA_RANK = 16
GLA_TAU = 16.0
GLA_CHUNK = 64
SWA_HD = 64
SWA_HEADS = D_MODEL // SWA_HD
SWA_KV_HEADS = 2
SWA_WINDOW = 128
ROPE_DIM = SWA_HD // 4
ROPE_THETA = 500000.0
D_FF = 2816
CONV_W = 3
N_BRANCH = 2
LN_EPS = 1e-5
RMS_EPS = 1e-6
ALPHA = (2.0 * DEPTH) ** 0.25
BETA = (8.0 * DEPTH) ** -0.25
N_MOD = 6

SPLITS = (
    GLA_HEADS * GLA_DK,
    GLA_HEADS * GLA_DK,
    GLA_HEADS * GLA_DV,
    GLA_HEADS * GLA_DV,
    GLA_RANK,
    SWA_HEADS * SWA_HD,
    SWA_KV_HEADS * SWA_HD,
    SWA_KV_HEADS * SWA_HD,
    N_BRANCH * D_MODEL,
)
D_IN = sum(SPLITS)
SPLIT_POINTS = tuple(int(v) for v in np.cumsum(SPLITS)[:-1])

kernel_name = "hybrid_gla_swa_convffn_deepnorm_adaln"


def layer_norm(x, eps=LN_EPS):
    xf = x.astype(jnp.float32)
    mu = jnp.mean(xf, -1, keepdims=True)
    var = jnp.mean(jnp.square(xf - mu), -1, keepdims=True)
    return ((xf - mu) * lax.rsqrt(var + eps)).astype(x.dtype)


def layer_norm_affine(x, g, b):
    return layer_norm(x) * g + b


def rms_norm(x, g, eps=RMS_EPS):
    xf = x.astype(jnp.float32)
    y = xf * lax.rsqrt(jnp.mean(jnp.square(xf), -1, keepdims=True) + eps)
    return y.astype(x.dtype) * g


def partial_rope(t, cos, sin):
    half = ROPE_DIM // 2
    t1, t2, rest = t[..., :half], t[..., half:ROPE_DIM], t[..., ROPE_DIM:]
    return jnp.concatenate([t1 * cos - t2 * sin, t2 * cos + t1 * sin, rest], axis=-1)


def gla_chunked(q, k, v, log_a):
    B, S, H, DK = q.shape
    DV = v.shape[-1]
    n = S // GLA_CHUNK

    def to_chunks(t):
        return t.reshape(B, n, GLA_CHUNK, H, t.shape[-1]).transpose(1, 0, 3, 2, 4)

    qc, kc, vc, gc = map(to_chunks, (q, k, v, log_a))
    gcum = jnp.cumsum(gc.astype(jnp.float32), axis=3)
    g_last = gcum[..., -1:, :]
    g_mid = gcum[..., GLA_CHUNK // 2 - 1:GLA_CHUNK // 2, :]
    q_mid = qc * jnp.exp(gcum - g_mid)
    k_mid = kc * jnp.exp(g_mid - gcum)
    a_intra = jnp.einsum('nbhik,nbhjk->nbhij', q_mid, k_mid)
    causal = jnp.tril(jnp.ones((GLA_CHUNK, GLA_CHUNK), dtype=bool))
    a_intra = jnp.where(causal, a_intra, 0.0)
    o_intra = jnp.einsum('nbhij,nbhjv->nbhiv', a_intra, vc.astype(jnp.float32))
    q_in = qc * jnp.exp(gcum)
    k_out = kc * jnp.exp(g_last - gcum)
    decay = jnp.exp(g_last)

    def step(state, inp):
        q_i, k_o, v_i, dec = inp
        o = jnp.einsum('bhik,bhkv->bhiv', q_i, state)
        state = state * dec[:, :, 0, :, None] + jnp.einsum('bhjk,bhjv->bhkv', k_o, v_i.astype(jnp.float32))
        return state, o

    state0 = jnp.zeros((B, H, DK, DV), jnp.float32)
    _, o_inter = lax.scan(step, state0, (q_in, k_out, vc, decay))
    o = (o_intra + o_inter).astype(v.dtype)
    return o.transpose(1, 0, 3, 2, 4).reshape(B, S, H, DV)


def swa_with_sinks(q, k, v, sinks):
    B, S, HQ, hd = q.shape
    HKV = k.shape[2]
    G = HQ // HKV
    W = SWA_WINDOW
    n = S // W
    qb = q.reshape(B, n, W, HKV, G, hd)

    def band(t):
        tb = t.reshape(B, n, W, HKV, hd)
        prev = jnp.pad(tb[:, :-1], ((0, 0), (1, 0), (0, 0), (0, 0), (0, 0)))
        return jnp.concatenate([prev, tb], axis=2)

    kb, vb = band(k), band(v)
    s = jnp.einsum('bnqkgd,bnjkd->bkgnqj', qb, kb).astype(jnp.float32) * (hd ** -0.5)
    qi = jnp.arange(W)[:, None]
    kj = jnp.arange(2 * W)[None, :]
    diff = W + qi - kj
    win = (diff >= 0) & (diff < W)
    blk = jnp.arange(n)[:, None, None]
    mask = win[None] & ((blk > 0) | (kj >= W)[None])
    s = jnp.where(mask[None, None, None], s, -jnp.inf)
    sink = sinks.astype(jnp.float32).reshape(HKV, G)[None, :, :, None, None, None]
    m = jnp.maximum(jnp.max(s, -1, keepdims=True), sink)
    p = jnp.exp(s - m)
    p = p / (jnp.sum(p, -1, keepdims=True) + jnp.exp(sink - m))
    o = jnp.einsum('bkgnqj,bnjkd->bnqkgd', p.astype(v.dtype), vb)
    return o.reshape(B, S, HQ, hd)


def setup_inputs(seed: int = 0) -> dict:
    key = jax.random.key(seed)
    ks = jax.random.split(key, 20)
    f32 = jnp.float32
    L, D = DEPTH, D_MODEL
    nrm = lambda k, shape, s: jax.random.normal(k, shape, f32) * s
    return {
        "x": nrm(ks[0], (BATCH, SEQ, D), 1.0),
        "c": nrm(ks[1], (BATCH, D), 1.0),
        "positions": jnp.broadcast_to(jnp.arange(SEQ, dtype=jnp.int32), (BATCH, SEQ)),
        "w_ada": nrm(ks[2], (L, D, N_MOD * D), 0.5 * D ** -0.5),
        "b_ada": nrm(ks[3], (L, N_MOD * D), 0.01),
        "w_in": nrm(ks[4], (L, D, D_IN), D ** -0.5),
        "gla_w_lr": nrm(ks[5], (L, GLA_RANK, GLA_HEADS * GLA_DK), GLA_RANK ** -0.5),
        "gla_b_lr": nrm(ks[6], (L, GLA_HEADS * GLA_DK), 0.01),
        "gla_norm_g": 1.0 + nrm(ks[7], (L, GLA_DV), 0.01),
        "swa_sinks": nrm(ks[8], (L, SWA_HEADS), 1.0),
        "w_o": nrm(ks[9], (L, D, D), BETA * D ** -0.5),
        "ln1_g": 1.0 + nrm(ks[10], (L, D), 0.01),
        "ln1_b": nrm(ks[11], (L, D), 0.01),
        "w_up": nrm(ks[12], (L, D, 2 * D_FF), D ** -0.5),
        "conv_w": nrm(ks[13], (L, CONV_W, 2 * D_FF), CONV_W ** -0.5),
        "conv_b": nrm(ks[14], (L, 2 * D_FF), 0.01),
        "w_down": nrm(ks[15], (L, D_FF, D), BETA * D_FF ** -0.5),
        "ln2_g": 1.0 + nrm(ks[16], (L, D), 0.01),
        "ln2_b": nrm(ks[17], (L, D), 0.01),
    }


def reference(x, c, positions, w_ada, b_ada, w_in, gla_w_lr, gla_b_lr, gla_norm_g, swa_sinks,
              w_o, ln1_g, ln1_b, w_up, conv_w, conv_b, w_down, ln2_g, ln2_b):
    B, S, D = x.shape
    inv_freq = ROPE_THETA ** (-(jnp.arange(0, ROPE_DIM, 2, dtype=jnp.float32) / ROPE_DIM))
    ang = positions.astype(jnp.float32)[..., None] * inv_freq
    cos = jnp.cos(ang)[:, :, None, :].astype(x.dtype)
    sin = jnp.sin(ang)[:, :, None, :].astype(x.dtype)
    c_act = jax.nn.silu(c)

    for l in range(DEPTH):
        mod = c_act @ w_ada[l] + b_ada[l]
        shift1, scale1, gate1, shift2, scale2, gate2 = [m[:, None, :] for m in jnp.split(mod, N_MOD, axis=-1)]

        h = layer_norm(x) * (1.0 + scale1) + shift1
        proj = h @ w_in[l]
        qa, ka, va, ra, lra, qb, kb, vb, gates = jnp.split(proj, SPLIT_POINTS, axis=-1)

        log_a = jax.nn.log_sigmoid((lra @ gla_w_lr[l] + gla_b_lr[l]).astype(jnp.float32)) / GLA_TAU
        qa = qa.reshape(B, S, GLA_HEADS, GLA_DK) * (GLA_DK ** -0.5)
        ka = ka.reshape(B, S, GLA_HEADS, GLA_DK)
        va = va.reshape(B, S, GLA_HEADS, GLA_DV)
        log_a = log_a.reshape(B, S, GLA_HEADS, GLA_DK)
        o_a = gla_chunked(qa, ka, va, log_a)
        y_a = (rms_norm(o_a, gla_norm_g[l]) * jax.nn.silu(ra.reshape(B, S, GLA_HEADS, GLA_DV))).reshape(B, S, D)

        qb = partial_rope(qb.reshape(B, S, SWA_HEADS, SWA_HD), cos, sin)
        kb = partial_rope(kb.reshape(B, S, SWA_KV_HEADS, SWA_HD), cos, sin)
        vb = vb.reshape(B, S, SWA_KV_HEADS, SWA_HD)
        y_b = swa_with_sinks(qb, kb, vb, swa_sinks[l]).reshape(B, S, D)

        g_a, g_b = jnp.split(gates, N_BRANCH, axis=-1)
        y = jax.nn.sigmoid(g_a) * y_a + jax.nn.sigmoid(g_b) * y_b
        x = layer_norm_affine(ALPHA * x + gate1 * (y @ w_o[l]), ln1_g[l], ln1_b[l])

        h2 = layer_norm(x) * (1.0 + scale2) + shift2
        u = h2 @ w_up[l]
        up = jnp.pad(u, ((0, 0), (CONV_W - 1, 0), (0, 0)))
        u = conv_b[l] + sum(conv_w[l, t] * up[:, t:t + S] for t in range(CONV_W))
        u_gate, u_val = jnp.split(u, 2, axis=-1)
        f = jax.nn.gelu(u_gate, approximate=False) * u_val
        x = layer_norm_affine(ALPHA * x + gate2 * (f @ w_down[l]), ln2_g[l], ln2_b[l])

    return x
```

```python
import numpy as np
from contextlib import ExitStack
import concourse.bass as bass
import concourse.mybir as mybir
from concourse.bass_utils import run_bass_kernel_spmd

F32 = mybir.dt.float32
BF16 = mybir.dt.bfloat16
I32 = mybir.dt.int32
AF = mybir.ActivationFunctionType
ALU = mybir.AluOpType

D = 1024
KC = 8
DFF = 2816
NUP = 44
LN_EPS = 1e-5
RMS_EPS = 1e-6
ALPHA = 2.0 ** 0.25
import os
ROPE_ADD_ENG = os.environ.get("ROPE_ADD_ENG", "pool")
INTERLEAVE = os.environ.get("KINTER", "1") == "1"
FFN_RATIO = int(os.environ.get("KFFNR", "2"))
TWO_PI = 2.0 * np.pi
CW1 = 6.28125
CW2 = float(TWO_PI - CW1)


class Buf:
    __slots__ = ("name", "w", "r", "excl")

    def __init__(self, name, excl=False):
        self.name = name
        self.excl = excl
        self.w = None
        self.r = {}


class Op:
    __slots__ = ("eng", "fn", "deps", "inc", "dma_key", "dma_seq")


class Prog:
    ENGS = ("pe", "act", "dve", "pool", "sp")

    def __init__(self):
        self.ops = {e: [] for e in self.ENGS}
        self.dma_cnt = {}
        self.dma_last = {}
        self.frozen = False

    def add(self, eng, fn, reads=(), writes=(), dma_key=None):
        if self.frozen:
            return None
        op = Op()
        op.eng, op.fn, op.inc, op.dma_key, op.dma_seq = eng, fn, False, dma_key, 0
        idx = len(self.ops[eng])
        me = (eng, idx)
        deps = set()
        for b in reads:
            if b.w is not None:
                deps.add((b.w, True))
            if b.excl:
                for v in b.r.values():
                    if v[0] != eng:
                        deps.add((v, True))
        for b in writes:
            if b.w is not None:
                deps.add((b.w, False))
            for v in b.r.values():
                deps.add((v, False))
        if dma_key is not None:
            self.dma_cnt[dma_key] = self.dma_cnt.get(dma_key, 0) + 1
            op.dma_seq = self.dma_cnt[dma_key]
            if dma_key in self.dma_last:
                deps.add((self.dma_last[dma_key], True))
            self.dma_last[dma_key] = me
        final = set()
        raw = set(d for d, is_raw in deps if is_raw)
        for d, _ in deps:
            o = self.ops[d[0]][d[1]]
            if o.dma_key is None and d[0] == eng:
                if eng == "pe":
                    continue
            final.add(d)
            if o.dma_key is None:
                o.inc = True
        op.deps = final
        self.ops[eng].append(op)
        for b in reads:
            key = eng if dma_key is None else ("dma", eng, idx)
            b.r[key] = me
        for b in writes:
            b.w = me
            b.r = {}
        return me

    def emit(self, nc, block, stack):
        names = {"pe": "tensor", "act": "scalar", "dve": "vector", "pool": "gpsimd", "sp": "sync"}
        esem = {e: stack.enter_context(nc.semaphore("s_" + e)) for e in self.ENGS}
        dsem = {k: stack.enter_context(nc.semaphore("d_%s" % str(k))) for k in self.dma_cnt}
        cnt = {}
        for e in self.ENGS:
            c = 0
            lst = []
            for o in self.ops[e]:
                if o.inc:
                    c += 1
                lst.append(c)
            cnt[e] = lst
        prog = self

        def make(e):
            def body(engine):
                waited = {}
                for o in prog.ops[e]:
                    for d in sorted(o.deps):
                        od = prog.ops[d[0]][d[1]]
                        if od.dma_key is not None:
                            s, v = dsem[od.dma_key], 16 * od.dma_seq
                        else:
                            s, v = esem[d[0]], cnt[d[0]][d[1]]
                        k = id(s)
                        if waited.get(k, 0) >= v:
                            continue
                        waited[k] = v
                        engine.wait_ge(s, v)
                    ins = o.fn(engine)
                    if o.dma_key is not None:
                        ins.then_inc(dsem[o.dma_key], 16)
                    elif o.inc:
                        ins.then_inc(esem[e], 1)
                if e == "sp":
                    for k, last in prog.dma_last.items():
                        engine.wait_ge(dsem[k], 16 * prog.dma_cnt[k])
            return body

        for e in self.ENGS:
            getattr(block, names[e])(make(e))


class StopBuild(Exception):
    pass


class Cfg:
    stop = None

    def __init__(self, pre_widths, main_widths, nslab=4):
        self.pre = list(pre_widths)
        self.main = list(main_widths)
        self.npre = sum(self.pre)
        self.nmain = sum(self.main)
        self.nout = self.nmain - self.main[0]
        self.wmax = max(self.pre + self.main)
        self.nslab = nslab


FULL = Cfg([256] * 15 + [128], [128] + [256] * 16)


def build(cfg):
    nc = bass.Bass("TRN2", target_bir_lowering=False)
    pg = Prog()
    WM = cfg.wmax
    NSM = WM // 128

    def din(name, shape, dt=F32):
        return nc.dram_tensor(name, list(shape), dt, kind="ExternalInput").ap()

    def dscr(name, shape, dt=BF16):
        return nc.dram_tensor(name, list(shape), dt, kind="Internal").ap()

    x_pre = din("x_pre", [cfg.npre, D])
    x_main = din("x_main", [cfg.nmain, D])
    pos_pre = din("pos_pre", [1, cfg.npre], I32)
    pos_main = din("pos_main", [1, cfg.nmain], I32)
    flag_d = din("flag", [128, 1])
    c_col_d = din("c_col", [128, KC])
    w_ada_d = din("w_ada", [D, 6 * D])
    b_ada_col_d = din("b_ada_col", [128, 48])
    b_ada_row_d = din("b_ada_row", [1, 6 * D])
    w_fm_d = din("w_fm", [D, 2048])
    w_tm_d = din("w_tm", [D, 4096])
    w_sm_d = din("w_sm", [D, 272])
    w_lr_d = din("w_lr", [16, 512])
    b_lr_col_d = din("b_lr_col", [128, 4])
    gnorm_d = din("gnorm", [1, 256])
    sinks_d = din("sinks", [1, 16])
    w_o_d = din("w_o", [D, D])
    ln_rows_d = din("ln_rows", [4, D])
    w_up_d = din("w_up", [D, 2 * DFF])
    cw_col_d = din("cw_col", [128, 3 * NUP])
    cb_col_d = din("cb_col", [128, NUP])
    w_down_d = din("w_down", [DFF, D])
    consts_d = din("consts", [128, 5 * 128 + 2])
    out_d = nc.dram_tensor("out", [cfg.nout, D], F32, kind="ExternalOutput").ap()

    s_ada = dscr("s_ada", [D, 6 * D])
    s_fm = dscr("s_fm", [D, 2048])
    s_tm = dscr("s_tm", [D, 4096])
    s_sm = dscr("s_sm", [D, 272])
    s_o = dscr("s_o", [D, D])
    s_up = dscr("s_up", [D, 2 * DFF])
    s_down = dscr("s_down", [DFF, D])

    st = ExitStack()
    with st:
        def sb(name, shape, dt=F32):
            return st.enter_context(nc.sbuf_tensor(name, list(shape), dt))

        def op(eng, method, reads, writes, *a, **kw):
            return pg.add(eng, lambda e: getattr(e, method)(*a, **kw), reads, writes)

        banks = [st.enter_context(nc.psum_tensor("bank%d" % i, [128, 512], F32)) for i in range(8)]
        bank_bufs = [Buf("bank%d" % i, excl=True) for i in range(8)]
        bank_ctr = [0, 0]
        bank_pool = [[0, 1, 2, 3, 4], [5, 6, 7]]

        def bank(k=0):
            pool = bank_pool[k]
            i = pool[bank_ctr[k] % len(pool)]
            bank_ctr[k] += 1
            return banks[i], bank_bufs[i]

        xbufs = [sb("xbuf%d" % i, [128, NSM, D]) for i in range(2)]
        B_xs = [[Buf("x%d_%d" % (i, s)) for s in range(NSM)] for i in range(2)]
        xns = [sb("xn%d" % i, [128, NSM, D], BF16) for i in range(2)]
        B_xns = [[Buf("xn%d_%d" % (i, s)) for s in range(NSM)] for i in range(2)]
        hTs = [sb("hT%d" % i, [128, KC, WM], BF16) for i in range(2)]
        B_hTs = [[Buf("hT%d_%d" % (i, kc)) for kc in range(KC)] for i in range(2)]
        slabs = [sb("slab%d" % i, [128, KC, 512], BF16) for i in range(cfg.nslab)]
        B_slab = [Buf("slab%d" % i) for i in range(cfg.nslab)]
        cst = sb("cst", [128, 5 * 128 + 2]); B_cst = Buf("cst")
        ident_bf = sb("ident_bf", [128, 128], BF16)
        perm_bf = sb("perm_bf", [128, 128], BF16)
        mcur_bf = sb("mcur_bf", [128, 128], BF16)
        mprev_bf = sb("mprev_bf", [128, 128], BF16)
        B_cbf = Buf("cbf")
        flag = sb("flag_sb", [128, 1]); B_flag = Buf("flag")
        c_col = sb("c_col_sb", [128, KC]); B_ccol = Buf("ccol")
        sc_bf = sb("sc_bf", [128, KC], BF16)
        B_sc = Buf("sc")
        b_ada_col = sb("b_ada_col_sb", [128, 48]); B_bac = Buf("bac")
        modcol = sb("modcol", [128, 4, KC]); B_mod = Buf("mod")
        gate_bc = sb("gate_bc", [128, 2, D]); B_gate = Buf("gate")
        ln_bc = sb("ln_bc", [128, 4, D]); B_lnbc = Buf("lnbc")
        w_lr = sb("w_lr_sb", [16, 512]); B_wlr = Buf("wlr")
        negb = sb("negb", [128, 4]); B_negb = Buf("negb")
        gnorm_bc = sb("gnorm_bc", [128, 256]); B_gn = Buf("gn")
        esink = sb("esink", [128, 16]); B_es = Buf("es")
        cw_col = sb("cw_col_sb", [128, 3 * NUP]); cb_col = sb("cb_col_sb", [128, NUP]); B_cw = Buf("cw")
        w_sm = sb("w_sm_sb", [128, KC, 272], BF16); B_wsm = Buf("wsm")
        statss = [sb("stats%d" % i, [128, NSM, 2, 6]) for i in range(2)]; mvs = [sb("mv%d" % i, [128, NSM, 2]) for i in range(2)]
        lnvs = [sb("lnv%d" % i, [128, NSM]) for i in range(2)]; rstds = [sb("rstd%d" % i, [128, NSM]) for i in range(2)]
        nbiass = [sb("nbias%d" % i, [128, NSM]) for i in range(2)]
        B_lns = [Buf("lnscratch0"), Buf("lnscratch1")]
        posi = sb("posi", [128, WM], I32); angf = sb("angf", [128, WM])
        angk = sb("angk", [128, WM], I32)
        cosT = sb("cosT", [128, WM]); sinT = sb("sinT", [128, WM])
        B_pos = Buf("pos"); B_ang = Buf("ang"); B_cs = Buf("cossin")
        class Rot:
            def __init__(self, name, shape, dt, n):
                self.t = [sb("%s_%d" % (name, i), shape, dt) for i in range(n)]
                self.b = [Buf("%s_%d" % (name, i)) for i in range(n)]
                self.i = 0

            def next(self):
                k = self.i % len(self.t)
                self.i += 1
                return self.t[k], self.b[k]

        R_rtb = Rot("rope_tb", [128, WM], BF16, 1); R_rt1 = Rot("rope_t1", [128, WM], F32, 2); R_rt2 = Rot("rope_t2", [128, WM], F32, 2)
        lraTs = [sb("lraT%d" % i, [16, WM]) for i in range(2)]; B_lras = [Buf("lra0"), Buf("lra1")]
        R_et = Rot("etmp", [128, WM], F32, 1); R_lt = Rot("ltmp", [128, WM], F32, 1); R_Lt = Rot("Ltmp", [128, WM], F32, 1)
        E1 = sb("E1", [128, 4, WM]); E2 = sb("E2", [128, 4, WM]); B_E1 = [Buf("E1_%d" % h) for h in range(4)]
        B_E2 = [Buf("E2_%d" % h) for h in range(4)]
        c_last = sb("c_last", [128, 4, NSM]); c_mid = sb("c_mid", [128, 4, NSM]); c_midn = sb("c_midn", [128, 4, NSM])
        B_cs_h = [Buf("csm%d" % h) for h in range(4)]
        B_cs2_h = [Buf("csm2_%d" % h) for h in range(4)]
        R_ft = Rot("ftmp", [128, WM], F32, 1); R_ft2 = Rot("ftmp2", [128, WM], F32, 1)
        qin = sb("qin", [128, 4, WM], BF16); qmid = sb("qmid", [128, 4, WM], BF16)
        kmid = sb("kmid", [128, 4, WM], BF16); kout = sb("kout", [128, 4, WM], BF16)
        B_qin = [Buf("qin%d" % h) for h in range(4)]; B_qmid = [Buf("qmid%d" % h) for h in range(4)]
        B_kmid = [Buf("kmid%d" % h) for h in range(4)]; B_kout = [Buf("kout%d" % h) for h in range(4)]
        qTb = sb("qTb", [128, 8, WM], BF16); B_qTb = [Buf("qTb%d" % c) for c in range(8)]
        kTb = sb("kTb", [128, 128 + WM], BF16); B_kTb = Buf("kTb")
        va = sb("va", [128, NSM, D], BF16); r1 = sb("r1", [128, NSM, D], BF16)
        sa = sb("sa", [128, NSM, D], BF16); sbg = sb("sbg", [128, NSM, D], BF16)
        B_va = [Buf("va%d" % s) for s in range(NSM)]; B_r1 = [Buf("r1_%d" % s) for s in range(NSM)]
        B_sa = [Buf("sa%d" % s) for s in range(NSM)]; B_sbg = [Buf("sbg%d" % s) for s in range(NSM)]
        R_sil = Rot("siltmp", [128, 512], F32, 1)
        vbx = sb("vbx", [128, 1 + NSM, 2, 65], BF16); B_vbx = [Buf("vbx%d" % s) for s in range(1 + NSM)]
        AT = sb("AT", [128, 4, 128], BF16); B_AT = Buf("AT")
        koutTM = sb("koutTM", [128, 4, 128], BF16); B_kTM = Buf("kTM")
        S = sb("S", [128, 4, 256]); Sb = sb("Sb", [128, 4, 256], BF16)
        B_S = [Buf("S%d" % h) for h in range(4)]; B_Sb = [Buf("Sb%d" % h) for h in range(4)]
        Pt = sb("Pt", [128, 8, 512], BF16); B_P = [Buf("P%d" % i) for i in range(8)]
        R_sqj = Rot("sqj", [128, 256], F32, 1)
        ssq = sb("ssq", [128, 4]); lnms = sb("lnms", [128, 4]); rstd_a = sb("rstd_a", [128, 4]); B_rms = Buf("rms")
        ya = sb("ya", [128, D]); B_ya = Buf("ya")
        sc_rep = ya[:, 0:512].bitcast(BF16).rearrange("p (k m) -> p k m", k=KC)
        tbv = sb("tbv", [128, D]); B_tb = Buf("tb")
        den = sb("den", [128, 16]); rden = sb("rden", [128, 16]); B_den = Buf("den")
        ybf = sb("ybf", [128, D], BF16); B_y = Buf("y")
        yT = sb("yT", [128, KC, 128], BF16); B_yT = Buf("yT")
        tres = ya; B_tres = B_ya
        fbuf = sb("fbuf", [128, 22, WM], BF16); B_f = [Buf("f%d" % j) for j in range(11)]
        acc = sb("acc", [128, 4, WM]); B_acc = [Buf("acc%d" % e) for e in range(4)]
        gtmp = sb("gtmp", [128, 2, WM]); B_gt = [Buf("gt%d" % e) for e in range(2)]
        R_u = Rot("ubuf", [128, WM], F32, 2)
        halo = sb("halo", [128, NUP, 2]); halo_new = sb("halo_new", [128, NUP, 2]); B_halo = Buf("halo"); B_halon = Buf("halon")

        ident_f = cst[:, 0:128]
        rmask128 = cst[:, 512:640]
        freq_col = cst[:, 640:641]
        sign_col = cst[:, 641:642]

        dma_rr = [0]

        def dma(eng, out, in_, reads, writes, key):
            return pg.add(eng, lambda e: e.dma_start(out=out, in_=in_), reads, writes, dma_key=key)

        B_scr = {n: Buf("scr_" + n) for n in ("ada", "fm", "tm", "sm", "o", "up", "down")}
        for n, (src, dst) in {"sm": (w_sm_d, s_sm), "ada": (w_ada_d, s_ada), "fm": (w_fm_d, s_fm),
                              "tm": (w_tm_d, s_tm), "o": (w_o_d, s_o), "up": (w_up_d, s_up),
                              "down": (w_down_d, s_down)}.items():
            rows = src.shape[0]
            for r0 in range(0, rows, 256):
                dma("pool", dst[r0:r0 + 256, :], src[r0:r0 + 256, :], [], [B_scr[n]], "cast_" + n)

        dma("sp", cst[:, :], consts_d[:, :], [], [B_cst], "c0")
        dma("sp", flag[:, :], flag_d[:, :], [], [B_flag], "c1")
        dma("sp", c_col[:, :], c_col_d[:, :], [], [B_ccol], "c2")
        dma("sp", b_ada_col[:, :], b_ada_col_d[:, :], [], [B_bac], "c3")
        dma("sp", w_lr[:, :], w_lr_d[:, :], [], [B_wlr], "c0")
        dma("sp", negb[:, :], b_lr_col_d[:, :], [], [B_negb], "c1")
        dma("sp", gnorm_bc[:, :], gnorm_d.partition_broadcast(128), [], [B_gn], "c2")
        dma("sp", esink[:, :], sinks_d.partition_broadcast(128), [], [B_es], "c3")
        dma("sp", cw_col[:, :], cw_col_d[:, :], [], [B_cw], "c0")
        dma("sp", cb_col[:, :], cb_col_d[:, :], [], [B_cw], "c1")
        for i in range(4):
            dma("sp", ln_bc[:, i, :], ln_rows_d[i:i + 1, :].partition_broadcast(128), [], [B_lnbc], "c2")
        for i, v in enumerate((2, 5)):
            dma("sp", gate_bc[:, i, :], b_ada_row_d[:, v * D:(v + 1) * D].partition_broadcast(128), [], [B_gate], "c3")
        dma("sp", w_sm[:, :, :], s_sm.rearrange("(k p) c -> p k c", p=128), [B_scr["sm"]], [B_wsm], "c0")

        op("dve", "tensor_copy", [B_cst], [B_cbf], out=ident_bf[:, :], in_=cst[:, 0:128])
        op("dve", "tensor_copy", [B_cst], [B_cbf], out=perm_bf[:, :], in_=cst[:, 128:256])
        op("dve", "tensor_copy", [B_cst], [B_cbf], out=mcur_bf[:, :], in_=cst[:, 256:384])
        op("dve", "tensor_copy", [B_cst], [B_cbf], out=mprev_bf[:, :], in_=cst[:, 384:512])
        op("dve", "tensor_scalar", [B_negb], [B_negb], out=negb[:, :], in0=negb[:, :], scalar1=-1.0, scalar2=None, op0=ALU.mult)
        op("act", "activation", [B_es], [B_es], out=esink[:, :], in_=esink[:, :], func=AF.Exp)
        op("pool", "memset", [], B_S, S[:, :, :], 0.0)
        op("pool", "memset", [], B_Sb, Sb[:, :, :], 0.0)
        op("pool", "memset", [], B_vbx, vbx[:, :, :, :], 1.0)
        op("pool", "memset", [], [B_halo], halo[:, :, :], 0.0)
        op("pool", "memset", [], [B_kTb], kTb[:, :], 0.0)
        op("act", "activation", [B_ccol], [B_ccol], out=c_col[:, :], in_=c_col[:, :], func=AF.Silu)
        op("dve", "tensor_copy", [B_ccol], [B_sc], out=sc_bf[:, :], in_=c_col[:, :])
        op("dve", "tensor_copy", [B_ccol], [B_sc, B_ya], out=sc_rep[:, :, :],
           in_=c_col[:, :].unsqueeze(2).to_broadcast([128, KC, 128]))

        def ckpt(name):
            if cfg.stop == name:
                pg.frozen = True

        slab_rr = [0, 0]
        slab_pool = [list(range(0, cfg.nslab - cfg.nslab // 2)), list(range(cfg.nslab - cfg.nslab // 2, cfg.nslab))]

        def load_slab(scr, bscr, k0, nk, c0, ncols=512, k=0):
            pool = slab_pool[k]
            i = pool[slab_rr[k] % len(pool)]
            slab_rr[k] += 1
            src = scr.rearrange("(k p) c -> p k c", p=128)[:, k0:k0 + nk, c0:c0 + ncols]
            dma("sp", slabs[i][:, 0:nk, 0:ncols], src, [bscr], [B_slab[i]], "slab%d" % i)
            return slabs[i], B_slab[i]

        colslot = {0: 0, 1: 1, 3: 2, 4: 3}
        for v in range(6):
            for hh in range(2):
                sl, bsl = load_slab(s_ada, B_scr["ada"], 0, KC, v * D + hh * 512)
                if v in colslot:
                    bk, bb = bank()
                    for cc in range(4):
                        for kc in range(KC):
                            op("pe", "matmul", [bsl, B_sc], [bb], bk[:, cc:cc + 1], lhsT=sl[:, kc, cc * 128:(cc + 1) * 128],
                               rhs=sc_bf[:, kc:kc + 1], start=(kc == 0), stop=(kc == KC - 1))
                    j = colslot[v]
                    op("dve", "tensor_tensor", [bb, B_bac], [B_mod], out=modcol[:, j, hh * 4:hh * 4 + 4], in0=bk[:, 0:4],
                       in1=b_ada_col[:, v * 8 + hh * 4: v * 8 + hh * 4 + 4], op=ALU.add)
                else:
                    g = 0 if v == 2 else 1
                    bk, bb = bank()
                    for kc in range(KC):
                        op("pe", "matmul", [bsl, B_sc, B_ya], [bb], bk[:, :], lhsT=sc_rep[:, kc, :], rhs=sl[:, kc, :],
                           start=(kc == 0), stop=(kc == KC - 1))
                    op("dve", "tensor_tensor", [bb, B_gate], [B_gate], out=gate_bc[:, g, hh * 512:(hh + 1) * 512], in0=bk[:, :],
                       in1=gate_bc[:, g, hh * 512:(hh + 1) * 512], op=ALU.add)
        for j in (1, 3):
            op("dve", "tensor_scalar", [B_mod], [B_mod], out=modcol[:, j, :], in0=modcol[:, j, :], scalar1=1.0, scalar2=None, op0=ALU.add)

        def ln_stats(srcs, nsub, k):
            stats, mv, lnv, rstd, nbias, B_ln = statss[k], mvs[k], lnvs[k], rstds[k], nbiass[k], B_lns[k]
            for s, (ap, b) in enumerate(srcs):
                for hlf in range(2):
                    op("dve", "bn_stats", [b], [B_ln], out=stats[:, s, hlf, :], in_=ap[:, hlf * 512:(hlf + 1) * 512])
                op("dve", "bn_aggr", [B_ln], [B_ln], out=mv[:, s, :], in_=stats[:, s, :, :].rearrange("p a b -> p (a b)"))
            op("act", "activation", [B_ln, B_eps], [B_ln], out=lnv[:, 0:nsub], in_=mv[:, 0:nsub, 1], func=AF.Ln, bias=eps_ln[:, 0:1], scale=1.0)
            op("act", "activation", [B_ln], [B_ln], out=rstd[:, 0:nsub], in_=lnv[:, 0:nsub], func=AF.Exp, scale=-0.5)
            op("dve", "scalar_tensor_tensor", [B_ln], [B_ln], out=nbias[:, 0:nsub], in0=mv[:, 0:nsub, 0], scalar=-1.0,
               in1=rstd[:, 0:nsub], op0=ALU.mult, op1=ALU.mult)

        epsc = sb("epsc", [128, 4]); B_eps = Buf("eps")
        op("pool", "memset", [], [B_eps], epsc[:, 0:1], LN_EPS)
        op("pool", "memset", [], [B_eps], epsc[:, 1:2], RMS_EPS)
        op("pool", "memset", [], [B_eps], epsc[:, 2:3], 1.0)
        eps_ln = epsc[:, 0:1]
        eps_rms = epsc[:, 1:2]
        one_col = epsc[:, 2:3]

        def to_featmajor(srcs, nsub, W, jmod, k):
            ln_stats(srcs, nsub, k)
            xn, B_xn, hT, B_hT, rstd, nbias, B_ln = xns[k], B_xns[k], hTs[k], B_hTs[k], rstds[k], nbiass[k], B_lns[k]
            for s, (ap, b) in enumerate(srcs):
                op("act", "activation", [b, B_ln], [B_xn[s]], out=xn[:, s, :], in_=ap, func=AF.Identity,
                   scale=rstd[:, s:s + 1], bias=nbias[:, s:s + 1])
            for kc in range(KC):
                bk, bb = bank(k)
                bkb = bk[:, :].bitcast(BF16)
                for s in range(nsub):
                    op("pe", "transpose", [B_xn[s], B_cbf], [bb], bkb[:, s * 128:(s + 1) * 128],
                       xn[:, s, kc * 128:(kc + 1) * 128], ident_bf[:, :])
                if kc % 2 == 0:
                    op("act", "activation", [bb, B_mod], [B_hT[kc]], out=hT[:, kc, 0:W], in_=bkb[:, 0:W], func=AF.Identity,
                       scale=modcol[:, jmod + 1, kc:kc + 1], bias=modcol[:, jmod, kc:kc + 1])
                else:
                    op("dve", "tensor_scalar", [bb, B_mod], [B_hT[kc]], out=hT[:, kc, 0:W], in0=bkb[:, 0:W],
                       scalar1=modcol[:, jmod + 1, kc:kc + 1], scalar2=modcol[:, jmod, kc:kc + 1], op0=ALU.mult, op1=ALU.add)

        def rope_tables(pos_ap, W):
            angt, angkf = R_et.t[0], R_lt.t[0]
            BA = [B_ang, R_et.b[0], R_lt.b[0]]
            dma("sp", posi[:, 0:W], pos_ap.partition_broadcast(128), [], [B_pos], "pos")
            op("dve", "tensor_copy", [B_pos], BA, out=angf[:, 0:W], in_=posi[:, 0:W])
            op("dve", "tensor_scalar", BA + [B_cst], BA, out=angf[:, 0:W], in0=angf[:, 0:W], scalar1=freq_col, scalar2=None, op0=ALU.mult)
            op("dve", "tensor_scalar", BA, BA, out=angt[:, 0:W], in0=angf[:, 0:W], scalar1=float(1.0 / TWO_PI), scalar2=None, op0=ALU.mult)
            op("dve", "tensor_copy", BA, BA, out=angk[:, 0:W], in_=angt[:, 0:W])
            op("dve", "tensor_copy", BA, BA, out=angkf[:, 0:W], in_=angk[:, 0:W])
            op("dve", "scalar_tensor_tensor", BA, BA, out=angf[:, 0:W], in0=angkf[:, 0:W], scalar=-CW1, in1=angf[:, 0:W], op0=ALU.mult, op1=ALU.add)
            op("dve", "scalar_tensor_tensor", BA, BA, out=angf[:, 0:W], in0=angkf[:, 0:W], scalar=-CW2, in1=angf[:, 0:W], op0=ALU.mult, op1=ALU.add)
            for shift, dst, use_sign in ((0.0, sinT, True), (float(np.pi / 2), cosT, False)):
                src = angf
                if shift != 0.0:
                    op("dve", "tensor_scalar", BA, BA, out=angkf[:, 0:W], in0=angf[:, 0:W], scalar1=shift, scalar2=None, op0=ALU.add)
                    src = angkf
                for _ in range(2):
                    op("dve", "tensor_scalar", BA, BA, out=angt[:, 0:W], in0=src[:, 0:W], scalar1=float(np.pi), scalar2=float(-TWO_PI), op0=ALU.is_gt, op1=ALU.mult)
                    op("dve", "tensor_tensor", BA, BA, out=src[:, 0:W], in0=src[:, 0:W], in1=angt[:, 0:W], op=ALU.add)
                    op("dve", "tensor_scalar", BA, BA, out=angt[:, 0:W], in0=src[:, 0:W], scalar1=float(-np.pi), scalar2=float(TWO_PI), op0=ALU.is_lt, op1=ALU.mult)
                    op("dve", "tensor_tensor", BA, BA, out=src[:, 0:W], in0=src[:, 0:W], in1=angt[:, 0:W], op=ALU.add)
                if use_sign:
                    op("act", "activation", BA + [B_cst], [B_cs], out=dst[:, 0:W], in_=src[:, 0:W], func=AF.Sin, scale=sign_col)
                else:
                    op("act", "activation", BA, [B_cs], out=dst[:, 0:W], in_=src[:, 0:W], func=AF.Sin)

        def rope(bk, bb, W, dst_ap, dst_buf):
            rope_tb, B_rtb = R_rtb.next(); rope_t1, B_rt1 = R_rt1.next(); rope_t2, B_rt2 = R_rt2.next()
            op("act", "activation", [bb], [B_rtb], out=rope_tb[:, 0:W], in_=bk[:, 0:W], func=AF.Copy)
            b2, bb2 = bank()
            op("pe", "matmul", [B_rtb, B_cbf], [bb2], b2[:, 0:W], lhsT=perm_bf[:, :], rhs=rope_tb[:, 0:W], start=True, stop=True)
            op("dve", "tensor_tensor", [bb, B_cs], [B_rt1], out=rope_t1[:, 0:W], in0=bk[:, 0:W], in1=cosT[:, 0:W], op=ALU.mult)
            op("dve", "tensor_tensor", [bb2, B_cs], [B_rt2], out=rope_t2[:, 0:W], in0=b2[:, 0:W], in1=sinT[:, 0:W], op=ALU.mult)
            op(ROPE_ADD_ENG, "tensor_tensor", [B_rt1, B_rt2], [dst_buf], out=dst_ap, in0=rope_t1[:, 0:W], in1=rope_t2[:, 0:W], op=ALU.add)

        def proj_fm(sl, bsl, c0, W, k=0, nrows=128):
            bk, bb = bank(k)
            for kc in range(KC):
                op("pe", "matmul", [bsl, B_hTs[k][kc]], [bb], bk[0:nrows, 0:W], lhsT=sl[:, kc, c0:c0 + nrows], rhs=hTs[k][:, kc, 0:W],
                   start=(kc == 0), stop=(kc == KC - 1))
            return bk, bb

        def proj_tm(sl, bsl, s, ncols=512, k=0):
            bk, bb = bank(k)
            for kc in range(KC):
                op("pe", "matmul", [bsl, B_hTs[k][kc]], [bb], bk[:, 0:ncols], lhsT=hTs[k][:, kc, s * 128:(s + 1) * 128], rhs=sl[:, kc, 0:ncols],
                   start=(kc == 0), stop=(kc == KC - 1))
            return bk, bb

        def mixer_gen(main, x_src, pos_ap, W, par, first_main_after_warm, last_pre, tag='', k=0):
            nsub = W // 128
            xbuf, B_x = xbufs[par], B_xs[par]
            hT, B_hT = hTs[k], B_hTs[k]
            lraT, B_lra = lraTs[k], B_lras[k]
            E2x, B_E2x = (E2, E1)[k], (B_E2, B_E1)[k]
            c_lastx, B_csx = (c_last, c_mid)[k], (B_cs_h, B_cs2_h)[k]
            koutx, B_koutx = (kout, kmid)[k], (B_kout, B_kmid)[k]
            vax, B_vax = (va, r1)[k], (B_va, B_r1)[k]
            kTMx, B_kTMx = (koutTM, AT)[k], (B_kTM, B_AT)[k]
            need_b = main or last_pre
            dma("sp", xbuf[:, 0:nsub, :], x_src.rearrange("(s p) d -> p s d", p=128), [], B_x[0:nsub], "x%d" % par)
            if main:
                op("pool", "tensor_copy", [B_kTb], [B_kTb], out=kTb[:, 0:128], in_=kTb[:, prev_w[0]:prev_w[0] + 128])
                op("pool", "tensor_copy", [B_vbx[prev_w[0] // 128]], [B_vbx[0]], out=vbx[:, 0, :, :], in_=vbx[:, prev_w[0] // 128, :, :])
            if first_main_after_warm:
                op("pool", "tensor_scalar", [B_vbx[0], B_flag], [B_vbx[0]], out=vbx[:, 0, :, :], in0=vbx[:, 0, :, :], scalar1=flag[:, 0:1], scalar2=None, op0=ALU.mult)
                for h in range(4):
                    op("pool", "tensor_scalar", [B_S[h], B_flag], [B_S[h]], out=S[:, h, :], in0=S[:, h, :], scalar1=flag[:, 0:1], scalar2=None, op0=ALU.mult)
                    op("pool", "tensor_copy", [B_S[h]], [B_Sb[h]], out=Sb[:, h, :], in_=S[:, h, :])
            prev_w[0] = W
            if need_b:
                rope_tables(pos_ap, W)
            yield
            ckpt(tag + ':A')
            to_featmajor([(xbuf[:, s, :], B_x[s]) for s in range(nsub)], nsub, W, 0, k)
            ckpt(tag + ':B')
            yield
            bk, bb = bank(k)
            for kc in range(KC):
                op("pe", "matmul", [B_wsm, B_hT[kc]], [bb], bk[0:16, 0:W], lhsT=w_sm[:, kc, 0:16], rhs=hT[:, kc, 0:W], start=(kc == 0), stop=(kc == KC - 1))
            op("act", "activation", [bb], [B_lra], out=lraT[:, 0:W], in_=bk[0:16, 0:W], func=AF.Copy)
            for h in range(4):
                bk, bb = bank(k)
                etmp, B_et = R_et.next(); ltmp, B_lt = R_lt.next(); Ltmp, B_Lt = R_Lt.next()
                op("pe", "matmul", [B_wlr, B_lra], [bb], bk[:, 0:W], lhsT=w_lr[:, h * 128:(h + 1) * 128], rhs=lraT[:, 0:W], start=True, stop=True)
                op("act", "activation", [bb, B_negb], [B_et], out=etmp[:, 0:W], in_=bk[:, 0:W], func=AF.Exp, scale=-1.0, bias=negb[:, h:h + 1])
                op("act", "activation", [B_et, B_eps], [B_lt], out=ltmp[:, 0:W], in_=etmp[:, 0:W], func=AF.Ln, bias=one_col, scale=1.0)
                for s in range(nsub):
                    op("dve", "tensor_tensor_scan", [B_lt, B_cst], [B_Lt], out=Ltmp[:, s * 128:(s + 1) * 128], data0=rmask128,
                       data1=ltmp[:, s * 128:(s + 1) * 128], initial=0.0, op0=ALU.mult, op1=ALU.add)
                op("act", "activation", [B_Lt], [B_E2x[h]], out=E2x[:, h, 0:W], in_=Ltmp[:, 0:W], func=AF.Exp, scale=1.0 / 16.0)
                L3 = Ltmp[:, 0:W].rearrange("p (s t) -> p s t", t=128)
                op("act", "activation", [B_Lt], [B_csx[h]], out=c_lastx[:, h, 0:nsub], in_=L3[:, :, 127], func=AF.Exp, scale=-1.0 / 16.0)
                if main:
                    op("act", "activation", [B_Lt], [B_E1[h]], out=E1[:, h, 0:W], in_=Ltmp[:, 0:W], func=AF.Exp, scale=-1.0 / 16.0)
                    op("act", "activation", [B_Lt], [B_cs_h[h]], out=c_mid[:, h, 0:nsub], in_=L3[:, :, 63], func=AF.Exp, scale=-1.0 / 16.0)
                    op("act", "activation", [B_Lt], [B_cs_h[h]], out=c_midn[:, h, 0:nsub], in_=L3[:, :, 63], func=AF.Exp, scale=1.0 / 16.0)
            ckpt(tag + ':C1')
            yield
            if need_b:
              bk, bb = bank(k)
              for kc in range(KC):
                op("pe", "matmul", [B_wsm, B_hT[kc]], [bb], bk[:, 0:W], lhsT=w_sm[:, kc, 16:144], rhs=hT[:, kc, 0:W], start=(kc == 0), stop=(kc == KC - 1))
              rope(bk, bb, W, kTb[:, 128:128 + W], B_kTb)
            ckpt(tag + ':C2')
            if main:
                sl, bsl = load_slab(s_fm, B_scr["fm"], 0, KC, 0)
                for h in range(4):
                    bk, bb = proj_fm(sl, bsl, h * 128, W, k)
                    ftmp, B_ft = R_ft.next()
                    op("dve", "scalar_tensor_tensor", [bb, B_E1[h]], [B_ft], out=ftmp[:, 0:W], in0=bk[:, 0:W], scalar=float(128 ** -0.5),
                       in1=E1[:, h, 0:W], op0=ALU.mult, op1=ALU.mult)
                    op("pool", "tensor_copy", [B_ft], [B_qin[h]], out=qin[:, h, 0:W], in_=ftmp[:, 0:W])
                    op("pool", "tensor_tensor", [B_ft, B_cs_h[h]], [B_qmid[h]], out=qmid[:, h, 0:W].rearrange("p (s t) -> p s t", t=128),
                       in0=ftmp[:, 0:W].rearrange("p (s t) -> p s t", t=128),
                       in1=c_midn[:, h, 0:nsub].unsqueeze(2).to_broadcast([128, nsub, 128]), op=ALU.mult)
            ckpt(tag + ':C3')
            yield
            sl, bsl = load_slab(s_fm, B_scr["fm"], 0, KC, 512, k=k)
            for h in range(4):
                bk, bb = proj_fm(sl, bsl, h * 128, W, k)
                ftmp2, B_ft2 = R_ft2.next()
                op("dve", "tensor_tensor", [bb, B_E2x[h]], [B_ft2], out=ftmp2[:, 0:W], in0=bk[:, 0:W], in1=E2x[:, h, 0:W], op=ALU.mult)
                op("pool", "tensor_tensor", [B_ft2, B_csx[h]], [B_koutx[h]], out=koutx[:, h, 0:W].rearrange("p (s t) -> p s t", t=128),
                   in0=ftmp2[:, 0:W].rearrange("p (s t) -> p s t", t=128),
                   in1=c_lastx[:, h, 0:nsub].unsqueeze(2).to_broadcast([128, nsub, 128]), op=ALU.mult)
                if main:
                    op("pool", "tensor_tensor", [B_ft2, B_cs_h[h]], [B_kmid[h]], out=kmid[:, h, 0:W].rearrange("p (s t) -> p s t", t=128),
                       in0=ftmp2[:, 0:W].rearrange("p (s t) -> p s t", t=128),
                       in1=c_mid[:, h, 0:nsub].unsqueeze(2).to_broadcast([128, nsub, 128]), op=ALU.mult)
            ckpt(tag + ':C4')
            yield
            if main:
                for j in range(2):
                    sl, bsl = load_slab(s_fm, B_scr["fm"], 0, KC, 1024 + j * 512)
                    for cc in range(4):
                        c = j * 4 + cc
                        bk, bb = proj_fm(sl, bsl, cc * 128, W)
                        ckpt(tag + ':Q%dp' % c)
                        rope(bk, bb, W, qTb[:, c, 0:W], B_qTb[c])
                        ckpt(tag + ':Q%d' % c)
                    yield
            ckpt(tag + ':C')
            yield
            for j in range(2):
                sl, bsl = load_slab(s_tm, B_scr["tm"], 0, KC, j * 512, k=k)
                for s in range(nsub):
                    bk, bb = proj_tm(sl, bsl, s, 512, k)
                    op("act", "activation", [bb], [B_vax[s]], out=vax[:, s, j * 512:(j + 1) * 512], in_=bk[:, :], func=AF.Copy)
            yield
            for s in (range(nsub) if need_b else ()):
                bk, bb = bank(k)
                for kc in range(KC):
                    op("pe", "matmul", [B_wsm, B_hT[kc]], [bb], bk[:, 0:128], lhsT=hT[:, kc, s * 128:(s + 1) * 128], rhs=w_sm[:, kc, 144:272],
                       start=(kc == 0), stop=(kc == KC - 1))
                op("dve", "tensor_copy", [bb], [B_vbx[1 + s]], out=vbx[:, 1 + s, :, 0:64], in_=bk[:, 0:128].rearrange("p (g d) -> p g d", g=2))
            if main:
                for j in range(2):
                    sl, bsl = load_slab(s_tm, B_scr["tm"], 0, KC, 1024 + j * 512)
                    for s in range(nsub):
                        bk, bb = proj_tm(sl, bsl, s, 512, k)
                        siltmp, B_sil = R_sil.next()
                        op("act", "activation", [bb], [B_sil], out=siltmp[:, :], in_=bk[:, :], func=AF.Silu)
                        op("pool", "tensor_tensor", [B_sil, B_gn], [B_r1[s]], out=r1[:, s, j * 512:(j + 1) * 512].rearrange("p (a v) -> p a v", a=2),
                           in0=siltmp[:, :].rearrange("p (a v) -> p a v", a=2),
                           in1=gnorm_bc[:, :].unsqueeze(1).to_broadcast([128, 2, 256]), op=ALU.mult)
                    yield
                for gi, (dstt, dstb) in enumerate(((sa, B_sa), (sbg, B_sbg))):
                    for j in range(2):
                        sl, bsl = load_slab(s_tm, B_scr["tm"], 0, KC, 2048 + gi * 1024 + j * 512)
                        for s in range(nsub):
                            bk, bb = proj_tm(sl, bsl, s, 512, k)
                            op("act", "activation", [bb], [dstb[s]], out=dstt[:, s, j * 512:(j + 1) * 512], in_=bk[:, :], func=AF.Sigmoid)
                        yield
                wo_slabs = [load_slab(s_o, B_scr["o"], 0, KC, j * 512) for j in range(2)]
            ckpt(tag + ':D')
            yield 'E'
            for s in range(nsub):
                t0 = s * 128
                if main:
                    bA, bbA = bank(k)
                    for h in range(4):
                        op("pe", "matmul", [B_kmid[h], B_qmid[h]], [bbA], bA[:, h * 128:(h + 1) * 128], lhsT=kmid[:, h, t0:t0 + 128],
                           rhs=qmid[:, h, t0:t0 + 128], start=True, stop=True)
                    op("dve", "tensor_tensor", [bbA, B_cbf], [B_AT], out=AT[:, :, :], in0=bA[:, :].rearrange("p (h t) -> p h t", h=4),
                       in1=mcur_bf[:, :].unsqueeze(1).to_broadcast([128, 4, 128]), op=ALU.mult)
                bT, bbT = bank(k)
                bTb = bT[:, :].bitcast(BF16)
                for h in range(4):
                    op("pe", "transpose", [B_koutx[h], B_cbf], [bbT], bTb[:, h * 128:(h + 1) * 128], koutx[:, h, t0:t0 + 128], ident_bf[:, :])
                op("act", "activation", [bbT], [B_kTMx], out=kTMx[:, :, :], in_=bTb[:, 0:512].rearrange("p (h t) -> p h t", h=4), func=AF.Copy)
                if main:
                    obanks = [bank(k), bank(k)]
                    for h in range(4):
                        bo, bbo = obanks[h // 2]
                        oap = bo[:, (h % 2) * 256:(h % 2) * 256 + 256]
                        op("pe", "matmul", [B_AT, B_va[s]], [bbo], oap, lhsT=AT[:, h, :], rhs=va[:, s, h * 256:(h + 1) * 256], start=True, stop=False)
                        op("pe", "matmul", [B_qin[h], B_Sb[h]], [bbo], oap, lhsT=qin[:, h, t0:t0 + 128], rhs=Sb[:, h, :], start=False, stop=True)
                sbanks = [bank(k), bank(k)]
                for h in range(4):
                    bs_, bbs = sbanks[h // 2]
                    sap = bs_[:, (h % 2) * 256:(h % 2) * 256 + 256]
                    op("pe", "matmul", [B_kTMx, B_vax[s]], [bbs], sap, lhsT=kTMx[:, h, :], rhs=vax[:, s, h * 256:(h + 1) * 256], start=True, stop=True)
                    op("dve", "scalar_tensor_tensor", [bbs, B_S[h], B_csx[h]], [B_S[h]], out=S[:, h, :], in0=S[:, h, :], scalar=c_lastx[:, h, s:s + 1],
                       in1=sap, op0=ALU.mult, op1=ALU.add)
                    op("pool", "tensor_copy", [B_S[h]], [B_Sb[h]], out=Sb[:, h, :], in_=S[:, h, :])
                if not main:
                    continue
                ckpt(tag + ':GLA0')
                yield
                for h in range(4):
                    bo, bbo = obanks[h // 2]
                    oap = bo[:, (h % 2) * 256:(h % 2) * 256 + 256]
                    sqj, B_sqj = R_sqj.next()
                    op("act", "activation", [bbo], [B_sqj, B_rms], out=sqj[:, :], in_=oap, func=AF.Square, accum_out=ssq[:, h:h + 1])
                op("act", "activation", [B_rms, B_eps], [B_rms], out=lnms[:, :], in_=ssq[:, :], func=AF.Ln, scale=1.0 / 256.0, bias=eps_rms)
                op("act", "activation", [B_rms], [B_rms], out=rstd_a[:, :], in_=lnms[:, :], func=AF.Exp, scale=-0.5)
                for h in range(4):
                    bo, bbo = obanks[h // 2]
                    oap = bo[:, (h % 2) * 256:(h % 2) * 256 + 256]
                    op("dve", "scalar_tensor_tensor", [bbo, B_rms, B_r1[s]], [B_ya], out=ya[:, h * 256:(h + 1) * 256], in0=oap, scalar=rstd_a[:, h:h + 1],
                       in1=r1[:, s, h * 256:(h + 1) * 256], op0=ALU.mult, op1=ALU.mult)
                ckpt(tag + ':GLA')
                yield
                for kb in range(2):
                    kcol = t0 + kb * 128
                    mk = mprev_bf if kb == 0 else mcur_bf
                    for half in range(2):
                        gb = [bank(k), bank(k)]
                        for sl_ in range(4):
                            c = half * 4 + sl_
                            for g in range(2):
                                bk, bb = gb[g]
                                op("pe", "matmul", [B_kTb, B_qTb[c]], [bb], bk[:, sl_ * 128:(sl_ + 1) * 128], lhsT=kTb[g * 64:(g + 1) * 64, kcol:kcol + 128],
                                   rhs=qTb[g * 64:(g + 1) * 64, c, t0:t0 + 128], start=True, stop=True)
                        for g in range(2):
                            bk, bb = gb[g]
                            pi = kb * 4 + half * 2 + g
                            op("act", "activation", [bb], [B_P[pi]], out=Pt[:, pi, :], in_=bk[:, :], func=AF.Exp, scale=0.125)
                            op("pool", "tensor_tensor", [B_P[pi], B_cbf], [B_P[pi]], out=Pt[:, pi, :].rearrange("p (a t) -> p a t", a=4),
                               in0=Pt[:, pi, :].rearrange("p (a t) -> p a t", a=4), in1=mk[:, :].unsqueeze(1).to_broadcast([128, 4, 128]), op=ALU.mult)
                ckpt(tag + ':SC')
                yield
                ebanks = [bank(k), bank(k), bank(k)]
                for hd in range(16):
                    g = hd // 8
                    c = hd % 8
                    half, sl_ = c // 4, c % 4
                    be, bbe = ebanks[hd // 7]
                    eap = be[:, (hd % 7) * 65:(hd % 7) * 65 + 65]
                    for kb in range(2):
                        pi = kb * 4 + half * 2 + g
                        op("pe", "matmul", [B_P[pi], B_vbx[s + kb]], [bbe], eap, lhsT=Pt[:, pi, sl_ * 128:(sl_ + 1) * 128],
                           rhs=vbx[:, s + kb, g, :], start=(kb == 0), stop=(kb == 1))
                for gi, (h0, nh) in enumerate(((0, 7), (7, 7), (14, 2))):
                    be, bbe = ebanks[gi]
                    e3 = be[:, 0:nh * 65].rearrange("p (h d) -> p h d", d=65)
                    op("dve", "tensor_tensor", [bbe, B_es], [B_den], out=den[:, h0:h0 + nh], in0=e3[:, :, 64], in1=esink[:, h0:h0 + nh], op=ALU.add)
                    op("dve", "reciprocal", [B_den], [B_den], out=rden[:, h0:h0 + nh], in_=den[:, h0:h0 + nh])
                    op("dve", "tensor_tensor", [bbe, B_den], [B_tb], out=tbv[:, h0 * 64:(h0 + nh) * 64].rearrange("p (h d) -> p h d", d=64),
                       in0=e3[:, :, 0:64], in1=rden[:, h0:h0 + nh].unsqueeze(2).to_broadcast([128, nh, 64]), op=ALU.mult)
                ckpt(tag + ':SWA')
                yield
                op("pool", "tensor_tensor", [B_ya, B_sa[s]], [B_ya], out=ya[:, :], in0=ya[:, :], in1=sa[:, s, :], op=ALU.mult)
                op("pool", "tensor_tensor", [B_tb, B_sbg[s]], [B_tb], out=tbv[:, :], in0=tbv[:, :], in1=sbg[:, s, :], op=ALU.mult)
                op("pool", "tensor_tensor", [B_ya, B_tb], [B_y], out=ybf[:, :], in0=ya[:, :], in1=tbv[:, :], op=ALU.add)
                bk, bb = bank(k)
                bkb = bk[:, :].bitcast(BF16)
                for kc in range(KC):
                    op("pe", "transpose", [B_y, B_cbf], [bb], bkb[:, kc * 128:(kc + 1) * 128], ybf[:, kc * 128:(kc + 1) * 128], ident_bf[:, :])
                op("act", "activation", [bb], [B_yT], out=yT[:, :, :], in_=bkb[:, :].rearrange("p (k t) -> p k t", k=KC), func=AF.Copy)
                for j in range(2):
                    sl, bsl = wo_slabs[j]
                    bk, bb = bank(k)
                    for kc in range(KC):
                        op("pe", "matmul", [B_yT, bsl], [bb], bk[:, :], lhsT=yT[:, kc, :], rhs=sl[:, kc, :], start=(kc == 0), stop=(kc == KC - 1))
                    op("dve", "tensor_tensor", [bb, B_gate], [B_tres], out=tres[:, j * 512:(j + 1) * 512], in0=bk[:, :], in1=gate_bc[:, 0, j * 512:(j + 1) * 512], op=ALU.mult)
                op("dve", "scalar_tensor_tensor", [B_x[s], B_tres], [B_x[s]], out=xbuf[:, s, :], in0=xbuf[:, s, :], scalar=float(ALPHA), in1=tres[:, :], op0=ALU.mult, op1=ALU.add)
            if not main:
                return
            ckpt(tag + ':WO')
            yield
            ln_affine(nsub, 0, xbuf, B_x, 0)
            ckpt(tag + ':LN')
            yield

        def ffn_gen(W, par, out_ap, scale_halo, tag=''):
            nsub = W // 128
            xbuf, B_x = xbufs[par], B_xs[par]
            if scale_halo:
                op("pool", "tensor_scalar", [B_halo, B_flag], [B_halo], out=halo[:, :, :], in0=halo[:, :, :], scalar1=flag[:, 0:1], scalar2=None, op0=ALU.mult)
            to_featmajor([(xbuf[:, s, :], B_x[s]) for s in range(nsub)], nsub, W, 2, 1)
            yield
            for j in range(11):
                sl, bsl = load_slab(s_up, B_scr["up"], 0, KC, j * 512, k=1)
                for e in range(4):
                    qi = 4 * j + e
                    bk, bb = proj_fm(sl, bsl, e * 128, W, 1)
                    ub, Bu = R_u.next()
                    op("act", "activation", [bb], [Bu], out=ub[:, 0:W], in_=bk[:, 0:W], func=AF.Copy)
                    op("pool", "tensor_scalar", [Bu, B_cw], [B_acc[e]], out=acc[:, e, 0:W], in0=ub[:, 0:W],
                       scalar1=cw_col[:, 2 * NUP + qi:2 * NUP + qi + 1], scalar2=cb_col[:, qi:qi + 1], op0=ALU.mult, op1=ALU.add)
                    op("pool", "tensor_copy", [Bu], [B_halon], out=halo_new[:, qi, :], in_=ub[:, W - 2:W])
                    op("dve", "scalar_tensor_tensor", [Bu, B_cw, B_acc[e]], [B_acc[e]], out=acc[:, e, 1:W], in0=ub[:, 0:W - 1], scalar=cw_col[:, NUP + qi:NUP + qi + 1],
                       in1=acc[:, e, 1:W], op0=ALU.mult, op1=ALU.add)
                    op("dve", "scalar_tensor_tensor", [Bu, B_cw, B_acc[e]], [B_acc[e]], out=acc[:, e, 2:W], in0=ub[:, 0:W - 2], scalar=cw_col[:, qi:qi + 1],
                       in1=acc[:, e, 2:W], op0=ALU.mult, op1=ALU.add)
                    op("dve", "scalar_tensor_tensor", [B_halo, B_cw, B_acc[e]], [B_acc[e]], out=acc[:, e, 0:2], in0=halo[:, qi, :], scalar=cw_col[:, qi:qi + 1],
                       in1=acc[:, e, 0:2], op0=ALU.mult, op1=ALU.add)
                    op("dve", "scalar_tensor_tensor", [B_halo, B_cw, B_acc[e]], [B_acc[e]], out=acc[:, e, 0:1], in0=halo[:, qi, 1:2], scalar=cw_col[:, NUP + qi:NUP + qi + 1],
                       in1=acc[:, e, 0:1], op0=ALU.mult, op1=ALU.add)
                for e in range(2):
                    op("act", "activation", [B_acc[e]], [B_gt[e]], out=gtmp[:, e, 0:W], in_=acc[:, e, 0:W], func=AF.Gelu)
                    op("pool", "tensor_tensor", [B_gt[e], B_acc[2 + e]], [B_f[j]], out=fbuf[:, 2 * j + e, 0:W], in0=gtmp[:, e, 0:W], in1=acc[:, 2 + e, 0:W], op=ALU.mult)
                yield
            ckpt(tag + ':UP')
            op("pool", "tensor_copy", [B_halon], [B_halo], out=halo[:, :, :], in_=halo_new[:, :, :])
            for hh in range(2):
                dbanks = [bank(1) for _ in range(nsub)]
                for kg in range(3):
                    nk = 8 if kg < 2 else 6
                    sl, bsl = load_slab(s_down, B_scr["down"], kg * 8, nk, hh * 512, k=1)
                    for s in range(nsub):
                        bk, bb = dbanks[s]
                        for kk in range(nk):
                            kc = kg * 8 + kk
                            op("pe", "matmul", [B_f[kc // 2], bsl], [bb], bk[:, :], lhsT=fbuf[:, kc, s * 128:(s + 1) * 128], rhs=sl[:, kk, :],
                               start=(kc == 0), stop=(kc == 21))
                    yield
                for s in range(nsub):
                    bk, bb = dbanks[s]
                    tb_ = B_acc[2 * hh]
                    op("dve", "tensor_tensor", [bb, B_gate, B_acc[2 * hh + 1]], [tb_, B_acc[2 * hh + 1]], out=tres_f[:, hh * 512:(hh + 1) * 512], in0=bk[:, :], in1=gate_bc[:, 1, hh * 512:(hh + 1) * 512], op=ALU.mult)
                    op("dve", "scalar_tensor_tensor", [B_x[s], tb_], [B_x[s]], out=xbuf[:, s, hh * 512:(hh + 1) * 512], in0=xbuf[:, s, hh * 512:(hh + 1) * 512],
                       scalar=float(ALPHA), in1=tres_f[:, hh * 512:(hh + 1) * 512], op0=ALU.mult, op1=ALU.add)
            ckpt(tag + ':DOWN')
            yield
            ln_affine(nsub, 2, xbuf, B_x, 1)
            if out_ap is not None:
                dma("sp", out_ap.rearrange("(s p) d -> p s d", p=128), xbuf[:, 0:nsub, :], B_x[0:nsub], [], "o%d" % par)
            yield

        tres_f = acc[:, :, :].rearrange("p e w -> p (e w)") if WM == 256 else sb("tres_f", [128, D])

        def ln_affine(nsub, gi, xbuf, B_x, k):
            ln_stats([(xbuf[:, s, :], B_x[s]) for s in range(nsub)], nsub, k)
            rstd, nbias, B_ln = rstds[k], nbiass[k], B_lns[k]
            for s in range(nsub):
                op("act", "activation", [B_x[s], B_ln], [B_x[s]], out=xbuf[:, s, :], in_=xbuf[:, s, :], func=AF.Identity,
                   scale=rstd[:, s:s + 1], bias=nbias[:, s:s + 1])
                op("pool", "tensor_tensor", [B_x[s], B_lnbc], [B_x[s]], out=xbuf[:, s, :], in0=xbuf[:, s, :], in1=ln_bc[:, gi, :], op=ALU.mult)
                op("pool", "tensor_tensor", [B_x[s], B_lnbc], [B_x[s]], out=xbuf[:, s, :], in0=xbuf[:, s, :], in1=ln_bc[:, gi + 1, :], op=ALU.add)

        def interleave(*gens):
            gens = [g for g in gens if g is not None]
            while gens:
                for g in list(gens):
                    try:
                        next(g)
                    except StopIteration:
                        gens.remove(g)

        def interleave_main(g1, g2):
            phase = 0
            while True:
                try:
                    v = next(g1)
                except StopIteration:
                    break
                if v == 'E':
                    phase = 1
                if phase == 1 and g2 is not None:
                    for _ in range(FFN_RATIO):
                        try:
                            next(g2)
                        except StopIteration:
                            g2 = None
                            break
            if g2 is not None:
                interleave(g2)

        prev_w = [128]
        ckpt("setup")
        r = 0
        pgens = []
        for i, W in enumerate(cfg.pre):
            pgens.append(mixer_gen(False, x_pre[r:r + W, :], pos_pre[:, r:r + W], W, i % 2, False, i == len(cfg.pre) - 1, 'pre%d' % i, k=i % 2))
            r += W
        for i in range(0, len(pgens), 2):
            if INTERLEAVE:
                interleave(*pgens[i:i + 2])
            else:
                for g in pgens[i:i + 2]:
                    interleave(g)
            ckpt("pre%d" % i)
        r = 0
        ro = 0
        pending = None
        for i, W in enumerate(cfg.main):
            oap = None
            if i > 0:
                oap = out_d[ro:ro + W, :]
                ro += W
            g1 = mixer_gen(True, x_main[r:r + W, :], pos_main[:, r:r + W], W, i % 2, i == 1, False, 'main%d' % i)
            if INTERLEAVE:
                interleave_main(g1, pending)
            else:
                interleave(pending)
                interleave(g1)
            pending = ffn_gen(W, i % 2, oap, i == 1, 'main%d' % i)
            r += W
            ckpt("main%d" % i)
        interleave(pending)

        with nc.Block() as block:
            pg.emit(nc, block, st)
    return nc


def make_consts():
    c = np.zeros((128, 5 * 128 + 2), np.float32)
    c[:, 0:128] = np.eye(128, dtype=np.float32)
    perm = np.zeros((128, 128), np.float32)
    for m in range(128):
        d = m % 64
        if d < 8:
            perm[m + 8, m] = 1.0
        elif d < 16:
            perm[m - 8, m] = 1.0
    c[:, 128:256] = perm
    j = np.arange(128)[:, None]
    i = np.arange(128)[None, :]
    c[:, 256:384] = (j <= i).astype(np.float32)
    c[:, 384:512] = (j > i).astype(np.float32)
    rm = np.ones((128, 128), np.float32)
    rm[:, 0] = 0.0
    c[:, 512:640] = rm
    inv_freq = (500000.0 ** (-(np.arange(0, 16, 2, dtype=np.float32) / 16.0))).astype(np.float32)
    for p in range(128):
        d = p % 64
        if d < 16:
            c[p, 640] = inv_freq[d % 8]
            c[p, 641] = -1.0 if d < 8 else 1.0
    return c


def host_inputs(cfg, x, c, positions, w_ada, b_ada, w_in, gla_w_lr, gla_b_lr, gla_norm_g, swa_sinks,
                w_o, ln1_g, ln1_b, w_up, conv_w, conv_b, w_down, ln2_g, ln2_b, core_map):
    f = np.float32
    w_in = np.asarray(w_in[0], f)
    sp = np.cumsum([512, 512, 1024, 1024, 16, 1024, 128, 128, 2048])[:-1]
    qa, ka, va_, ra, lra, qb, kb, vb, gates = np.split(w_in, sp, axis=1)
    qb_perm = np.concatenate([np.concatenate([qb[:, 64 * cc:64 * cc + 64], qb[:, 64 * (cc + 8):64 * (cc + 8) + 64]], axis=1) for cc in range(8)], axis=1)
    w_fm = np.ascontiguousarray(np.concatenate([qa, ka, qb_perm], axis=1))
    w_tm = np.ascontiguousarray(np.concatenate([va_, ra, gates], axis=1))
    w_sm = np.ascontiguousarray(np.concatenate([lra, kb, vb], axis=1))
    wu = np.asarray(w_up[0], f)
    cwv = np.asarray(conv_w[0], f)
    cbv = np.asarray(conv_b[0], f)
    order = []
    for j in range(11):
        order += list(range(256 * j, 256 * j + 256)) + list(range(DFF + 256 * j, DFF + 256 * j + 256))
    order = np.asarray(order)
    w_up_p = np.ascontiguousarray(wu[:, order])
    cw_p = cwv[:, order]
    cb_p = cbv[order]
    cw_col = np.ascontiguousarray(cw_p.reshape(3, NUP, 128).transpose(2, 0, 1).reshape(128, 3 * NUP))
    cb_col = np.ascontiguousarray(cb_p.reshape(NUP, 128).T)
    b_ada_v = np.asarray(b_ada[0], f)
    b_ada_col = np.ascontiguousarray(b_ada_v.reshape(6, KC, 128).transpose(2, 0, 1).reshape(128, 48))
    ln_rows = np.stack([np.asarray(a[0], f) for a in (ln1_g, ln1_b, ln2_g, ln2_b)])
    consts = make_consts()
    shared = {
        "w_ada": np.ascontiguousarray(np.asarray(w_ada[0], f)), "b_ada_col": b_ada_col, "b_ada_row": b_ada_v[None, :].copy(),
        "w_fm": w_fm, "w_tm": w_tm, "w_sm": w_sm, "w_lr": np.ascontiguousarray(np.asarray(gla_w_lr[0], f)),
        "b_lr_col": np.ascontiguousarray(np.asarray(gla_b_lr[0], f).reshape(4, 128).T),
        "gnorm": np.asarray(gla_norm_g[0], f)[None, :].copy(), "sinks": np.asarray(swa_sinks[0], f)[None, :].copy(),
        "w_o": np.ascontiguousarray(np.asarray(w_o[0], f)), "ln_rows": ln_rows, "w_up": w_up_p, "cw_col": cw_col, "cb_col": cb_col,
        "w_down": np.ascontiguousarray(np.asarray(w_down[0], f)), "consts": consts,
    }
    x = np.asarray(x, f)
    positions = np.asarray(positions, np.int32)
    c = np.asarray(c, f)
    maps = []
    for (b, start) in core_map:
        m = dict(shared)
        warm = cfg.main[0]
        if start > 0:
            assert start - warm == cfg.npre
            m["x_pre"] = np.ascontiguousarray(x[b, 0:cfg.npre])
            m["pos_pre"] = np.ascontiguousarray(positions[b, 0:cfg.npre][None, :])
            m["x_main"] = np.ascontiguousarray(x[b, start - warm:start + cfg.nout])
            m["pos_main"] = np.ascontiguousarray(positions[b, start - warm:start + cfg.nout][None, :])
            m["flag"] = np.ones((128, 1), f)
        else:
            m["x_pre"] = np.ascontiguousarray(x[b, 0:cfg.npre])
            m["pos_pre"] = np.ascontiguousarray(positions[b, 0:cfg.npre][None, :])
            m["x_main"] = np.ascontiguousarray(np.concatenate([x[b, 0:warm], x[b, 0:cfg.nout]], axis=0))
            m["pos_main"] = np.ascontiguousarray(np.concatenate([positions[b, 0:warm], positions[b, 0:cfg.nout]])[None, :])
            m["flag"] = np.zeros((128, 1), f)
        m["c_col"] = np.ascontiguousarray(c[b].reshape(KC, 128).T)
        maps.append(m)
    return maps


_NC_CACHE = {}


def kernel(x, c, positions, w_ada, b_ada, w_in, gla_w_lr, gla_b_lr, gla_norm_g, swa_sinks,
           w_o, ln1_g, ln1_b, w_up, conv_w, conv_b, w_down, ln2_g, ln2_b):
    cfg = FULL
    B, S, _ = x.shape
    half = S // 2
    core_map = [(b, hf * half) for b in range(B) for hf in range(2)]
    maps = host_inputs(cfg, x, c, positions, w_ada, b_ada, w_in, gla_w_lr, gla_b_lr, gla_norm_g, swa_sinks,
                       w_o, ln1_g, ln1_b, w_up, conv_w, conv_b, w_down, ln2_g, ln2_b, core_map)
    if "nc" not in _NC_CACHE:
        _NC_CACHE["nc"] = build(cfg)
    res = run_bass_kernel_spmd(_NC_CACHE["nc"], maps, core_ids=list(range(len(maps))))
    out = np.zeros((B, S, D), np.float32)
    for i, (b, start) in enumerate(core_map):
        out[b, start:start + cfg.nout] = np.asarray(res.results[i]["out"], np.float32)
    return out
```

```python
import numpy as np
from contextlib import ExitStack
import concourse.bass as bass
import concourse.mybir as mybir
from concourse.bass_utils import run_bass_kernel_spmd

F32 = mybir.dt.float32
BF16 = mybir.dt.bfloat16
I32 = mybir.dt.int32
AF = mybir.ActivationFunctionType
ALU = mybir.AluOpType

D = 1024
KC = 8
DFF = 2816
NUP = 44
LN_EPS = 1e-5
RMS_EPS = 1e-6
ALPHA = 2.0 ** 0.25
import os
ROPE_ADD_ENG = os.environ.get("ROPE_ADD_ENG", "pool")
INTERLEAVE = os.environ.get("KINTER", "1") == "1"
FFN_RATIO = int(os.environ.get("KFFNR", "0"))
TWO_PI = 2.0 * np.pi
CW1 = 6.28125
CW2 = float(TWO_PI - CW1)


class Buf:
    __slots__ = ("name", "w", "r", "excl")

    def __init__(self, name, excl=False):
        self.name = name
        self.excl = excl
        self.w = None
        self.r = {}


class Op:
    __slots__ = ("eng", "fn", "deps", "inc", "dma_key", "dma_seq")


class Prog:
    ENGS = ("pe", "act", "dve", "pool", "sp")

    def __init__(self):
        self.ops = {e: [] for e in self.ENGS}
        self.dma_cnt = {}
        self.dma_last = {}
        self.frozen = False

    def add(self, eng, fn, reads=(), writes=(), dma_key=None):
        if self.frozen:
            return None
        op = Op()
        op.eng, op.fn, op.inc, op.dma_key, op.dma_seq = eng, fn, False, dma_key, 0
        idx = len(self.ops[eng])
        me = (eng, idx)
        deps = set()
        for b in reads:
            if b.w is not None:
                deps.add((b.w, True))
            if b.excl:
                for v in b.r.values():
                    if v[0] != eng:
                        deps.add((v, True))
        for b in writes:
            if b.w is not None:
                deps.add((b.w, False))
            for v in b.r.values():
                deps.add((v, False))
        if dma_key is not None:
            self.dma_cnt[dma_key] = self.dma_cnt.get(dma_key, 0) + 1
            op.dma_seq = self.dma_cnt[dma_key]
            if dma_key in self.dma_last:
                deps.add((self.dma_last[dma_key], True))
            self.dma_last[dma_key] = me
        final = set()
        raw = set(d for d, is_raw in deps if is_raw)
        for d, _ in deps:
            o = self.ops[d[0]][d[1]]
            if o.dma_key is None and d[0] == eng:
                if eng == "pe":
                    continue
            final.add(d)
            if o.dma_key is None:
                o.inc = True
        op.deps = final
        self.ops[eng].append(op)
        for b in reads:
            key = eng if dma_key is None else ("dma", eng, idx)
            b.r[key] = me
        for b in writes:
            b.w = me
            b.r = {}
        return me

    def emit(self, nc, block, stack):
        names = {"pe": "tensor", "act": "scalar", "dve": "vector", "pool": "gpsimd", "sp": "sync"}
        esem = {e: stack.enter_context(nc.semaphore("s_" + e)) for e in self.ENGS}
        dsem = {k: stack.enter_context(nc.semaphore("d_%s" % str(k))) for k in self.dma_cnt}
        cnt = {}
        for e in self.ENGS:
            c = 0
            lst = []
            for o in self.ops[e]:
                if o.inc:
                    c += 1
                lst.append(c)
            cnt[e] = lst
        prog = self

        def make(e):
            def body(engine):
                waited = {}
                for o in prog.ops[e]:
                    for d in sorted(o.deps):
                        od = prog.ops[d[0]][d[1]]
                        if od.dma_key is not None:
                            s, v = dsem[od.dma_key], 16 * od.dma_seq
                        else:
                            s, v = esem[d[0]], cnt[d[0]][d[1]]
                        k = id(s)
                        if waited.get(k, 0) >= v:
                            continue
                        waited[k] = v
                        engine.wait_ge(s, v)
                    ins = o.fn(engine)
                    if o.dma_key is not None:
                        ins.then_inc(dsem[o.dma_key], 16)
                    elif o.inc:
                        ins.then_inc(esem[e], 1)
                if e == "sp":
                    for k, last in prog.dma_last.items():
                        engine.wait_ge(dsem[k], 16 * prog.dma_cnt[k])
            return body

        for e in self.ENGS:
            getattr(block, names[e])(make(e))


class StopBuild(Exception):
    pass


class Cfg:
    stop = None

    def __init__(self, pre_widths, main_widths, nslab=4):
        self.pre = list(pre_widths)
        self.main = list(main_widths)
        self.npre = sum(self.pre)
        self.nmain = sum(self.main)
        self.nout = self.nmain - self.main[0]
        self.wmax = max(self.pre + self.main)
        self.nslab = nslab


FULL = Cfg([256] * 15 + [128], [128] + [256] * 16)


def build(cfg):
    nc = bass.Bass("TRN2", target_bir_lowering=False)
    pg = Prog()
    WM = cfg.wmax
    NSM = WM // 128

    def din(name, shape, dt=F32):
        return nc.dram_tensor(name, list(shape), dt, kind="ExternalInput").ap()

    def dscr(name, shape, dt=BF16):
        return nc.dram_tensor(name, list(shape), dt, kind="Internal").ap()

    x_pre = din("x_pre", [cfg.npre, D])
    x_main = din("x_main", [cfg.nmain, D])
    pos_pre = din("pos_pre", [1, cfg.npre], I32)
    pos_main = din("pos_main", [1, cfg.nmain], I32)
    flag_d = din("flag", [128, 1])
    c_col_d = din("c_col", [128, KC])
    w_ada_d = din("w_ada", [D, 6 * D])
    b_ada_col_d = din("b_ada_col", [128, 48])
    b_ada_row_d = din("b_ada_row", [1, 6 * D])
    w_fm_d = din("w_fm", [D, 2048])
    w_tm_d = din("w_tm", [D, 4096])
    w_sm_d = din("w_sm", [D, 272])
    w_lr_d = din("w_lr", [16, 512])
    b_lr_col_d = din("b_lr_col", [128, 4])
    gnorm_d = din("gnorm", [1, 256])
    sinks_d = din("sinks", [1, 16])
    w_o_d = din("w_o", [D, D])
    ln_rows_d = din("ln_rows", [4, D])
    w_up_d = din("w_up", [D, 2 * DFF])
    cw_col_d = din("cw_col", [128, 3 * NUP])
    cb_col_d = din("cb_col", [128, NUP])
    w_down_d = din("w_down", [DFF, D])
    consts_d = din("consts", [128, 5 * 128 + 2])
    out_d = nc.dram_tensor("out", [cfg.nout, D], F32, kind="ExternalOutput").ap()

    s_ada = dscr("s_ada", [D, 6 * D])
    s_fm = dscr("s_fm", [D, 2048])
    s_tm = dscr("s_tm", [D, 4096])
    s_sm = dscr("s_sm", [D, 272])
    s_o = dscr("s_o", [D, D])
    s_up = dscr("s_up", [D, 2 * DFF])
    s_down = dscr("s_down", [DFF, D])

    st = ExitStack()
    with st:
        def sb(name, shape, dt=F32):
            return st.enter_context(nc.sbuf_tensor(name, list(shape), dt))

        def op(eng, method, reads, writes, *a, **kw):
            return pg.add(eng, lambda e: getattr(e, method)(*a, **kw), reads, writes)

        banks = [st.enter_context(nc.psum_tensor("bank%d" % i, [128, 512], F32)) for i in range(8)]
        bank_bufs = [Buf("bank%d" % i, excl=True) for i in range(8)]
        bank_ctr = [0, 0]
        bank_pool = [[0, 1, 2, 3, 4], [5, 6, 7]]

        def bank(k=0):
            pool = bank_pool[k]
            i = pool[bank_ctr[k] % len(pool)]
            bank_ctr[k] += 1
            return banks[i], bank_bufs[i]

        xbufs = [sb("xbuf%d" % i, [128, NSM, D]) for i in range(2)]
        B_xs = [[Buf("x%d_%d" % (i, s)) for s in range(NSM)] for i in range(2)]
        xns = [sb("xn%d" % i, [128, NSM, D], BF16) for i in range(2)]
        B_xns = [[Buf("xn%d_%d" % (i, s)) for s in range(NSM)] for i in range(2)]
        hTs = [sb("hT%d" % i, [128, KC, WM], BF16) for i in range(2)]
        B_hTs = [[Buf("hT%d_%d" % (i, kc)) for kc in range(KC)] for i in range(2)]
        slabs = [sb("slab%d" % i, [128, KC, 512], BF16) for i in range(cfg.nslab)]
        B_slab = [Buf("slab%d" % i) for i in range(cfg.nslab)]
        cst = sb("cst", [128, 5 * 128 + 2]); B_cst = Buf("cst")
        ident_bf = sb("ident_bf", [128, 128], BF16)
        perm_bf = sb("perm_bf", [128, 128], BF16)
        mcur_bf = sb("mcur_bf", [128, 128], BF16)
        mprev_bf = sb("mprev_bf", [128, 128], BF16)
        B_cbf = Buf("cbf")
        flag = sb("flag_sb", [128, 1]); B_flag = Buf("flag")
        c_col = sb("c_col_sb", [128, KC]); B_ccol = Buf("ccol")
        sc_bf = sb("sc_bf", [128, KC], BF16)
        B_sc = Buf("sc")
        b_ada_col = sb("b_ada_col_sb", [128, 48]); B_bac = Buf("bac")
        modcol = sb("modcol", [128, 4, KC]); B_mod = Buf("mod")
        gate_bc = sb("gate_bc", [128, 2, D]); B_gate = Buf("gate")
        ln_bc = sb("ln_bc", [128, 4, D]); B_lnbc = Buf("lnbc")
        w_lr = sb("w_lr_sb", [16, 512]); B_wlr = Buf("wlr")
        negb = sb("negb", [128, 4]); B_negb = Buf("negb")
        gnorm_bc = sb("gnorm_bc", [128, 256]); B_gn = Buf("gn")
        esink = sb("esink", [128, 16]); B_es = Buf("es")
        cw_col = sb("cw_col_sb", [128, 3 * NUP]); cb_col = sb("cb_col_sb", [128, NUP]); B_cw = Buf("cw")
        w_sm = sb("w_sm_sb", [128, KC, 272], BF16); B_wsm = Buf("wsm")
        statss = [sb("stats%d" % i, [128, NSM, 2, 6]) for i in range(2)]; mvs = [sb("mv%d" % i, [128, NSM, 2]) for i in range(2)]
        lnvs = [sb("lnv%d" % i, [128, NSM]) for i in range(2)]; rstds = [sb("rstd%d" % i, [128, NSM]) for i in range(2)]
        nbiass = [sb("nbias%d" % i, [128, NSM]) for i in range(2)]
        B_lns = [Buf("lnscratch0"), Buf("lnscratch1")]
        posi = sb("posi", [128, WM], I32); angf = sb("angf", [128, WM])
        angk = sb("angk", [128, WM], I32)
        cosT = sb("cosT", [128, WM]); sinT = sb("sinT", [128, WM])
        B_pos = Buf("pos"); B_ang = Buf("ang"); B_cs = Buf("cossin")
        class Rot:
            def __init__(self, name, shape, dt, n):
                self.t = [sb("%s_%d" % (name, i), shape, dt) for i in range(n)]
                self.b = [Buf("%s_%d" % (name, i)) for i in range(n)]
                self.i = 0

            def next(self):
                k = self.i % len(self.t)
                self.i += 1
                return self.t[k], self.b[k]

        R_rtb = Rot("rope_tb", [128, WM], BF16, 1); R_rt1 = Rot("rope_t1", [128, WM], F32, 2); R_rt2 = Rot("rope_t2", [128, WM], F32, 2)
        lraTs = [sb("lraT%d" % i, [16, WM]) for i in range(2)]; B_lras = [Buf("lra0"), Buf("lra1")]
        R_et = Rot("etmp", [128, WM], F32, 1); R_lt = Rot("ltmp", [128, WM], F32, 1); R_Lt = Rot("Ltmp", [128, WM], F32, 1)
        E1 = sb("E1", [128, 4, WM]); E2 = sb("E2", [128, 4, WM]); B_E1 = [Buf("E1_%d" % h) for h in range(4)]
        B_E2 = [Buf("E2_%d" % h) for h in range(4)]
        c_last = sb("c_last", [128, 4, NSM]); c_mid = sb("c_mid", [128, 4, NSM]); c_midn = sb("c_midn", [128, 4, NSM])
        B_cs_h = [Buf("csm%d" % h) for h in range(4)]
        B_cs2_h = [Buf("csm2_%d" % h) for h in range(4)]
        R_ft = Rot("ftmp", [128, WM], F32, 1); R_ft2 = Rot("ftmp2", [128, WM], F32, 1)
        qin = sb("qin", [128, 4, WM], BF16); qmid = sb("qmid", [128, 4, WM], BF16)
        kmid = sb("kmid", [128, 4, WM], BF16); kout = sb("kout", [128, 4, WM], BF16)
        B_qin = [Buf("qin%d" % h) for h in range(4)]; B_qmid = [Buf("qmid%d" % h) for h in range(4)]
        B_kmid = [Buf("kmid%d" % h) for h in range(4)]; B_kout = [Buf("kout%d" % h) for h in range(4)]
        qTb = sb("qTb", [128, 8, WM], BF16); B_qTb = [Buf("qTb%d" % c) for c in range(8)]
        kTb = sb("kTb", [128, 128 + WM], BF16); B_kTb = Buf("kTb")
        va = sb("va", [128, NSM, D], BF16); r1 = sb("r1", [128, NSM, D], BF16)
        sa = sb("sa", [128, NSM, D], BF16); sbg = sb("sbg", [128, NSM, D], BF16)
        B_va = [Buf("va%d" % s) for s in range(NSM)]; B_r1 = [Buf("r1_%d" % s) for s in range(NSM)]
        B_sa = [Buf("sa%d" % s) for s in range(NSM)]; B_sbg = [Buf("sbg%d" % s) for s in range(NSM)]
        R_sil = Rot("siltmp", [128, 512], F32, 1)
        vbx = sb("vbx", [128, 1 + NSM, 2, 65], BF16); B_vbx = [Buf("vbx%d" % s) for s in range(1 + NSM)]
        AT = sb("AT", [128, 4, 128], BF16); B_AT = Buf("AT")
        koutTM = sb("koutTM", [128, 4, 128], BF16); B_kTM = Buf("kTM")
        S = sb("S", [128, 4, 256]); Sb = sb("Sb", [128, 4, 256], BF16)
        B_S = [Buf("S%d" % h) for h in range(4)]; B_Sb = [Buf("Sb%d" % h) for h in range(4)]
        Pt = sb("Pt", [128, 8, 512], BF16); B_P = [Buf("P%d" % i) for i in range(8)]
        R_sqj = Rot("sqj", [128, 256], F32, 1)
        ssq = sb("ssq", [128, 4]); lnms = sb("lnms", [128, 4]); rstd_a = sb("rstd_a", [128, 4]); B_rms = Buf("rms")
        ya = sb("ya", [128, D]); B_ya = Buf("ya")
        sc_rep = ya[:, 0:512].bitcast(BF16).rearrange("p (k m) -> p k m", k=KC)
        tbv = sb("tbv", [128, D]); B_tb = Buf("tb")
        den = sb("den", [128, 16]); rden = sb("rden", [128, 16]); B_den = Buf("den")
        ybf = sb("ybf", [128, D], BF16); B_y = Buf("y")
        yT = sb("yT", [128, KC, 128], BF16); B_yT = Buf("yT")
        tres = ya; B_tres = B_ya
        fbuf = sb("fbuf", [128, 22, WM], BF16); B_f = [Buf("f%d" % j) for j in range(11)]
        acc = sb("acc", [128, 4, WM]); B_acc = [Buf("acc%d" % e) for e in range(4)]
        gtmp = sb("gtmp", [128, 2, WM]); B_gt = [Buf("gt%d" % e) for e in range(2)]
        R_u = Rot("ubuf", [128, WM], F32, 2)
        halo = sb("halo", [128, NUP, 2]); halo_new = sb("halo_new", [128, NUP, 2]); B_halo = Buf("halo"); B_halon = Buf("halon")

        ident_f = cst[:, 0:128]
        rmask128 = cst[:, 512:640]
        freq_col = cst[:, 640:641]
        sign_col = cst[:, 641:642]

        dma_rr = [0]

        def dma(eng, out, in_, reads, writes, key):
            return pg.add(eng, lambda e: e.dma_start(out=out, in_=in_), reads, writes, dma_key=key)

        B_scr = {n: Buf("scr_" + n) for n in ("ada", "fm", "tm", "sm", "o", "up", "down")}
        for n, (src, dst) in {"sm": (w_sm_d, s_sm), "ada": (w_ada_d, s_ada), "fm": (w_fm_d, s_fm),
                              "tm": (w_tm_d, s_tm), "o": (w_o_d, s_o), "up": (w_up_d, s_up),
                              "down": (w_down_d, s_down)}.items():
            rows = src.shape[0]
            for r0 in range(0, rows, 256):
                dma("pool", dst[r0:r0 + 256, :], src[r0:r0 + 256, :], [], [B_scr[n]], "cast_" + n)

        dma("sp", cst[:, :], consts_d[:, :], [], [B_cst], "c0")
        dma("sp", flag[:, :], flag_d[:, :], [], [B_flag], "c1")
        dma("sp", c_col[:, :], c_col_d[:, :], [], [B_ccol], "c2")
        dma("sp", b_ada_col[:, :], b_ada_col_d[:, :], [], [B_bac], "c3")
        dma("sp", w_lr[:, :], w_lr_d[:, :], [], [B_wlr], "c0")
        dma("sp", negb[:, :], b_lr_col_d[:, :], [], [B_negb], "c1")
        dma("sp", gnorm_bc[:, :], gnorm_d.partition_broadcast(128), [], [B_gn], "c2")
        dma("sp", esink[:, :], sinks_d.partition_broadcast(128), [], [B_es], "c3")
        dma("sp", cw_col[:, :], cw_col_d[:, :], [], [B_cw], "c0")
        dma("sp", cb_col[:, :], cb_col_d[:, :], [], [B_cw], "c1")
        for i in range(4):
            dma("sp", ln_bc[:, i, :], ln_rows_d[i:i + 1, :].partition_broadcast(128), [], [B_lnbc], "c2")
        for i, v in enumerate((2, 5)):
            dma("sp", gate_bc[:, i, :], b_ada_row_d[:, v * D:(v + 1) * D].partition_broadcast(128), [], [B_gate], "c3")
        dma("sp", w_sm[:, :, :], s_sm.rearrange("(k p) c -> p k c", p=128), [B_scr["sm"]], [B_wsm], "c0")

        op("dve", "tensor_copy", [B_cst], [B_cbf], out=ident_bf[:, :], in_=cst[:, 0:128])
        op("dve", "tensor_copy", [B_cst], [B_cbf], out=perm_bf[:, :], in_=cst[:, 128:256])
        op("dve", "tensor_copy", [B_cst], [B_cbf], out=mcur_bf[:, :], in_=cst[:, 256:384])
        op("dve", "tensor_copy", [B_cst], [B_cbf], out=mprev_bf[:, :], in_=cst[:, 384:512])
        op("dve", "tensor_scalar", [B_negb], [B_negb], out=negb[:, :], in0=negb[:, :], scalar1=-1.0, scalar2=None, op0=ALU.mult)
        op("act", "activation", [B_es], [B_es], out=esink[:, :], in_=esink[:, :], func=AF.Exp)
        op("pool", "memset", [], B_S, S[:, :, :], 0.0)
        op("pool", "memset", [], B_Sb, Sb[:, :, :], 0.0)
        op("pool", "memset", [], B_vbx, vbx[:, :, :, :], 1.0)
        op("pool", "memset", [], [B_halo], halo[:, :, :], 0.0)
        op("pool", "memset", [], [B_kTb], kTb[:, :], 0.0)
        op("act", "activation", [B_ccol], [B_ccol], out=c_col[:, :], in_=c_col[:, :], func=AF.Silu)
        op("dve", "tensor_copy", [B_ccol], [B_sc], out=sc_bf[:, :], in_=c_col[:, :])
        op("dve", "tensor_copy", [B_ccol], [B_sc, B_ya], out=sc_rep[:, :, :],
           in_=c_col[:, :].unsqueeze(2).to_broadcast([128, KC, 128]))

        def ckpt(name):
            if cfg.stop == name:
                pg.frozen = True

        slab_rr = [0, 0]
        slab_pool = [list(range(0, cfg.nslab - cfg.nslab // 2)), list(range(cfg.nslab - cfg.nslab // 2, cfg.nslab))]

        def load_slab(scr, bscr, k0, nk, c0, ncols=512, k=0):
            pool = slab_pool[k]
            i = pool[slab_rr[k] % len(pool)]
            slab_rr[k] += 1
            src = scr.rearrange("(k p) c -> p k c", p=128)[:, k0:k0 + nk, c0:c0 + ncols]
            dma("sp", slabs[i][:, 0:nk, 0:ncols], src, [bscr], [B_slab[i]], "slab%d" % i)
            return slabs[i], B_slab[i]

        colslot = {0: 0, 1: 1, 3: 2, 4: 3}
        for v in range(6):
            for hh in range(2):
                sl, bsl = load_slab(s_ada, B_scr["ada"], 0, KC, v * D + hh * 512)
                if v in colslot:
                    bk, bb = bank()
                    for cc in range(4):
                        for kc in range(KC):
                            op("pe", "matmul", [bsl, B_sc], [bb], bk[:, cc:cc + 1], lhsT=sl[:, kc, cc * 128:(cc + 1) * 128],
                               rhs=sc_bf[:, kc:kc + 1], start=(kc == 0), stop=(kc == KC - 1))
                    j = colslot[v]
                    op("dve", "tensor_tensor", [bb, B_bac], [B_mod], out=modcol[:, j, hh * 4:hh * 4 + 4], in0=bk[:, 0:4],
                       in1=b_ada_col[:, v * 8 + hh * 4: v * 8 + hh * 4 + 4], op=ALU.add)
                else:
                    g = 0 if v == 2 else 1
                    bk, bb = bank()
                    for kc in range(KC):
                        op("pe", "matmul", [bsl, B_sc, B_ya], [bb], bk[:, :], lhsT=sc_rep[:, kc, :], rhs=sl[:, kc, :],
                           start=(kc == 0), stop=(kc == KC - 1))
                    op("dve", "tensor_tensor", [bb, B_gate], [B_gate], out=gate_bc[:, g, hh * 512:(hh + 1) * 512], in0=bk[:, :],
                       in1=gate_bc[:, g, hh * 512:(hh + 1) * 512], op=ALU.add)
        for j in (1, 3):
            op("dve", "tensor_scalar", [B_mod], [B_mod], out=modcol[:, j, :], in0=modcol[:, j, :], scalar1=1.0, scalar2=None, op0=ALU.add)

        def ln_stats(srcs, nsub, k):
            stats, mv, lnv, rstd, nbias, B_ln = statss[k], mvs[k], lnvs[k], rstds[k], nbiass[k], B_lns[k]
            for s, (ap, b) in enumerate(srcs):
                for hlf in range(2):
                    op("dve", "bn_stats", [b], [B_ln], out=stats[:, s, hlf, :], in_=ap[:, hlf * 512:(hlf + 1) * 512])
                op("dve", "bn_aggr", [B_ln], [B_ln], out=mv[:, s, :], in_=stats[:, s, :, :].rearrange("p a b -> p (a b)"))
            op("act", "activation", [B_ln, B_eps], [B_ln], out=lnv[:, 0:nsub], in_=mv[:, 0:nsub, 1], func=AF.Ln, bias=eps_ln[:, 0:1], scale=1.0)
            op("act", "activation", [B_ln], [B_ln], out=rstd[:, 0:nsub], in_=lnv[:, 0:nsub], func=AF.Exp, scale=-0.5)
            op("dve", "scalar_tensor_tensor", [B_ln], [B_ln], out=nbias[:, 0:nsub], in0=mv[:, 0:nsub, 0], scalar=-1.0,
               in1=rstd[:, 0:nsub], op0=ALU.mult, op1=ALU.mult)

        epsc = sb("epsc", [128, 4]); B_eps = Buf("eps")
        op("pool", "memset", [], [B_eps], epsc[:, 0:1], LN_EPS)
        op("pool", "memset", [], [B_eps], epsc[:, 1:2], RMS_EPS)
        op("pool", "memset", [], [B_eps], epsc[:, 2:3], 1.0)
        eps_ln = epsc[:, 0:1]
        eps_rms = epsc[:, 1:2]
        one_col = epsc[:, 2:3]

        def to_featmajor(srcs, nsub, W, jmod, k):
            ln_stats(srcs, nsub, k)
            xn, B_xn, hT, B_hT, rstd, nbias, B_ln = xns[k], B_xns[k], hTs[k], B_hTs[k], rstds[k], nbiass[k], B_lns[k]
            for s, (ap, b) in enumerate(srcs):
                op("act", "activation", [b, B_ln], [B_xn[s]], out=xn[:, s, :], in_=ap, func=AF.Identity,
                   scale=rstd[:, s:s + 1], bias=nbias[:, s:s + 1])
            for kc in range(KC):
                bk, bb = bank(k)
                bkb = bk[:, :].bitcast(BF16)
                for s in range(nsub):
                    op("pe", "transpose", [B_xn[s], B_cbf], [bb], bkb[:, s * 128:(s + 1) * 128],
                       xn[:, s, kc * 128:(kc + 1) * 128], ident_bf[:, :])
                if kc % 2 == 0:
                    op("act", "activation", [bb, B_mod], [B_hT[kc]], out=hT[:, kc, 0:W], in_=bkb[:, 0:W], func=AF.Identity,
                       scale=modcol[:, jmod + 1, kc:kc + 1], bias=modcol[:, jmod, kc:kc + 1])
                else:
                    op("dve", "tensor_scalar", [bb, B_mod], [B_hT[kc]], out=hT[:, kc, 0:W], in0=bkb[:, 0:W],
                       scalar1=modcol[:, jmod + 1, kc:kc + 1], scalar2=modcol[:, jmod, kc:kc + 1], op0=ALU.mult, op1=ALU.add)

        def rope_tables(pos_ap, W):
            angt, angkf = R_et.t[0], R_lt.t[0]
            BA = [B_ang, R_et.b[0], R_lt.b[0]]
            dma("sp", posi[:, 0:W], pos_ap.partition_broadcast(128), [], [B_pos], "pos")
            op("dve", "tensor_copy", [B_pos], BA, out=angf[:, 0:W], in_=posi[:, 0:W])
            op("dve", "tensor_scalar", BA + [B_cst], BA, out=angf[:, 0:W], in0=angf[:, 0:W], scalar1=freq_col, scalar2=None, op0=ALU.mult)
            op("dve", "tensor_scalar", BA, BA, out=angt[:, 0:W], in0=angf[:, 0:W], scalar1=float(1.0 / TWO_PI), scalar2=None, op0=ALU.mult)
            op("dve", "tensor_copy", BA, BA, out=angk[:, 0:W], in_=angt[:, 0:W])
            op("dve", "tensor_copy", BA, BA, out=angkf[:, 0:W], in_=angk[:, 0:W])
            op("dve", "scalar_tensor_tensor", BA, BA, out=angf[:, 0:W], in0=angkf[:, 0:W], scalar=-CW1, in1=angf[:, 0:W], op0=ALU.mult, op1=ALU.add)
            op("dve", "scalar_tensor_tensor", BA, BA, out=angf[:, 0:W], in0=angkf[:, 0:W], scalar=-CW2, in1=angf[:, 0:W], op0=ALU.mult, op1=ALU.add)
            for shift, dst, use_sign in ((0.0, sinT, True), (float(np.pi / 2), cosT, False)):
                src = angf
                if shift != 0.0:
                    op("dve", "tensor_scalar", BA, BA, out=angkf[:, 0:W], in0=angf[:, 0:W], scalar1=shift, scalar2=None, op0=ALU.add)
                    src = angkf
                for _ in range(2):
                    op("dve", "tensor_scalar", BA, BA, out=angt[:, 0:W], in0=src[:, 0:W], scalar1=float(np.pi), scalar2=float(-TWO_PI), op0=ALU.is_gt, op1=ALU.mult)
                    op("dve", "tensor_tensor", BA, BA, out=src[:, 0:W], in0=src[:, 0:W], in1=angt[:, 0:W], op=ALU.add)
                    op("dve", "tensor_scalar", BA, BA, out=angt[:, 0:W], in0=src[:, 0:W], scalar1=float(-np.pi), scalar2=float(TWO_PI), op0=ALU.is_lt, op1=ALU.mult)
                    op("dve", "tensor_tensor", BA, BA, out=src[:, 0:W], in0=src[:, 0:W], in1=angt[:, 0:W], op=ALU.add)
                if use_sign:
                    op("act", "activation", BA + [B_cst], [B_cs], out=dst[:, 0:W], in_=src[:, 0:W], func=AF.Sin, scale=sign_col)
                else:
                    op("act", "activation", BA, [B_cs], out=dst[:, 0:W], in_=src[:, 0:W], func=AF.Sin)

        def rope(bk, bb, W, dst_ap, dst_buf):
            rope_tb, B_rtb = R_rtb.next(); rope_t1, B_rt1 = R_rt1.next(); rope_t2, B_rt2 = R_rt2.next()
            op("act", "activation", [bb], [B_rtb], out=rope_tb[:, 0:W], in_=bk[:, 0:W], func=AF.Copy)
            b2, bb2 = bank()
            op("pe", "matmul", [B_rtb, B_cbf], [bb2], b2[:, 0:W], lhsT=perm_bf[:, :], rhs=rope_tb[:, 0:W], start=True, stop=True)
            op("dve", "tensor_tensor", [bb, B_cs], [B_rt1], out=rope_t1[:, 0:W], in0=bk[:, 0:W], in1=cosT[:, 0:W], op=ALU.mult)
            op("dve", "tensor_tensor", [bb2, B_cs], [B_rt2], out=rope_t2[:, 0:W], in0=b2[:, 0:W], in1=sinT[:, 0:W], op=ALU.mult)
            op(ROPE_ADD_ENG, "tensor_tensor", [B_rt1, B_rt2], [dst_buf], out=dst_ap, in0=rope_t1[:, 0:W], in1=rope_t2[:, 0:W], op=ALU.add)

        def proj_fm(sl, bsl, c0, W, k=0, nrows=128):
            bk, bb = bank(k)
            for kc in range(KC):
                op("pe", "matmul", [bsl, B_hTs[k][kc]], [bb], bk[0:nrows, 0:W], lhsT=sl[:, kc, c0:c0 + nrows], rhs=hTs[k][:, kc, 0:W],
                   start=(kc == 0), stop=(kc == KC - 1))
            return bk, bb

        def proj_tm(sl, bsl, s, ncols=512, k=0):
            bk, bb = bank(k)
            for kc in range(KC):
                op("pe", "matmul", [bsl, B_hTs[k][kc]], [bb], bk[:, 0:ncols], lhsT=hTs[k][:, kc, s * 128:(s + 1) * 128], rhs=sl[:, kc, 0:ncols],
                   start=(kc == 0), stop=(kc == KC - 1))
            return bk, bb

        def mixer_gen(main, x_src, pos_ap, W, par, first_main_after_warm, last_pre, tag='', k=0):
            nsub = W // 128
            xbuf, B_x = xbufs[par], B_xs[par]
            hT, B_hT = hTs[k], B_hTs[k]
            lraT, B_lra = lraTs[k], B_lras[k]
            E2x, B_E2x = (E2, E1)[k], (B_E2, B_E1)[k]
            c_lastx, B_csx = (c_last, c_mid)[k], (B_cs_h, B_cs2_h)[k]
            koutx, B_koutx = (kout, kmid)[k], (B_kout, B_kmid)[k]
            vax, B_vax = (va, r1)[k], (B_va, B_r1)[k]
            kTMx, B_kTMx = (koutTM, AT)[k], (B_kTM, B_AT)[k]
            need_b = main or last_pre
            dma("sp", xbuf[:, 0:nsub, :], x_src.rearrange("(s p) d -> p s d", p=128), [], B_x[0:nsub], "x%d" % par)
            if main:
                op("pool", "tensor_copy", [B_kTb], [B_kTb], out=kTb[:, 0:128], in_=kTb[:, prev_w[0]:prev_w[0] + 128])
                op("pool", "tensor_copy", [B_vbx[prev_w[0] // 128]], [B_vbx[0]], out=vbx[:, 0, :, :], in_=vbx[:, prev_w[0] // 128, :, :])
            if first_main_after_warm:
                op("pool", "tensor_scalar", [B_vbx[0], B_flag], [B_vbx[0]], out=vbx[:, 0, :, :], in0=vbx[:, 0, :, :], scalar1=flag[:, 0:1], scalar2=None, op0=ALU.mult)
                for h in range(4):
                    op("pool", "tensor_scalar", [B_S[h], B_flag], [B_S[h]], out=S[:, h, :], in0=S[:, h, :], scalar1=flag[:, 0:1], scalar2=None, op0=ALU.mult)
                    op("pool", "tensor_copy", [B_S[h]], [B_Sb[h]], out=Sb[:, h, :], in_=S[:, h, :])
            prev_w[0] = W
            if need_b:
                rope_tables(pos_ap, W)
            yield
            ckpt(tag + ':A')
            to_featmajor([(xbuf[:, s, :], B_x[s]) for s in range(nsub)], nsub, W, 0, k)
            ckpt(tag + ':B')
            yield
            bk, bb = bank(k)
            for kc in range(KC):
                op("pe", "matmul", [B_wsm, B_hT[kc]], [bb], bk[0:16, 0:W], lhsT=w_sm[:, kc, 0:16], rhs=hT[:, kc, 0:W], start=(kc == 0), stop=(kc == KC - 1))
            op("act", "activation", [bb], [B_lra], out=lraT[:, 0:W], in_=bk[0:16, 0:W], func=AF.Copy)
            for h in range(4):
                bk, bb = bank(k)
                etmp, B_et = R_et.next(); ltmp, B_lt = R_lt.next(); Ltmp, B_Lt = R_Lt.next()
                op("pe", "matmul", [B_wlr, B_lra], [bb], bk[:, 0:W], lhsT=w_lr[:, h * 128:(h + 1) * 128], rhs=lraT[:, 0:W], start=True, stop=True)
                op("act", "activation", [bb, B_negb], [B_et], out=etmp[:, 0:W], in_=bk[:, 0:W], func=AF.Exp, scale=-1.0, bias=negb[:, h:h + 1])
                op("act", "activation", [B_et, B_eps], [B_lt], out=ltmp[:, 0:W], in_=etmp[:, 0:W], func=AF.Ln, bias=one_col, scale=1.0)
                for s in range(nsub):
                    op("dve", "tensor_tensor_scan", [B_lt, B_cst], [B_Lt], out=Ltmp[:, s * 128:(s + 1) * 128], data0=rmask128,
                       data1=ltmp[:, s * 128:(s + 1) * 128], initial=0.0, op0=ALU.mult, op1=ALU.add)
                op("act", "activation", [B_Lt], [B_E2x[h]], out=E2x[:, h, 0:W], in_=Ltmp[:, 0:W], func=AF.Exp, scale=1.0 / 16.0)
                L3 = Ltmp[:, 0:W].rearrange("p (s t) -> p s t", t=128)
                op("act", "activation", [B_Lt], [B_csx[h]], out=c_lastx[:, h, 0:nsub], in_=L3[:, :, 127], func=AF.Exp, scale=-1.0 / 16.0)
                if main:
                    op("act", "activation", [B_Lt], [B_E1[h]], out=E1[:, h, 0:W], in_=Ltmp[:, 0:W], func=AF.Exp, scale=-1.0 / 16.0)
                    op("act", "activation", [B_Lt], [B_cs_h[h]], out=c_mid[:, h, 0:nsub], in_=L3[:, :, 63], func=AF.Exp, scale=-1.0 / 16.0)
                    op("act", "activation", [B_Lt], [B_cs_h[h]], out=c_midn[:, h, 0:nsub], in_=L3[:, :, 63], func=AF.Exp, scale=1.0 / 16.0)
            ckpt(tag + ':C1')
            yield
            if need_b:
              bk, bb = bank(k)
              for kc in range(KC):
                op("pe", "matmul", [B_wsm, B_hT[kc]], [bb], bk[:, 0:W], lhsT=w_sm[:, kc, 16:144], rhs=hT[:, kc, 0:W], start=(kc == 0), stop=(kc == KC - 1))
              rope(bk, bb, W, kTb[:, 128:128 + W], B_kTb)
            ckpt(tag + ':C2')
            if main:
                sl, bsl = load_slab(s_fm, B_scr["fm"], 0, KC, 0)
                for h in range(4):
                    bk, bb = proj_fm(sl, bsl, h * 128, W, k)
                    ftmp, B_ft = R_ft.next()
                    op("dve", "scalar_tensor_tensor", [bb, B_E1[h]], [B_ft], out=ftmp[:, 0:W], in0=bk[:, 0:W], scalar=float(128 ** -0.5),
                       in1=E1[:, h, 0:W], op0=ALU.mult, op1=ALU.mult)
                    op("pool", "tensor_copy", [B_ft], [B_qin[h]], out=qin[:, h, 0:W], in_=ftmp[:, 0:W])
                    op("pool", "tensor_tensor", [B_ft, B_cs_h[h]], [B_qmid[h]], out=qmid[:, h, 0:W].rearrange("p (s t) -> p s t", t=128),
                       in0=ftmp[:, 0:W].rearrange("p (s t) -> p s t", t=128),
                       in1=c_midn[:, h, 0:nsub].unsqueeze(2).to_broadcast([128, nsub, 128]), op=ALU.mult)
            ckpt(tag + ':C3')
            yield
            sl, bsl = load_slab(s_fm, B_scr["fm"], 0, KC, 512, k=k)
            for h in range(4):
                bk, bb = proj_fm(sl, bsl, h * 128, W, k)
                ftmp2, B_ft2 = R_ft2.next()
                op("dve", "tensor_tensor", [bb, B_E2x[h]], [B_ft2], out=ftmp2[:, 0:W], in0=bk[:, 0:W], in1=E2x[:, h, 0:W], op=ALU.mult)
                op("pool", "tensor_tensor", [B_ft2, B_csx[h]], [B_koutx[h]], out=koutx[:, h, 0:W].rearrange("p (s t) -> p s t", t=128),
                   in0=ftmp2[:, 0:W].rearrange("p (s t) -> p s t", t=128),
                   in1=c_lastx[:, h, 0:nsub].unsqueeze(2).to_broadcast([128, nsub, 128]), op=ALU.mult)
                if main:
                    op("pool", "tensor_tensor", [B_ft2, B_cs_h[h]], [B_kmid[h]], out=kmid[:, h, 0:W].rearrange("p (s t) -> p s t", t=128),
                       in0=ftmp2[:, 0:W].rearrange("p (s t) -> p s t", t=128),
                       in1=c_mid[:, h, 0:nsub].unsqueeze(2).to_broadcast([128, nsub, 128]), op=ALU.mult)
            ckpt(tag + ':C4')
            yield
            if main:
                for j in range(2):
                    sl, bsl = load_slab(s_fm, B_scr["fm"], 0, KC, 1024 + j * 512)
                    for cc in range(4):
                        c = j * 4 + cc
                        bk, bb = proj_fm(sl, bsl, cc * 128, W)
                        ckpt(tag + ':Q%dp' % c)
                        rope(bk, bb, W, qTb[:, c, 0:W], B_qTb[c])
                        ckpt(tag + ':Q%d' % c)
                    yield
            ckpt(tag + ':C')
            yield
            for j in range(2):
                sl, bsl = load_slab(s_tm, B_scr["tm"], 0, KC, j * 512, k=k)
                for s in range(nsub):
                    bk, bb = proj_tm(sl, bsl, s, 512, k)
                    op("act", "activation", [bb], [B_vax[s]], out=vax[:, s, j * 512:(j + 1) * 512], in_=bk[:, :], func=AF.Copy)
            yield
            for s in (range(nsub) if need_b else ()):
                bk, bb = bank(k)
                for kc in range(KC):
                    op("pe", "matmul", [B_wsm, B_hT[kc]], [bb], bk[:, 0:128], lhsT=hT[:, kc, s * 128:(s + 1) * 128], rhs=w_sm[:, kc, 144:272],
                       start=(kc == 0), stop=(kc == KC - 1))
                op("dve", "tensor_copy", [bb], [B_vbx[1 + s]], out=vbx[:, 1 + s, :, 0:64], in_=bk[:, 0:128].rearrange("p (g d) -> p g d", g=2))
            if main:
                for j in range(2):
                    sl, bsl = load_slab(s_tm, B_scr["tm"], 0, KC, 1024 + j * 512)
                    for s in range(nsub):
                        bk, bb = proj_tm(sl, bsl, s, 512, k)
                        siltmp, B_sil = R_sil.next()
                        op("act", "activation", [bb], [B_sil], out=siltmp[:, :], in_=bk[:, :], func=AF.Silu)
                        op("pool", "tensor_tensor", [B_sil, B_gn], [B_r1[s]], out=r1[:, s, j * 512:(j + 1) * 512].rearrange("p (a v) -> p a v", a=2),
                           in0=siltmp[:, :].rearrange("p (a v) -> p a v", a=2),
                           in1=gnorm_bc[:, :].unsqueeze(1).to_broadcast([128, 2, 256]), op=ALU.mult)
                    yield
                for gi, (dstt, dstb) in enumerate(((sa, B_sa), (sbg, B_sbg))):
                    for j in range(2):
                        sl, bsl = load_slab(s_tm, B_scr["tm"], 0, KC, 2048 + gi * 1024 + j * 512)
                        for s in range(nsub):
                            bk, bb = proj_tm(sl, bsl, s, 512, k)
                            op("act", "activation", [bb], [dstb[s]], out=dstt[:, s, j * 512:(j + 1) * 512], in_=bk[:, :], func=AF.Sigmoid)
                        yield
                wo_slabs = [load_slab(s_o, B_scr["o"], 0, KC, j * 512) for j in range(2)]
            ckpt(tag + ':D')
            yield 'E'
            for s in range(nsub):
                t0 = s * 128
                if main:
                    bA, bbA = bank(k)
                    for h in range(4):
                        op("pe", "matmul", [B_kmid[h], B_qmid[h]], [bbA], bA[:, h * 128:(h + 1) * 128], lhsT=kmid[:, h, t0:t0 + 128],
                           rhs=qmid[:, h, t0:t0 + 128], start=True, stop=True)
                    op("dve", "tensor_tensor", [bbA, B_cbf], [B_AT], out=AT[:, :, :], in0=bA[:, :].rearrange("p (h t) -> p h t", h=4),
                       in1=mcur_bf[:, :].unsqueeze(1).to_broadcast([128, 4, 128]), op=ALU.mult)
                bT, bbT = bank(k)
                bTb = bT[:, :].bitcast(BF16)
                for h in range(4):
                    op("pe", "transpose", [B_koutx[h], B_cbf], [bbT], bTb[:, h * 128:(h + 1) * 128], koutx[:, h, t0:t0 + 128], ident_bf[:, :])
                op("act", "activation", [bbT], [B_kTMx], out=kTMx[:, :, :], in_=bTb[:, 0:512].rearrange("p (h t) -> p h t", h=4), func=AF.Copy)
                if main:
                    obanks = [bank(k), bank(k)]
                    for h in range(4):
                        bo, bbo = obanks[h // 2]
                        oap = bo[:, (h % 2) * 256:(h % 2) * 256 + 256]
                        op("pe", "matmul", [B_AT, B_va[s]], [bbo], oap, lhsT=AT[:, h, :], rhs=va[:, s, h * 256:(h + 1) * 256], start=True, stop=False)
                        op("pe", "matmul", [B_qin[h], B_Sb[h]], [bbo], oap, lhsT=qin[:, h, t0:t0 + 128], rhs=Sb[:, h, :], start=False, stop=True)
                sbanks = [bank(k), bank(k)]
                for h in range(4):
                    bs_, bbs = sbanks[h // 2]
                    sap = bs_[:, (h % 2) * 256:(h % 2) * 256 + 256]
                    op("pe", "matmul", [B_kTMx, B_vax[s]], [bbs], sap, lhsT=kTMx[:, h, :], rhs=vax[:, s, h * 256:(h + 1) * 256], start=True, stop=True)
                    op("dve", "scalar_tensor_tensor", [bbs, B_S[h], B_csx[h]], [B_S[h]], out=S[:, h, :], in0=S[:, h, :], scalar=c_lastx[:, h, s:s + 1],
                       in1=sap, op0=ALU.mult, op1=ALU.add)
                    op("pool", "tensor_copy", [B_S[h]], [B_Sb[h]], out=Sb[:, h, :], in_=S[:, h, :])
                if not main:
                    continue
                ckpt(tag + ':GLA0')
                yield
                for h in range(4):
                    bo, bbo = obanks[h // 2]
                    oap = bo[:, (h % 2) * 256:(h % 2) * 256 + 256]
                    sqj, B_sqj = R_sqj.next()
                    op("act", "activation", [bbo], [B_sqj, B_rms], out=sqj[:, :], in_=oap, func=AF.Square, accum_out=ssq[:, h:h + 1])
                op("act", "activation", [B_rms, B_eps], [B_rms], out=lnms[:, :], in_=ssq[:, :], func=AF.Ln, scale=1.0 / 256.0, bias=eps_rms)
                op("act", "activation", [B_rms], [B_rms], out=rstd_a[:, :], in_=lnms[:, :], func=AF.Exp, scale=-0.5)
                for h in range(4):
                    bo, bbo = obanks[h // 2]
                    oap = bo[:, (h % 2) * 256:(h % 2) * 256 + 256]
                    op("dve", "scalar_tensor_tensor", [bbo, B_rms, B_r1[s]], [B_ya], out=ya[:, h * 256:(h + 1) * 256], in0=oap, scalar=rstd_a[:, h:h + 1],
                       in1=r1[:, s, h * 256:(h + 1) * 256], op0=ALU.mult, op1=ALU.mult)
                ckpt(tag + ':GLA')
                yield
                for kb in range(2):
                    kcol = t0 + kb * 128
                    mk = mprev_bf if kb == 0 else mcur_bf
                    for half in range(2):
                        gb = [bank(k), bank(k)]
                        for sl_ in range(4):
                            c = half * 4 + sl_
                            for g in range(2):
                                bk, bb = gb[g]
                                op("pe", "matmul", [B_kTb, B_qTb[c]], [bb], bk[:, sl_ * 128:(sl_ + 1) * 128], lhsT=kTb[g * 64:(g + 1) * 64, kcol:kcol + 128],
                                   rhs=qTb[g * 64:(g + 1) * 64, c, t0:t0 + 128], start=True, stop=True)
                        for g in range(2):
                            bk, bb = gb[g]
                            pi = kb * 4 + half * 2 + g
                            op("act", "activation", [bb], [B_P[pi]], out=Pt[:, pi, :], in_=bk[:, :], func=AF.Exp, scale=0.125)
                            op("pool", "tensor_tensor", [B_P[pi], B_cbf], [B_P[pi]], out=Pt[:, pi, :].rearrange("p (a t) -> p a t", a=4),
                               in0=Pt[:, pi, :].rearrange("p (a t) -> p a t", a=4), in1=mk[:, :].unsqueeze(1).to_broadcast([128, 4, 128]), op=ALU.mult)
                ckpt(tag + ':SC')
                yield
                ebanks = [bank(k), bank(k), bank(k)]
                for hd in range(16):
                    g = hd // 8
                    c = hd % 8
                    half, sl_ = c // 4, c % 4
                    be, bbe = ebanks[hd // 7]
                    eap = be[:, (hd % 7) * 65:(hd % 7) * 65 + 65]
                    for kb in range(2):
                        pi = kb * 4 + half * 2 + g
                        op("pe", "matmul", [B_P[pi], B_vbx[s + kb]], [bbe], eap, lhsT=Pt[:, pi, sl_ * 128:(sl_ + 1) * 128],
                           rhs=vbx[:, s + kb, g, :], start=(kb == 0), stop=(kb == 1))
                for gi, (h0, nh) in enumerate(((0, 7), (7, 7), (14, 2))):
                    be, bbe = ebanks[gi]
                    e3 = be[:, 0:nh * 65].rearrange("p (h d) -> p h d", d=65)
                    op("dve", "tensor_tensor", [bbe, B_es], [B_den], out=den[:, h0:h0 + nh], in0=e3[:, :, 64], in1=esink[:, h0:h0 + nh], op=ALU.add)
                    op("dve", "reciprocal", [B_den], [B_den], out=rden[:, h0:h0 + nh], in_=den[:, h0:h0 + nh])
                    op("dve", "tensor_tensor", [bbe, B_den], [B_tb], out=tbv[:, h0 * 64:(h0 + nh) * 64].rearrange("p (h d) -> p h d", d=64),
                       in0=e3[:, :, 0:64], in1=rden[:, h0:h0 + nh].unsqueeze(2).to_broadcast([128, nh, 64]), op=ALU.mult)
                ckpt(tag + ':SWA')
                yield
                op("pool", "tensor_tensor", [B_ya, B_sa[s]], [B_ya], out=ya[:, :], in0=ya[:, :], in1=sa[:, s, :], op=ALU.mult)
                op("pool", "tensor_tensor", [B_tb, B_sbg[s]], [B_tb], out=tbv[:, :], in0=tbv[:, :], in1=sbg[:, s, :], op=ALU.mult)
                op("pool", "tensor_tensor", [B_ya, B_tb], [B_y], out=ybf[:, :], in0=ya[:, :], in1=tbv[:, :], op=ALU.add)
                bk, bb = bank(k)
                bkb = bk[:, :].bitcast(BF16)
                for kc in range(KC):
                    op("pe", "transpose", [B_y, B_cbf], [bb], bkb[:, kc * 128:(kc + 1) * 128], ybf[:, kc * 128:(kc + 1) * 128], ident_bf[:, :])
                op("act", "activation", [bb], [B_yT], out=yT[:, :, :], in_=bkb[:, :].rearrange("p (k t) -> p k t", k=KC), func=AF.Copy)
                for j in range(2):
                    sl, bsl = wo_slabs[j]
                    bk, bb = bank(k)
                    for kc in range(KC):
                        op("pe", "matmul", [B_yT, bsl], [bb], bk[:, :], lhsT=yT[:, kc, :], rhs=sl[:, kc, :], start=(kc == 0), stop=(kc == KC - 1))
                    op("dve", "tensor_tensor", [bb, B_gate], [B_tres], out=tres[:, j * 512:(j + 1) * 512], in0=bk[:, :], in1=gate_bc[:, 0, j * 512:(j + 1) * 512], op=ALU.mult)
                op("dve", "scalar_tensor_tensor", [B_x[s], B_tres], [B_x[s]], out=xbuf[:, s, :], in0=xbuf[:, s, :], scalar=float(ALPHA), in1=tres[:, :], op0=ALU.mult, op1=ALU.add)
            if not main:
                return
            ckpt(tag + ':WO')
            yield
            ln_affine(nsub, 0, xbuf, B_x, 0)
            ckpt(tag + ':LN')
            yield

        def ffn_gen(W, par, out_ap, scale_halo, tag=''):
            nsub = W // 128
            xbuf, B_x = xbufs[par], B_xs[par]
            if scale_halo:
                op("pool", "tensor_scalar", [B_halo, B_flag], [B_halo], out=halo[:, :, :], in0=halo[:, :, :], scalar1=flag[:, 0:1], scalar2=None, op0=ALU.mult)
            to_featmajor([(xbuf[:, s, :], B_x[s]) for s in range(nsub)], nsub, W, 2, 1)
            yield
            for j in range(11):
                sl, bsl = load_slab(s_up, B_scr["up"], 0, KC, j * 512, k=1)
                for e in range(4):
                    qi = 4 * j + e
                    bk, bb = proj_fm(sl, bsl, e * 128, W, 1)
                    ub, Bu = R_u.next()
                    op("act", "activation", [bb], [Bu], out=ub[:, 0:W], in_=bk[:, 0:W], func=AF.Copy)
                    op("pool", "tensor_scalar", [Bu, B_cw], [B_acc[e]], out=acc[:, e, 0:W], in0=ub[:, 0:W],
                       scalar1=cw_col[:, 2 * NUP + qi:2 * NUP + qi + 1], scalar2=cb_col[:, qi:qi + 1], op0=ALU.mult, op1=ALU.add)
                    op("pool", "tensor_copy", [Bu], [B_halon], out=halo_new[:, qi, :], in_=ub[:, W - 2:W])
                    op("dve", "scalar_tensor_tensor", [Bu, B_cw, B_acc[e]], [B_acc[e]], out=acc[:, e, 1:W], in0=ub[:, 0:W - 1], scalar=cw_col[:, NUP + qi:NUP + qi + 1],
                       in1=acc[:, e, 1:W], op0=ALU.mult, op1=ALU.add)
                    op("dve", "scalar_tensor_tensor", [Bu, B_cw, B_acc[e]], [B_acc[e]], out=acc[:, e, 2:W], in0=ub[:, 0:W - 2], scalar=cw_col[:, qi:qi + 1],
                       in1=acc[:, e, 2:W], op0=ALU.mult, op1=ALU.add)
                    op("dve", "scalar_tensor_tensor", [B_halo, B_cw, B_acc[e]], [B_acc[e]], out=acc[:, e, 0:2], in0=halo[:, qi, :], scalar=cw_col[:, qi:qi + 1],
                       in1=acc[:, e, 0:2], op0=ALU.mult, op1=ALU.add)
                    op("dve", "scalar_tensor_tensor", [B_halo, B_cw, B_acc[e]], [B_acc[e]], out=acc[:, e, 0:1], in0=halo[:, qi, 1:2], scalar=cw_col[:, NUP + qi:NUP + qi + 1],
                       in1=acc[:, e, 0:1], op0=ALU.mult, op1=ALU.add)
                for e in range(2):
                    op("act", "activation", [B_acc[e]], [B_gt[e]], out=gtmp[:, e, 0:W], in_=acc[:, e, 0:W], func=AF.Gelu)
                    op("pool", "tensor_tensor", [B_gt[e], B_acc[2 + e]], [B_f[j]], out=fbuf[:, 2 * j + e, 0:W], in0=gtmp[:, e, 0:W], in1=acc[:, 2 + e, 0:W], op=ALU.mult)
                yield
            ckpt(tag + ':UP')
            op("pool", "tensor_copy", [B_halon], [B_halo], out=halo[:, :, :], in_=halo_new[:, :, :])
            for hh in range(2):
                dbanks = [bank(1) for _ in range(nsub)]
                for kg in range(3):
                    nk = 8 if kg < 2 else 6
                    sl, bsl = load_slab(s_down, B_scr["down"], kg * 8, nk, hh * 512, k=1)
                    for s in range(nsub):
                        bk, bb = dbanks[s]
                        for kk in range(nk):
                            kc = kg * 8 + kk
                            op("pe", "matmul", [B_f[kc // 2], bsl], [bb], bk[:, :], lhsT=fbuf[:, kc, s * 128:(s + 1) * 128], rhs=sl[:, kk, :],
                               start=(kc == 0), stop=(kc == 21))
                    yield
                for s in range(nsub):
                    bk, bb = dbanks[s]
                    tb_ = B_acc[2 * hh]
                    op("dve", "tensor_tensor", [bb, B_gate, B_acc[2 * hh + 1]], [tb_, B_acc[2 * hh + 1]], out=tres_f[:, hh * 512:(hh + 1) * 512], in0=bk[:, :], in1=gate_bc[:, 1, hh * 512:(hh + 1) * 512], op=ALU.mult)
                    op("dve", "scalar_tensor_tensor", [B_x[s], tb_], [B_x[s]], out=xbuf[:, s, hh * 512:(hh + 1) * 512], in0=xbuf[:, s, hh * 512:(hh + 1) * 512],
                       scalar=float(ALPHA), in1=tres_f[:, hh * 512:(hh + 1) * 512], op0=ALU.mult, op1=ALU.add)
            ckpt(tag + ':DOWN')
            yield
            ln_affine(nsub, 2, xbuf, B_x, 1)
            if out_ap is not None:
                dma("sp", out_ap.rearrange("(s p) d -> p s d", p=128), xbuf[:, 0:nsub, :], B_x[0:nsub], [], "o%d" % par)
            yield

        tres_f = acc[:, :, :].rearrange("p e w -> p (e w)") if WM == 256 else sb("tres_f", [128, D])

        def ln_affine(nsub, gi, xbuf, B_x, k):
            ln_stats([(xbuf[:, s, :], B_x[s]) for s in range(nsub)], nsub, k)
            rstd, nbias, B_ln = rstds[k], nbiass[k], B_lns[k]
            for s in range(nsub):
                op("act", "activation", [B_x[s], B_ln], [B_x[s]], out=xbuf[:, s, :], in_=xbuf[:, s, :], func=AF.Identity,
                   scale=rstd[:, s:s + 1], bias=nbias[:, s:s + 1])
                op("pool", "tensor_tensor", [B_x[s], B_lnbc], [B_x[s]], out=xbuf[:, s, :], in0=xbuf[:, s, :], in1=ln_bc[:, gi, :], op=ALU.mult)
                op("pool", "tensor_tensor", [B_x[s], B_lnbc], [B_x[s]], out=xbuf[:, s, :], in0=xbuf[:, s, :], in1=ln_bc[:, gi + 1, :], op=ALU.add)

        def interleave(*gens):
            gens = [g for g in gens if g is not None]
            while gens:
                for g in list(gens):
                    try:
                        next(g)
                    except StopIteration:
                        gens.remove(g)

        def interleave_main(g1, g2):
            phase = 0
            while True:
                try:
                    v = next(g1)
                except StopIteration:
                    break
                if v == 'E':
                    phase = 1
                if phase == 1 and g2 is not None:
                    for _ in range(FFN_RATIO):
                        try:
                            next(g2)
                        except StopIteration:
                            g2 = None
                            break
            if g2 is not None:
                interleave(g2)

        prev_w = [128]
        ckpt("setup")
        r = 0
        pgens = []
        for i, W in enumerate(cfg.pre):
            pgens.append(mixer_gen(False, x_pre[r:r + W, :], pos_pre[:, r:r + W], W, i % 2, False, i == len(cfg.pre) - 1, 'pre%d' % i, k=i % 2))
            r += W
        for i in range(0, len(pgens), 2):
            if INTERLEAVE:
                interleave(*pgens[i:i + 2])
            else:
                for g in pgens[i:i + 2]:
                    interleave(g)
            ckpt("pre%d" % i)
        r = 0
        ro = 0
        pending = None
        for i, W in enumerate(cfg.main):
            oap = None
            if i > 0:
                oap = out_d[ro:ro + W, :]
                ro += W
            g1 = mixer_gen(True, x_main[r:r + W, :], pos_main[:, r:r + W], W, i % 2, i == 1, False, 'main%d' % i)
            if INTERLEAVE and FFN_RATIO > 0:
                interleave_main(g1, pending)
            elif INTERLEAVE:
                interleave(g1, pending)
            else:
                interleave(pending)
                interleave(g1)
            pending = ffn_gen(W, i % 2, oap, i == 1, 'main%d' % i)
            r += W
            ckpt("main%d" % i)
        interleave(pending)

        with nc.Block() as block:
            pg.emit(nc, block, st)
    return nc


def make_consts():
    c = np.zeros((128, 5 * 128 + 2), np.float32)
    c[:, 0:128] = np.eye(128, dtype=np.float32)
    perm = np.zeros((128, 128), np.float32)
    for m in range(128):
        d = m % 64
        if d < 8:
            perm[m + 8, m] = 1.0
        elif d < 16:
            perm[m - 8, m] = 1.0
    c[:, 128:256] = perm
    j = np.arange(128)[:, None]
    i = np.arange(128)[None, :]
    c[:, 256:384] = (j <= i).astype(np.float32)
    c[:, 384:512] = (j > i).astype(np.float32)
    rm = np.ones((128, 128), np.float32)
    rm[:, 0] = 0.0
    c[:, 512:640] = rm
    inv_freq = (500000.0 ** (-(np.arange(0, 16, 2, dtype=np.float32) / 16.0))).astype(np.float32)
    for p in range(128):
        d = p % 64
        if d < 16:
            c[p, 640] = inv_freq[d % 8]
            c[p, 641] = -1.0 if d < 8 else 1.0
    return c


def host_inputs(cfg, x, c, positions, w_ada, b_ada, w_in, gla_w_lr, gla_b_lr, gla_norm_g, swa_sinks,
                w_o, ln1_g, ln1_b, w_up, conv_w, conv_b, w_down, ln2_g, ln2_b, core_map):
    f = np.float32
    w_in = np.asarray(w_in[0], f)
    sp = np.cumsum([512, 512, 1024, 1024, 16, 1024, 128, 128, 2048])[:-1]
    qa, ka, va_, ra, lra, qb, kb, vb, gates = np.split(w_in, sp, axis=1)
    qb_perm = np.concatenate([np.concatenate([qb[:, 64 * cc:64 * cc + 64], qb[:, 64 * (cc + 8):64 * (cc + 8) + 64]], axis=1) for cc in range(8)], axis=1)
    w_fm = np.ascontiguousarray(np.concatenate([qa, ka, qb_perm], axis=1))
    w_tm = np.ascontiguousarray(np.concatenate([va_, ra, gates], axis=1))
    w_sm = np.ascontiguousarray(np.concatenate([lra, kb, vb], axis=1))
    wu = np.asarray(w_up[0], f)
    cwv = np.asarray(conv_w[0], f)
    cbv = np.asarray(conv_b[0], f)
    order = []
    for j in range(11):
        order += list(range(256 * j, 256 * j + 256)) + list(range(DFF + 256 * j, DFF + 256 * j + 256))
    order = np.asarray(order)
    w_up_p = np.ascontiguousarray(wu[:, order])
    cw_p = cwv[:, order]
    cb_p = cbv[order]
    cw_col = np.ascontiguousarray(cw_p.reshape(3, NUP, 128).transpose(2, 0, 1).reshape(128, 3 * NUP))
    cb_col = np.ascontiguousarray(cb_p.reshape(NUP, 128).T)
    b_ada_v = np.asarray(b_ada[0], f)
    b_ada_col = np.ascontiguousarray(b_ada_v.reshape(6, KC, 128).transpose(2, 0, 1).reshape(128, 48))
    ln_rows = np.stack([np.asarray(a[0], f) for a in (ln1_g, ln1_b, ln2_g, ln2_b)])
    consts = make_consts()
    shared = {
        "w_ada": np.ascontiguousarray(np.asarray(w_ada[0], f)), "b_ada_col": b_ada_col, "b_ada_row": b_ada_v[None, :].copy(),
        "w_fm": w_fm, "w_tm": w_tm, "w_sm": w_sm, "w_lr": np.ascontiguousarray(np.asarray(gla_w_lr[0], f)),
        "b_lr_col": np.ascontiguousarray(np.asarray(gla_b_lr[0], f).reshape(4, 128).T),
        "gnorm": np.asarray(gla_norm_g[0], f)[None, :].copy(), "sinks": np.asarray(swa_sinks[0], f)[None, :].copy(),
        "w_o": np.ascontiguousarray(np.asarray(w_o[0], f)), "ln_rows": ln_rows, "w_up": w_up_p, "cw_col": cw_col, "cb_col": cb_col,
        "w_down": np.ascontiguousarray(np.asarray(w_down[0], f)), "consts": consts,
    }
    x = np.asarray(x, f)
    positions = np.asarray(positions, np.int32)
    c = np.asarray(c, f)
    maps = []
    for (b, start) in core_map:
        m = dict(shared)
        warm = cfg.main[0]
        if start > 0:
            assert start - warm == cfg.npre
            m["x_pre"] = np.ascontiguousarray(x[b, 0:cfg.npre])
            m["pos_pre"] = np.ascontiguousarray(positions[b, 0:cfg.npre][None, :])
            m["x_main"] = np.ascontiguousarray(x[b, start - warm:start + cfg.nout])
            m["pos_main"] = np.ascontiguousarray(positions[b, start - warm:start + cfg.nout][None, :])
            m["flag"] = np.ones((128, 1), f)
        else:
            m["x_pre"] = np.ascontiguousarray(x[b, 0:cfg.npre])
            m["pos_pre"] = np.ascontiguousarray(positions[b, 0:cfg.npre][None, :])
            m["x_main"] = np.ascontiguousarray(np.concatenate([x[b, 0:warm], x[b, 0:cfg.nout]], axis=0))
            m["pos_main"] = np.ascontiguousarray(np.concatenate([positions[b, 0:warm], positions[b, 0:cfg.nout]])[None, :])
            m["flag"] = np.zeros((128, 1), f)
        m["c_col"] = np.ascontiguousarray(c[b].reshape(KC, 128).T)
        maps.append(m)
    return maps


_NC_CACHE = {}


def kernel(x, c, positions, w_ada, b_ada, w_in, gla_w_lr, gla_b_lr, gla_norm_g, swa_sinks,
           w_o, ln1_g, ln1_b, w_up, conv_w, conv_b, w_down, ln2_g, ln2_b):
    cfg = FULL
    B, S, _ = x.shape
    half = S // 2
    core_map = [(b, hf * half) for b in range(B) for hf in range(2)]
    maps = host_inputs(cfg, x, c, positions, w_ada, b_ada, w_in, gla_w_lr, gla_b_lr, gla_norm_g, swa_sinks,
                       w_o, ln1_g, ln1_b, w_up, conv_w, conv_b, w_down, ln2_g, ln2_b, core_map)
    if "nc" not in _NC_CACHE:
        _NC_CACHE["nc"] = build(cfg)
    res = run_bass_kernel_spmd(_NC_CACHE["nc"], maps, core_ids=list(range(len(maps))))
    out = np.zeros((B, S, D), np.float32)
    for i, (b, start) in enumerate(core_map):
        out[b, start:start + cfg.nout] = np.asarray(res.results[i]["out"], np.float32)
    return out
```

```python
import numpy as np
from contextlib import ExitStack
import concourse.bass as bass
import concourse.mybir as mybir
from concourse.bass_utils import run_bass_kernel_spmd

F32 = mybir.dt.float32
BF16 = mybir.dt.bfloat16
I32 = mybir.dt.int32
AF = mybir.ActivationFunctionType
ALU = mybir.AluOpType

D = 1024
KC = 8
DFF = 2816
NUP = 44
LN_EPS = 1e-5
RMS_EPS = 1e-6
ALPHA = 2.0 ** 0.25
import os
ROPE_ADD_ENG = os.environ.get("ROPE_ADD_ENG", "pool")
INTERLEAVE = os.environ.get("KINTER", "1") == "1"
TWO_PI = 2.0 * np.pi
CW1 = 6.28125
CW2 = float(TWO_PI - CW1)


class Buf:
    __slots__ = ("name", "w", "r", "excl")

    def __init__(self, name, excl=False):
        self.name = name
        self.excl = excl
        self.w = None
        self.r = {}


class Op:
    __slots__ = ("eng", "fn", "deps", "inc", "dma_key", "dma_seq")


class Prog:
    ENGS = ("pe", "act", "dve", "pool", "sp")

    def __init__(self):
        self.ops = {e: [] for e in self.ENGS}
        self.dma_cnt = {}
        self.dma_last = {}
        self.frozen = False

    def add(self, eng, fn, reads=(), writes=(), dma_key=None):
        if self.frozen:
            return None
        op = Op()
        op.eng, op.fn, op.inc, op.dma_key, op.dma_seq = eng, fn, False, dma_key, 0
        idx = len(self.ops[eng])
        me = (eng, idx)
        deps = set()
        for b in reads:
            if b.w is not None:
                deps.add((b.w, True))
            if b.excl:
                for v in b.r.values():
                    if v[0] != eng:
                        deps.add((v, True))
        for b in writes:
            if b.w is not None:
                deps.add((b.w, False))
            for v in b.r.values():
                deps.add((v, False))
        if dma_key is not None:
            self.dma_cnt[dma_key] = self.dma_cnt.get(dma_key, 0) + 1
            op.dma_seq = self.dma_cnt[dma_key]
            if dma_key in self.dma_last:
                deps.add((self.dma_last[dma_key], True))
            self.dma_last[dma_key] = me
        final = set()
        raw = set(d for d, is_raw in deps if is_raw)
        for d, _ in deps:
            o = self.ops[d[0]][d[1]]
            if o.dma_key is None and d[0] == eng:
                if eng == "pe":
                    continue
            final.add(d)
            if o.dma_key is None:
                o.inc = True
        op.deps = final
        self.ops[eng].append(op)
        for b in reads:
            key = eng if dma_key is None else ("dma", eng, idx)
            b.r[key] = me
        for b in writes:
            b.w = me
            b.r = {}
        return me

    def emit(self, nc, block, stack):
        names = {"pe": "tensor", "act": "scalar", "dve": "vector", "pool": "gpsimd", "sp": "sync"}
        esem = {e: stack.enter_context(nc.semaphore("s_" + e)) for e in self.ENGS}
        dsem = {k: stack.enter_context(nc.semaphore("d_%s" % str(k))) for k in self.dma_cnt}
        cnt = {}
        for e in self.ENGS:
            c = 0
            lst = []
            for o in self.ops[e]:
                if o.inc:
                    c += 1
                lst.append(c)
            cnt[e] = lst
        prog = self

        def make(e):
            def body(engine):
                waited = {}
                for o in prog.ops[e]:
                    for d in sorted(o.deps):
                        od = prog.ops[d[0]][d[1]]
                        if od.dma_key is not None:
                            s, v = dsem[od.dma_key], 16 * od.dma_seq
                        else:
                            s, v = esem[d[0]], cnt[d[0]][d[1]]
                        k = id(s)
                        if waited.get(k, 0) >= v:
                            continue
                        waited[k] = v
                        engine.wait_ge(s, v)
                    ins = o.fn(engine)
                    if o.dma_key is not None:
                        ins.then_inc(dsem[o.dma_key], 16)
                    elif o.inc:
                        ins.then_inc(esem[e], 1)
                if e == "sp":
                    for k, last in prog.dma_last.items():
                        engine.wait_ge(dsem[k], 16 * prog.dma_cnt[k])
            return body

        for e in self.ENGS:
            getattr(block, names[e])(make(e))


class StopBuild(Exception):
    pass


class Cfg:
    stop = None

    def __init__(self, pre_widths, main_widths, nslab=4):
        self.pre = list(pre_widths)
        self.main = list(main_widths)
        self.npre = sum(self.pre)
        self.nmain = sum(self.main)
        self.nout = self.nmain - self.main[0]
        self.wmax = max(self.pre + self.main)
        self.nslab = nslab


FULL = Cfg([256] * 15 + [128], [128] + [256] * 16)


def build(cfg):
    nc = bass.Bass("TRN2", target_bir_lowering=False)
    pg = Prog()
    WM = cfg.wmax
    NSM = WM // 128

    def din(name, shape, dt=F32):
        return nc.dram_tensor(name, list(shape), dt, kind="ExternalInput").ap()

    def dscr(name, shape, dt=BF16):
        return nc.dram_tensor(name, list(shape), dt, kind="Internal").ap()

    x_pre = din("x_pre", [cfg.npre, D])
    x_main = din("x_main", [cfg.nmain, D])
    pos_pre = din("pos_pre", [1, cfg.npre], I32)
    pos_main = din("pos_main", [1, cfg.nmain], I32)
    flag_d = din("flag", [128, 1])
    c_col_d = din("c_col", [128, KC])
    w_ada_d = din("w_ada", [D, 6 * D])
    b_ada_col_d = din("b_ada_col", [128, 48])
    b_ada_row_d = din("b_ada_row", [1, 6 * D])
    w_fm_d = din("w_fm", [D, 2048])
    w_tm_d = din("w_tm", [D, 4096])
    w_sm_d = din("w_sm", [D, 272])
    w_lr_d = din("w_lr", [16, 512])
    b_lr_col_d = din("b_lr_col", [128, 4])
    gnorm_d = din("gnorm", [1, 256])
    sinks_d = din("sinks", [1, 16])
    w_o_d = din("w_o", [D, D])
    ln_rows_d = din("ln_rows", [4, D])
    w_up_d = din("w_up", [D, 2 * DFF])
    cw_col_d = din("cw_col", [128, 3 * NUP])
    cb_col_d = din("cb_col", [128, NUP])
    w_down_d = din("w_down", [DFF, D])
    consts_d = din("consts", [128, 5 * 128 + 2])
    out_d = nc.dram_tensor("out", [cfg.nout, D], F32, kind="ExternalOutput").ap()

    s_ada = dscr("s_ada", [D, 6 * D])
    s_fm = dscr("s_fm", [D, 2048])
    s_tm = dscr("s_tm", [D, 4096])
    s_sm = dscr("s_sm", [D, 272])
    s_o = dscr("s_o", [D, D])
    s_up = dscr("s_up", [D, 2 * DFF])
    s_down = dscr("s_down", [DFF, D])

    st = ExitStack()
    with st:
        def sb(name, shape, dt=F32):
            return st.enter_context(nc.sbuf_tensor(name, list(shape), dt))

        def op(eng, method, reads, writes, *a, **kw):
            return pg.add(eng, lambda e: getattr(e, method)(*a, **kw), reads, writes)

        banks = [st.enter_context(nc.psum_tensor("bank%d" % i, [128, 512], F32)) for i in range(8)]
        bank_bufs = [Buf("bank%d" % i, excl=True) for i in range(8)]
        bank_ctr = [0, 0]
        bank_pool = [[0, 1, 2, 3], [4, 5, 6, 7]]

        def bank(k=0):
            pool = bank_pool[k]
            i = pool[bank_ctr[k] % len(pool)]
            bank_ctr[k] += 1
            return banks[i], bank_bufs[i]

        xbufs = [sb("xbuf%d" % i, [128, NSM, D]) for i in range(2)]
        B_xs = [[Buf("x%d_%d" % (i, s)) for s in range(NSM)] for i in range(2)]
        xns = [sb("xn%d" % i, [128, NSM, D], BF16) for i in range(2)]
        B_xns = [[Buf("xn%d_%d" % (i, s)) for s in range(NSM)] for i in range(2)]
        hTs = [sb("hT%d" % i, [128, KC, WM], BF16) for i in range(2)]
        B_hTs = [[Buf("hT%d_%d" % (i, kc)) for kc in range(KC)] for i in range(2)]
        slabs = [sb("slab%d" % i, [128, KC, 512], BF16) for i in range(cfg.nslab)]
        B_slab = [Buf("slab%d" % i) for i in range(cfg.nslab)]
        cst = sb("cst", [128, 5 * 128 + 2]); B_cst = Buf("cst")
        ident_bf = sb("ident_bf", [128, 128], BF16)
        perm_bf = sb("perm_bf", [128, 128], BF16)
        mcur_bf = sb("mcur_bf", [128, 128], BF16)
        mprev_bf = sb("mprev_bf", [128, 128], BF16)
        B_cbf = Buf("cbf")
        flag = sb("flag_sb", [128, 1]); B_flag = Buf("flag")
        c_col = sb("c_col_sb", [128, KC]); B_ccol = Buf("ccol")
        sc_bf = sb("sc_bf", [128, KC], BF16)
        B_sc = Buf("sc")
        b_ada_col = sb("b_ada_col_sb", [128, 48]); B_bac = Buf("bac")
        modcol = sb("modcol", [128, 4, KC]); B_mod = Buf("mod")
        gate_bc = sb("gate_bc", [128, 2, D]); B_gate = Buf("gate")
        ln_bc = sb("ln_bc", [128, 4, D]); B_lnbc = Buf("lnbc")
        w_lr = sb("w_lr_sb", [16, 512]); B_wlr = Buf("wlr")
        negb = sb("negb", [128, 4]); B_negb = Buf("negb")
        gnorm_bc = sb("gnorm_bc", [128, 256]); B_gn = Buf("gn")
        esink = sb("esink", [128, 16]); B_es = Buf("es")
        cw_col = sb("cw_col_sb", [128, 3 * NUP]); cb_col = sb("cb_col_sb", [128, NUP]); B_cw = Buf("cw")
        w_sm = sb("w_sm_sb", [128, KC, 272], BF16); B_wsm = Buf("wsm")
        statss = [sb("stats%d" % i, [128, NSM, 2, 6]) for i in range(2)]; mvs = [sb("mv%d" % i, [128, NSM, 2]) for i in range(2)]
        lnvs = [sb("lnv%d" % i, [128, NSM]) for i in range(2)]; rstds = [sb("rstd%d" % i, [128, NSM]) for i in range(2)]
        nbiass = [sb("nbias%d" % i, [128, NSM]) for i in range(2)]
        B_lns = [Buf("lnscratch0"), Buf("lnscratch1")]
        posi = sb("posi", [128, WM], I32); angf = sb("angf", [128, WM]); angt = sb("angt", [128, WM])
        angk = sb("angk", [128, WM], I32); angkf = sb("angkf", [128, WM])
        cosT = sb("cosT", [128, WM]); sinT = sb("sinT", [128, WM])
        B_pos = Buf("pos"); B_ang = Buf("ang"); B_cs = Buf("cossin")
        class Rot:
            def __init__(self, name, shape, dt, n):
                self.t = [sb("%s_%d" % (name, i), shape, dt) for i in range(n)]
                self.b = [Buf("%s_%d" % (name, i)) for i in range(n)]
                self.i = 0

            def next(self):
                k = self.i % len(self.t)
                self.i += 1
                return self.t[k], self.b[k]

        R_rtb = Rot("rope_tb", [128, WM], BF16, 1); R_rt1 = Rot("rope_t1", [128, WM], F32, 2); R_rt2 = Rot("rope_t2", [128, WM], F32, 2)
        lraTs = [sb("lraT%d" % i, [16, WM]) for i in range(2)]; B_lras = [Buf("lra0"), Buf("lra1")]
        R_et = Rot("etmp", [128, WM], F32, 1); R_lt = Rot("ltmp", [128, WM], F32, 1); R_Lt = Rot("Ltmp", [128, WM], F32, 1)
        E1 = sb("E1", [128, 4, WM]); E2 = sb("E2", [128, 4, WM]); B_E1 = [Buf("E1_%d" % h) for h in range(4)]
        B_E2 = [Buf("E2_%d" % h) for h in range(4)]
        c_last = sb("c_last", [128, 4, NSM]); c_mid = sb("c_mid", [128, 4, NSM]); c_midn = sb("c_midn", [128, 4, NSM])
        B_cs_h = [Buf("csm%d" % h) for h in range(4)]
        B_cs2_h = [Buf("csm2_%d" % h) for h in range(4)]
        R_ft = Rot("ftmp", [128, WM], F32, 1); R_ft2 = Rot("ftmp2", [128, WM], F32, 1)
        qin = sb("qin", [128, 4, WM], BF16); qmid = sb("qmid", [128, 4, WM], BF16)
        kmid = sb("kmid", [128, 4, WM], BF16); kout = sb("kout", [128, 4, WM], BF16)
        B_qin = [Buf("qin%d" % h) for h in range(4)]; B_qmid = [Buf("qmid%d" % h) for h in range(4)]
        B_kmid = [Buf("kmid%d" % h) for h in range(4)]; B_kout = [Buf("kout%d" % h) for h in range(4)]
        qTb = sb("qTb", [128, 8, WM], BF16); B_qTb = [Buf("qTb%d" % c) for c in range(8)]
        kTb = sb("kTb", [128, 128 + WM], BF16); B_kTb = Buf("kTb")
        va = sb("va", [128, NSM, D], BF16); r1 = sb("r1", [128, NSM, D], BF16)
        sa = sb("sa", [128, NSM, D], BF16); sbg = sb("sbg", [128, NSM, D], BF16)
        B_va = [Buf("va%d" % s) for s in range(NSM)]; B_r1 = [Buf("r1_%d" % s) for s in range(NSM)]
        B_sa = [Buf("sa%d" % s) for s in range(NSM)]; B_sbg = [Buf("sbg%d" % s) for s in range(NSM)]
        R_sil = Rot("siltmp", [128, 512], F32, 1)
        vbx = sb("vbx", [128, 1 + NSM, 2, 65], BF16); B_vbx = [Buf("vbx%d" % s) for s in range(1 + NSM)]
        AT = sb("AT", [128, 4, 128], BF16); B_AT = Buf("AT")
        koutTM = sb("koutTM", [128, 4, 128], BF16); B_kTM = Buf("kTM")
        S = sb("S", [128, 4, 256]); Sb = sb("Sb", [128, 4, 256], BF16)
        B_S = [Buf("S%d" % h) for h in range(4)]; B_Sb = [Buf("Sb%d" % h) for h in range(4)]
        Pt = sb("Pt", [128, 8, 512], BF16); B_P = [Buf("P%d" % i) for i in range(8)]
        R_sqj = Rot("sqj", [128, 256], F32, 1)
        ssq = sb("ssq", [128, 4]); lnms = sb("lnms", [128, 4]); rstd_a = sb("rstd_a", [128, 4]); B_rms = Buf("rms")
        ya = sb("ya", [128, D]); B_ya = Buf("ya")
        sc_rep = ya[:, 0:512].bitcast(BF16).rearrange("p (k m) -> p k m", k=KC)
        tbv = sb("tbv", [128, D]); B_tb = Buf("tb")
        den = sb("den", [128, 16]); rden = sb("rden", [128, 16]); B_den = Buf("den")
        ybf = sb("ybf", [128, D], BF16); B_y = Buf("y")
        yT = sb("yT", [128, KC, 128], BF16); B_yT = Buf("yT")
        tres = ya; B_tres = B_ya
        fbuf = sb("fbuf", [128, 22, WM], BF16); B_f = [Buf("f%d" % j) for j in range(11)]
        acc = sb("acc", [128, 4, WM]); B_acc = [Buf("acc%d" % e) for e in range(4)]
        gtmp = sb("gtmp", [128, 2, WM]); B_gt = [Buf("gt%d" % e) for e in range(2)]
        halo = sb("halo", [128, NUP, 2]); halo_new = sb("halo_new", [128, NUP, 2]); B_halo = Buf("halo"); B_halon = Buf("halon")

        ident_f = cst[:, 0:128]
        rmask128 = cst[:, 512:640]
        freq_col = cst[:, 640:641]
        sign_col = cst[:, 641:642]

        dma_rr = [0]

        def dma(eng, out, in_, reads, writes, key):
            return pg.add(eng, lambda e: e.dma_start(out=out, in_=in_), reads, writes, dma_key=key)

        B_scr = {n: Buf("scr_" + n) for n in ("ada", "fm", "tm", "sm", "o", "up", "down")}
        for n, (src, dst) in {"sm": (w_sm_d, s_sm), "ada": (w_ada_d, s_ada), "fm": (w_fm_d, s_fm),
                              "tm": (w_tm_d, s_tm), "o": (w_o_d, s_o), "up": (w_up_d, s_up),
                              "down": (w_down_d, s_down)}.items():
            rows = src.shape[0]
            for r0 in range(0, rows, 256):
                dma("pool", dst[r0:r0 + 256, :], src[r0:r0 + 256, :], [], [B_scr[n]], "cast_" + n)

        dma("sp", cst[:, :], consts_d[:, :], [], [B_cst], "c0")
        dma("sp", flag[:, :], flag_d[:, :], [], [B_flag], "c1")
        dma("sp", c_col[:, :], c_col_d[:, :], [], [B_ccol], "c2")
        dma("sp", b_ada_col[:, :], b_ada_col_d[:, :], [], [B_bac], "c3")
        dma("sp", w_lr[:, :], w_lr_d[:, :], [], [B_wlr], "c0")
        dma("sp", negb[:, :], b_lr_col_d[:, :], [], [B_negb], "c1")
        dma("sp", gnorm_bc[:, :], gnorm_d.partition_broadcast(128), [], [B_gn], "c2")
        dma("sp", esink[:, :], sinks_d.partition_broadcast(128), [], [B_es], "c3")
        dma("sp", cw_col[:, :], cw_col_d[:, :], [], [B_cw], "c0")
        dma("sp", cb_col[:, :], cb_col_d[:, :], [], [B_cw], "c1")
        for i in range(4):
            dma("sp", ln_bc[:, i, :], ln_rows_d[i:i + 1, :].partition_broadcast(128), [], [B_lnbc], "c2")
        for i, v in enumerate((2, 5)):
            dma("sp", gate_bc[:, i, :], b_ada_row_d[:, v * D:(v + 1) * D].partition_broadcast(128), [], [B_gate], "c3")
        dma("sp", w_sm[:, :, :], s_sm.rearrange("(k p) c -> p k c", p=128), [B_scr["sm"]], [B_wsm], "c0")

        op("dve", "tensor_copy", [B_cst], [B_cbf], out=ident_bf[:, :], in_=cst[:, 0:128])
        op("dve", "tensor_copy", [B_cst], [B_cbf], out=perm_bf[:, :], in_=cst[:, 128:256])
        op("dve", "tensor_copy", [B_cst], [B_cbf], out=mcur_bf[:, :], in_=cst[:, 256:384])
        op("dve", "tensor_copy", [B_cst], [B_cbf], out=mprev_bf[:, :], in_=cst[:, 384:512])
        op("dve", "tensor_scalar", [B_negb], [B_negb], out=negb[:, :], in0=negb[:, :], scalar1=-1.0, scalar2=None, op0=ALU.mult)
        op("act", "activation", [B_es], [B_es], out=esink[:, :], in_=esink[:, :], func=AF.Exp)
        op("pool", "memset", [], B_S, S[:, :, :], 0.0)
        op("pool", "memset", [], B_Sb, Sb[:, :, :], 0.0)
        op("pool", "memset", [], B_vbx, vbx[:, :, :, :], 1.0)
        op("pool", "memset", [], [B_halo], halo[:, :, :], 0.0)
        op("pool", "memset", [], [B_kTb], kTb[:, :], 0.0)
        op("act", "activation", [B_ccol], [B_ccol], out=c_col[:, :], in_=c_col[:, :], func=AF.Silu)
        op("dve", "tensor_copy", [B_ccol], [B_sc], out=sc_bf[:, :], in_=c_col[:, :])
        op("dve", "tensor_copy", [B_ccol], [B_sc, B_ya], out=sc_rep[:, :, :],
           in_=c_col[:, :].unsqueeze(2).to_broadcast([128, KC, 128]))

        def ckpt(name):
            if cfg.stop == name:
                pg.frozen = True

        slab_rr = [0, 0]
        slab_pool = [list(range(0, cfg.nslab - cfg.nslab // 2)), list(range(cfg.nslab - cfg.nslab // 2, cfg.nslab))]

        def load_slab(scr, bscr, k0, nk, c0, ncols=512, k=0):
            pool = slab_pool[k]
            i = pool[slab_rr[k] % len(pool)]
            slab_rr[k] += 1
            src = scr.rearrange("(k p) c -> p k c", p=128)[:, k0:k0 + nk, c0:c0 + ncols]
            dma("sp", slabs[i][:, 0:nk, 0:ncols], src, [bscr], [B_slab[i]], "slab%d" % i)
            return slabs[i], B_slab[i]

        colslot = {0: 0, 1: 1, 3: 2, 4: 3}
        for v in range(6):
            for hh in range(2):
                sl, bsl = load_slab(s_ada, B_scr["ada"], 0, KC, v * D + hh * 512)
                if v in colslot:
                    bk, bb = bank()
                    for cc in range(4):
                        for kc in range(KC):
                            op("pe", "matmul", [bsl, B_sc], [bb], bk[:, cc:cc + 1], lhsT=sl[:, kc, cc * 128:(cc + 1) * 128],
                               rhs=sc_bf[:, kc:kc + 1], start=(kc == 0), stop=(kc == KC - 1))
                    j = colslot[v]
                    op("dve", "tensor_tensor", [bb, B_bac], [B_mod], out=modcol[:, j, hh * 4:hh * 4 + 4], in0=bk[:, 0:4],
                       in1=b_ada_col[:, v * 8 + hh * 4: v * 8 + hh * 4 + 4], op=ALU.add)
                else:
                    g = 0 if v == 2 else 1
                    bk, bb = bank()
                    for kc in range(KC):
                        op("pe", "matmul", [bsl, B_sc, B_ya], [bb], bk[:, :], lhsT=sc_rep[:, kc, :], rhs=sl[:, kc, :],
                           start=(kc == 0), stop=(kc == KC - 1))
                    op("dve", "tensor_tensor", [bb, B_gate], [B_gate], out=gate_bc[:, g, hh * 512:(hh + 1) * 512], in0=bk[:, :],
                       in1=gate_bc[:, g, hh * 512:(hh + 1) * 512], op=ALU.add)
        for j in (1, 3):
            op("dve", "tensor_scalar", [B_mod], [B_mod], out=modcol[:, j, :], in0=modcol[:, j, :], scalar1=1.0, scalar2=None, op0=ALU.add)

        def ln_stats(srcs, nsub, k):
            stats, mv, lnv, rstd, nbias, B_ln = statss[k], mvs[k], lnvs[k], rstds[k], nbiass[k], B_lns[k]
            for s, (ap, b) in enumerate(srcs):
                for hlf in range(2):
                    op("dve", "bn_stats", [b], [B_ln], out=stats[:, s, hlf, :], in_=ap[:, hlf * 512:(hlf + 1) * 512])
                op("dve", "bn_aggr", [B_ln], [B_ln], out=mv[:, s, :], in_=stats[:, s, :, :].rearrange("p a b -> p (a b)"))
            op("act", "activation", [B_ln, B_eps], [B_ln], out=lnv[:, 0:nsub], in_=mv[:, 0:nsub, 1], func=AF.Ln, bias=eps_ln[:, 0:1], scale=1.0)
            op("act", "activation", [B_ln], [B_ln], out=rstd[:, 0:nsub], in_=lnv[:, 0:nsub], func=AF.Exp, scale=-0.5)
            op("dve", "scalar_tensor_tensor", [B_ln], [B_ln], out=nbias[:, 0:nsub], in0=mv[:, 0:nsub, 0], scalar=-1.0,
               in1=rstd[:, 0:nsub], op0=ALU.mult, op1=ALU.mult)

        epsc = sb("epsc", [128, 4]); B_eps = Buf("eps")
        op("pool", "memset", [], [B_eps], epsc[:, 0:1], LN_EPS)
        op("pool", "memset", [], [B_eps], epsc[:, 1:2], RMS_EPS)
        op("pool", "memset", [], [B_eps], epsc[:, 2:3], 1.0)
        eps_ln = epsc[:, 0:1]
        eps_rms = epsc[:, 1:2]
        one_col = epsc[:, 2:3]

        def to_featmajor(srcs, nsub, W, jmod, k):
            ln_stats(srcs, nsub, k)
            xn, B_xn, hT, B_hT, rstd, nbias, B_ln = xns[k], B_xns[k], hTs[k], B_hTs[k], rstds[k], nbiass[k], B_lns[k]
            for s, (ap, b) in enumerate(srcs):
                op("act", "activation", [b, B_ln], [B_xn[s]], out=xn[:, s, :], in_=ap, func=AF.Identity,
                   scale=rstd[:, s:s + 1], bias=nbias[:, s:s + 1])
            for kc in range(KC):
                bk, bb = bank(k)
                bkb = bk[:, :].bitcast(BF16)
                for s in range(nsub):
                    op("pe", "transpose", [B_xn[s], B_cbf], [bb], bkb[:, s * 128:(s + 1) * 128],
                       xn[:, s, kc * 128:(kc + 1) * 128], ident_bf[:, :])
                if kc % 2 == 0:
                    op("act", "activation", [bb, B_mod], [B_hT[kc]], out=hT[:, kc, 0:W], in_=bkb[:, 0:W], func=AF.Identity,
                       scale=modcol[:, jmod + 1, kc:kc + 1], bias=modcol[:, jmod, kc:kc + 1])
                else:
                    op("dve", "tensor_scalar", [bb, B_mod], [B_hT[kc]], out=hT[:, kc, 0:W], in0=bkb[:, 0:W],
                       scalar1=modcol[:, jmod + 1, kc:kc + 1], scalar2=modcol[:, jmod, kc:kc + 1], op0=ALU.mult, op1=ALU.add)

        def rope_tables(pos_ap, W):
            dma("sp", posi[:, 0:W], pos_ap.partition_broadcast(128), [], [B_pos], "pos")
            op("dve", "tensor_copy", [B_pos], [B_ang], out=angf[:, 0:W], in_=posi[:, 0:W])
            op("dve", "tensor_scalar", [B_ang, B_cst], [B_ang], out=angf[:, 0:W], in0=angf[:, 0:W], scalar1=freq_col, scalar2=None, op0=ALU.mult)
            op("dve", "tensor_scalar", [B_ang], [B_ang], out=angt[:, 0:W], in0=angf[:, 0:W], scalar1=float(1.0 / TWO_PI), scalar2=None, op0=ALU.mult)
            op("dve", "tensor_copy", [B_ang], [B_ang], out=angk[:, 0:W], in_=angt[:, 0:W])
            op("dve", "tensor_copy", [B_ang], [B_ang], out=angkf[:, 0:W], in_=angk[:, 0:W])
            op("dve", "scalar_tensor_tensor", [B_ang], [B_ang], out=angf[:, 0:W], in0=angkf[:, 0:W], scalar=-CW1, in1=angf[:, 0:W], op0=ALU.mult, op1=ALU.add)
            op("dve", "scalar_tensor_tensor", [B_ang], [B_ang], out=angf[:, 0:W], in0=angkf[:, 0:W], scalar=-CW2, in1=angf[:, 0:W], op0=ALU.mult, op1=ALU.add)
            for shift, dst, use_sign in ((0.0, sinT, True), (float(np.pi / 2), cosT, False)):
                src = angf
                if shift != 0.0:
                    op("dve", "tensor_scalar", [B_ang], [B_ang], out=angkf[:, 0:W], in0=angf[:, 0:W], scalar1=shift, scalar2=None, op0=ALU.add)
                    src = angkf
                for _ in range(2):
                    op("dve", "tensor_scalar", [B_ang], [B_ang], out=angt[:, 0:W], in0=src[:, 0:W], scalar1=float(np.pi), scalar2=float(-TWO_PI), op0=ALU.is_gt, op1=ALU.mult)
                    op("dve", "tensor_tensor", [B_ang], [B_ang], out=src[:, 0:W], in0=src[:, 0:W], in1=angt[:, 0:W], op=ALU.add)
                    op("dve", "tensor_scalar", [B_ang], [B_ang], out=angt[:, 0:W], in0=src[:, 0:W], scalar1=float(-np.pi), scalar2=float(TWO_PI), op0=ALU.is_lt, op1=ALU.mult)
                    op("dve", "tensor_tensor", [B_ang], [B_ang], out=src[:, 0:W], in0=src[:, 0:W], in1=angt[:, 0:W], op=ALU.add)
                if use_sign:
                    op("act", "activation", [B_ang, B_cst], [B_cs], out=dst[:, 0:W], in_=src[:, 0:W], func=AF.Sin, scale=sign_col)
                else:
                    op("act", "activation", [B_ang], [B_cs], out=dst[:, 0:W], in_=src[:, 0:W], func=AF.Sin)

        def rope(bk, bb, W, dst_ap, dst_buf):
            rope_tb, B_rtb = R_rtb.next(); rope_t1, B_rt1 = R_rt1.next(); rope_t2, B_rt2 = R_rt2.next()
            op("act", "activation", [bb], [B_rtb], out=rope_tb[:, 0:W], in_=bk[:, 0:W], func=AF.Copy)
            b2, bb2 = bank()
            op("pe", "matmul", [B_rtb, B_cbf], [bb2], b2[:, 0:W], lhsT=perm_bf[:, :], rhs=rope_tb[:, 0:W], start=True, stop=True)
            op("dve", "tensor_tensor", [bb, B_cs], [B_rt1], out=rope_t1[:, 0:W], in0=bk[:, 0:W], in1=cosT[:, 0:W], op=ALU.mult)
            op("dve", "tensor_tensor", [bb2, B_cs], [B_rt2], out=rope_t2[:, 0:W], in0=b2[:, 0:W], in1=sinT[:, 0:W], op=ALU.mult)
            op(ROPE_ADD_ENG, "tensor_tensor", [B_rt1, B_rt2], [dst_buf], out=dst_ap, in0=rope_t1[:, 0:W], in1=rope_t2[:, 0:W], op=ALU.add)

        def proj_fm(sl, bsl, c0, W, k=0, nrows=128):
            bk, bb = bank(k)
            for kc in range(KC):
                op("pe", "matmul", [bsl, B_hTs[k][kc]], [bb], bk[0:nrows, 0:W], lhsT=sl[:, kc, c0:c0 + nrows], rhs=hTs[k][:, kc, 0:W],
                   start=(kc == 0), stop=(kc == KC - 1))
            return bk, bb

        def proj_tm(sl, bsl, s, ncols=512, k=0):
            bk, bb = bank(k)
            for kc in range(KC):
                op("pe", "matmul", [bsl, B_hTs[k][kc]], [bb], bk[:, 0:ncols], lhsT=hTs[k][:, kc, s * 128:(s + 1) * 128], rhs=sl[:, kc, 0:ncols],
                   start=(kc == 0), stop=(kc == KC - 1))
            return bk, bb

        def mixer_gen(main, x_src, pos_ap, W, par, first_main_after_warm, last_pre, tag='', k=0):
            nsub = W // 128
            xbuf, B_x = xbufs[par], B_xs[par]
            hT, B_hT = hTs[k], B_hTs[k]
            lraT, B_lra = lraTs[k], B_lras[k]
            E2x, B_E2x = (E2, E1)[k], (B_E2, B_E1)[k]
            c_lastx, B_csx = (c_last, c_mid)[k], (B_cs_h, B_cs2_h)[k]
            koutx, B_koutx = (kout, kmid)[k], (B_kout, B_kmid)[k]
            vax, B_vax = (va, r1)[k], (B_va, B_r1)[k]
            kTMx, B_kTMx = (koutTM, AT)[k], (B_kTM, B_AT)[k]
            need_b = main or last_pre
            dma("sp", xbuf[:, 0:nsub, :], x_src.rearrange("(s p) d -> p s d", p=128), [], B_x[0:nsub], "x%d" % par)
            if main:
                op("pool", "tensor_copy", [B_kTb], [B_kTb], out=kTb[:, 0:128], in_=kTb[:, prev_w[0]:prev_w[0] + 128])
                op("pool", "tensor_copy", [B_vbx[prev_w[0] // 128]], [B_vbx[0]], out=vbx[:, 0, :, :], in_=vbx[:, prev_w[0] // 128, :, :])
            if first_main_after_warm:
                op("pool", "tensor_scalar", [B_vbx[0], B_flag], [B_vbx[0]], out=vbx[:, 0, :, :], in0=vbx[:, 0, :, :], scalar1=flag[:, 0:1], scalar2=None, op0=ALU.mult)
                for h in range(4):
                    op("pool", "tensor_scalar", [B_S[h], B_flag], [B_S[h]], out=S[:, h, :], in0=S[:, h, :], scalar1=flag[:, 0:1], scalar2=None, op0=ALU.mult)
                    op("pool", "tensor_copy", [B_S[h]], [B_Sb[h]], out=Sb[:, h, :], in_=S[:, h, :])
            prev_w[0] = W
            if need_b:
                rope_tables(pos_ap, W)
            yield
            ckpt(tag + ':A')
            to_featmajor([(xbuf[:, s, :], B_x[s]) for s in range(nsub)], nsub, W, 0, k)
            ckpt(tag + ':B')
            yield
            bk, bb = bank(k)
            for kc in range(KC):
                op("pe", "matmul", [B_wsm, B_hT[kc]], [bb], bk[0:16, 0:W], lhsT=w_sm[:, kc, 0:16], rhs=hT[:, kc, 0:W], start=(kc == 0), stop=(kc == KC - 1))
            op("act", "activation", [bb], [B_lra], out=lraT[:, 0:W], in_=bk[0:16, 0:W], func=AF.Copy)
            for h in range(4):
                bk, bb = bank(k)
                etmp, B_et = R_et.next(); ltmp, B_lt = R_lt.next(); Ltmp, B_Lt = R_Lt.next()
                op("pe", "matmul", [B_wlr, B_lra], [bb], bk[:, 0:W], lhsT=w_lr[:, h * 128:(h + 1) * 128], rhs=lraT[:, 0:W], start=True, stop=True)
                op("act", "activation", [bb, B_negb], [B_et], out=etmp[:, 0:W], in_=bk[:, 0:W], func=AF.Exp, scale=-1.0, bias=negb[:, h:h + 1])
                op("act", "activation", [B_et, B_eps], [B_lt], out=ltmp[:, 0:W], in_=etmp[:, 0:W], func=AF.Ln, bias=one_col, scale=1.0)
                for s in range(nsub):
                    op("dve", "tensor_tensor_scan", [B_lt, B_cst], [B_Lt], out=Ltmp[:, s * 128:(s + 1) * 128], data0=rmask128,
                       data1=ltmp[:, s * 128:(s + 1) * 128], initial=0.0, op0=ALU.mult, op1=ALU.add)
                op("act", "activation", [B_Lt], [B_E2x[h]], out=E2x[:, h, 0:W], in_=Ltmp[:, 0:W], func=AF.Exp, scale=1.0 / 16.0)
                L3 = Ltmp[:, 0:W].rearrange("p (s t) -> p s t", t=128)
                op("act", "activation", [B_Lt], [B_csx[h]], out=c_lastx[:, h, 0:nsub], in_=L3[:, :, 127], func=AF.Exp, scale=-1.0 / 16.0)
                if main:
                    op("act", "activation", [B_Lt], [B_E1[h]], out=E1[:, h, 0:W], in_=Ltmp[:, 0:W], func=AF.Exp, scale=-1.0 / 16.0)
                    op("act", "activation", [B_Lt], [B_cs_h[h]], out=c_mid[:, h, 0:nsub], in_=L3[:, :, 63], func=AF.Exp, scale=-1.0 / 16.0)
                    op("act", "activation", [B_Lt], [B_cs_h[h]], out=c_midn[:, h, 0:nsub], in_=L3[:, :, 63], func=AF.Exp, scale=1.0 / 16.0)
            ckpt(tag + ':C1')
            yield
            if need_b:
              bk, bb = bank(k)
              for kc in range(KC):
                op("pe", "matmul", [B_wsm, B_hT[kc]], [bb], bk[:, 0:W], lhsT=w_sm[:, kc, 16:144], rhs=hT[:, kc, 0:W], start=(kc == 0), stop=(kc == KC - 1))
              rope(bk, bb, W, kTb[:, 128:128 + W], B_kTb)
            ckpt(tag + ':C2')
            if main:
                sl, bsl = load_slab(s_fm, B_scr["fm"], 0, KC, 0)
                for h in range(4):
                    bk, bb = proj_fm(sl, bsl, h * 128, W, k)
                    ftmp, B_ft = R_ft.next()
                    op("dve", "scalar_tensor_tensor", [bb, B_E1[h]], [B_ft], out=ftmp[:, 0:W], in0=bk[:, 0:W], scalar=float(128 ** -0.5),
                       in1=E1[:, h, 0:W], op0=ALU.mult, op1=ALU.mult)
                    op("pool", "tensor_copy", [B_ft], [B_qin[h]], out=qin[:, h, 0:W], in_=ftmp[:, 0:W])
                    op("pool", "tensor_tensor", [B_ft, B_cs_h[h]], [B_qmid[h]], out=qmid[:, h, 0:W].rearrange("p (s t) -> p s t", t=128),
                       in0=ftmp[:, 0:W].rearrange("p (s t) -> p s t", t=128),
                       in1=c_midn[:, h, 0:nsub].unsqueeze(2).to_broadcast([128, nsub, 128]), op=ALU.mult)
            ckpt(tag + ':C3')
            yield
            sl, bsl = load_slab(s_fm, B_scr["fm"], 0, KC, 512, k=k)
            for h in range(4):
                bk, bb = proj_fm(sl, bsl, h * 128, W, k)
                ftmp2, B_ft2 = R_ft2.next()
                op("dve", "tensor_tensor", [bb, B_E2x[h]], [B_ft2], out=ftmp2[:, 0:W], in0=bk[:, 0:W], in1=E2x[:, h, 0:W], op=ALU.mult)
                op("pool", "tensor_tensor", [B_ft2, B_csx[h]], [B_koutx[h]], out=koutx[:, h, 0:W].rearrange("p (s t) -> p s t", t=128),
                   in0=ftmp2[:, 0:W].rearrange("p (s t) -> p s t", t=128),
                   in1=c_lastx[:, h, 0:nsub].unsqueeze(2).to_broadcast([128, nsub, 128]), op=ALU.mult)
                if main:
                    op("pool", "tensor_tensor", [B_ft2, B_cs_h[h]], [B_kmid[h]], out=kmid[:, h, 0:W].rearrange("p (s t) -> p s t", t=128),
                       in0=ftmp2[:, 0:W].rearrange("p (s t) -> p s t", t=128),
                       in1=c_mid[:, h, 0:nsub].unsqueeze(2).to_broadcast([128, nsub, 128]), op=ALU.mult)
            ckpt(tag + ':C4')
            yield
            if main:
                for j in range(2):
                    sl, bsl = load_slab(s_fm, B_scr["fm"], 0, KC, 1024 + j * 512)
                    for cc in range(4):
                        c = j * 4 + cc
                        bk, bb = proj_fm(sl, bsl, cc * 128, W)
                        ckpt(tag + ':Q%dp' % c)
                        rope(bk, bb, W, qTb[:, c, 0:W], B_qTb[c])
                        ckpt(tag + ':Q%d' % c)
                    yield
            ckpt(tag + ':C')
            yield
            for j in range(2):
                sl, bsl = load_slab(s_tm, B_scr["tm"], 0, KC, j * 512, k=k)
                for s in range(nsub):
                    bk, bb = proj_tm(sl, bsl, s, 512, k)
                    op("act", "activation", [bb], [B_vax[s]], out=vax[:, s, j * 512:(j + 1) * 512], in_=bk[:, :], func=AF.Copy)
            yield
            for s in (range(nsub) if need_b else ()):
                bk, bb = bank(k)
                for kc in range(KC):
                    op("pe", "matmul", [B_wsm, B_hT[kc]], [bb], bk[:, 0:128], lhsT=hT[:, kc, s * 128:(s + 1) * 128], rhs=w_sm[:, kc, 144:272],
                       start=(kc == 0), stop=(kc == KC - 1))
                op("dve", "tensor_copy", [bb], [B_vbx[1 + s]], out=vbx[:, 1 + s, :, 0:64], in_=bk[:, 0:128].rearrange("p (g d) -> p g d", g=2))
            if main:
                for j in range(2):
                    sl, bsl = load_slab(s_tm, B_scr["tm"], 0, KC, 1024 + j * 512)
                    for s in range(nsub):
                        bk, bb = proj_tm(sl, bsl, s, 512, k)
                        siltmp, B_sil = R_sil.next()
                        op("act", "activation", [bb], [B_sil], out=siltmp[:, :], in_=bk[:, :], func=AF.Silu)
                        op("pool", "tensor_tensor", [B_sil, B_gn], [B_r1[s]], out=r1[:, s, j * 512:(j + 1) * 512].rearrange("p (a v) -> p a v", a=2),
                           in0=siltmp[:, :].rearrange("p (a v) -> p a v", a=2),
                           in1=gnorm_bc[:, :].unsqueeze(1).to_broadcast([128, 2, 256]), op=ALU.mult)
                    yield
                for gi, (dstt, dstb) in enumerate(((sa, B_sa), (sbg, B_sbg))):
                    for j in range(2):
                        sl, bsl = load_slab(s_tm, B_scr["tm"], 0, KC, 2048 + gi * 1024 + j * 512)
                        for s in range(nsub):
                            bk, bb = proj_tm(sl, bsl, s, 512, k)
                            op("act", "activation", [bb], [dstb[s]], out=dstt[:, s, j * 512:(j + 1) * 512], in_=bk[:, :], func=AF.Sigmoid)
                        yield
                wo_slabs = [load_slab(s_o, B_scr["o"], 0, KC, j * 512) for j in range(2)]
            ckpt(tag + ':D')
            yield
            for s in range(nsub):
                t0 = s * 128
                if main:
                    bA, bbA = bank(k)
                    for h in range(4):
                        op("pe", "matmul", [B_kmid[h], B_qmid[h]], [bbA], bA[:, h * 128:(h + 1) * 128], lhsT=kmid[:, h, t0:t0 + 128],
                           rhs=qmid[:, h, t0:t0 + 128], start=True, stop=True)
                    op("dve", "tensor_tensor", [bbA, B_cbf], [B_AT], out=AT[:, :, :], in0=bA[:, :].rearrange("p (h t) -> p h t", h=4),
                       in1=mcur_bf[:, :].unsqueeze(1).to_broadcast([128, 4, 128]), op=ALU.mult)
                bT, bbT = bank(k)
                bTb = bT[:, :].bitcast(BF16)
                for h in range(4):
                    op("pe", "transpose", [B_koutx[h], B_cbf], [bbT], bTb[:, h * 128:(h + 1) * 128], koutx[:, h, t0:t0 + 128], ident_bf[:, :])
                op("act", "activation", [bbT], [B_kTMx], out=kTMx[:, :, :], in_=bTb[:, 0:512].rearrange("p (h t) -> p h t", h=4), func=AF.Copy)
                if main:
                    obanks = [bank(k), bank(k)]
                    for h in range(4):
                        bo, bbo = obanks[h // 2]
                        oap = bo[:, (h % 2) * 256:(h % 2) * 256 + 256]
                        op("pe", "matmul", [B_AT, B_va[s]], [bbo], oap, lhsT=AT[:, h, :], rhs=va[:, s, h * 256:(h + 1) * 256], start=True, stop=False)
                        op("pe", "matmul", [B_qin[h], B_Sb[h]], [bbo], oap, lhsT=qin[:, h, t0:t0 + 128], rhs=Sb[:, h, :], start=False, stop=True)
                sbanks = [bank(k), bank(k)]
                for h in range(4):
                    bs_, bbs = sbanks[h // 2]
                    sap = bs_[:, (h % 2) * 256:(h % 2) * 256 + 256]
                    op("pe", "matmul", [B_kTMx, B_vax[s]], [bbs], sap, lhsT=kTMx[:, h, :], rhs=vax[:, s, h * 256:(h + 1) * 256], start=True, stop=True)
                    op("dve", "scalar_tensor_tensor", [bbs, B_S[h], B_csx[h]], [B_S[h]], out=S[:, h, :], in0=S[:, h, :], scalar=c_lastx[:, h, s:s + 1],
                       in1=sap, op0=ALU.mult, op1=ALU.add)
                    op("pool", "tensor_copy", [B_S[h]], [B_Sb[h]], out=Sb[:, h, :], in_=S[:, h, :])
                if not main:
                    continue
                ckpt(tag + ':GLA0')
                yield
                for h in range(4):
                    bo, bbo = obanks[h // 2]
                    oap = bo[:, (h % 2) * 256:(h % 2) * 256 + 256]
                    sqj, B_sqj = R_sqj.next()
                    op("act", "activation", [bbo], [B_sqj, B_rms], out=sqj[:, :], in_=oap, func=AF.Square, accum_out=ssq[:, h:h + 1])
                op("act", "activation", [B_rms, B_eps], [B_rms], out=lnms[:, :], in_=ssq[:, :], func=AF.Ln, scale=1.0 / 256.0, bias=eps_rms)
                op("act", "activation", [B_rms], [B_rms], out=rstd_a[:, :], in_=lnms[:, :], func=AF.Exp, scale=-0.5)
                for h in range(4):
                    bo, bbo = obanks[h // 2]
                    oap = bo[:, (h % 2) * 256:(h % 2) * 256 + 256]
                    op("dve", "scalar_tensor_tensor", [bbo, B_rms, B_r1[s]], [B_ya], out=ya[:, h * 256:(h + 1) * 256], in0=oap, scalar=rstd_a[:, h:h + 1],
                       in1=r1[:, s, h * 256:(h + 1) * 256], op0=ALU.mult, op1=ALU.mult)
                ckpt(tag + ':GLA')
                yield
                for kb in range(2):
                    kcol = t0 + kb * 128
                    mk = mprev_bf if kb == 0 else mcur_bf
                    for half in range(2):
                        gb = [bank(k), bank(k)]
                        for sl_ in range(4):
                            c = half * 4 + sl_
                            for g in range(2):
                                bk, bb = gb[g]
                                op("pe", "matmul", [B_kTb, B_qTb[c]], [bb], bk[:, sl_ * 128:(sl_ + 1) * 128], lhsT=kTb[g * 64:(g + 1) * 64, kcol:kcol + 128],
                                   rhs=qTb[g * 64:(g + 1) * 64, c, t0:t0 + 128], start=True, stop=True)
                        for g in range(2):
                            bk, bb = gb[g]
                            pi = kb * 4 + half * 2 + g
                            op("act", "activation", [bb], [B_P[pi]], out=Pt[:, pi, :], in_=bk[:, :], func=AF.Exp, scale=0.125)
                            op("pool", "tensor_tensor", [B_P[pi], B_cbf], [B_P[pi]], out=Pt[:, pi, :].rearrange("p (a t) -> p a t", a=4),
                               in0=Pt[:, pi, :].rearrange("p (a t) -> p a t", a=4), in1=mk[:, :].unsqueeze(1).to_broadcast([128, 4, 128]), op=ALU.mult)
                ckpt(tag + ':SC')
                yield
                ebanks = [bank(k), bank(k), bank(k)]
                for hd in range(16):
                    g = hd // 8
                    c = hd % 8
                    half, sl_ = c // 4, c % 4
                    be, bbe = ebanks[hd // 7]
                    eap = be[:, (hd % 7) * 65:(hd % 7) * 65 + 65]
                    for kb in range(2):
                        pi = kb * 4 + half * 2 + g
                        op("pe", "matmul", [B_P[pi], B_vbx[s + kb]], [bbe], eap, lhsT=Pt[:, pi, sl_ * 128:(sl_ + 1) * 128],
                           rhs=vbx[:, s + kb, g, :], start=(kb == 0), stop=(kb == 1))
                for gi, (h0, nh) in enumerate(((0, 7), (7, 7), (14, 2))):
                    be, bbe = ebanks[gi]
                    e3 = be[:, 0:nh * 65].rearrange("p (h d) -> p h d", d=65)
                    op("dve", "tensor_tensor", [bbe, B_es], [B_den], out=den[:, h0:h0 + nh], in0=e3[:, :, 64], in1=esink[:, h0:h0 + nh], op=ALU.add)
                    op("dve", "reciprocal", [B_den], [B_den], out=rden[:, h0:h0 + nh], in_=den[:, h0:h0 + nh])
                    op("dve", "tensor_tensor", [bbe, B_den], [B_tb], out=tbv[:, h0 * 64:(h0 + nh) * 64].rearrange("p (h d) -> p h d", d=64),
                       in0=e3[:, :, 0:64], in1=rden[:, h0:h0 + nh].unsqueeze(2).to_broadcast([128, nh, 64]), op=ALU.mult)
                ckpt(tag + ':SWA')
                yield
                op("pool", "tensor_tensor", [B_ya, B_sa[s]], [B_ya], out=ya[:, :], in0=ya[:, :], in1=sa[:, s, :], op=ALU.mult)
                op("pool", "tensor_tensor", [B_tb, B_sbg[s]], [B_tb], out=tbv[:, :], in0=tbv[:, :], in1=sbg[:, s, :], op=ALU.mult)
                op("pool", "tensor_tensor", [B_ya, B_tb], [B_y], out=ybf[:, :], in0=ya[:, :], in1=tbv[:, :], op=ALU.add)
                bk, bb = bank(k)
                bkb = bk[:, :].bitcast(BF16)
                for kc in range(KC):
                    op("pe", "transpose", [B_y, B_cbf], [bb], bkb[:, kc * 128:(kc + 1) * 128], ybf[:, kc * 128:(kc + 1) * 128], ident_bf[:, :])
                op("act", "activation", [bb], [B_yT], out=yT[:, :, :], in_=bkb[:, :].rearrange("p (k t) -> p k t", k=KC), func=AF.Copy)
                for j in range(2):
                    sl, bsl = wo_slabs[j]
                    bk, bb = bank(k)
                    for kc in range(KC):
                        op("pe", "matmul", [B_yT, bsl], [bb], bk[:, :], lhsT=yT[:, kc, :], rhs=sl[:, kc, :], start=(kc == 0), stop=(kc == KC - 1))
                    op("dve", "tensor_tensor", [bb, B_gate], [B_tres], out=tres[:, j * 512:(j + 1) * 512], in0=bk[:, :], in1=gate_bc[:, 0, j * 512:(j + 1) * 512], op=ALU.mult)
                op("dve", "scalar_tensor_tensor", [B_x[s], B_tres], [B_x[s]], out=xbuf[:, s, :], in0=xbuf[:, s, :], scalar=float(ALPHA), in1=tres[:, :], op0=ALU.mult, op1=ALU.add)
            if not main:
                return
            ckpt(tag + ':WO')
            yield
            ln_affine(nsub, 0, xbuf, B_x, 0)
            ckpt(tag + ':LN')
            yield

        def ffn_gen(W, par, out_ap, scale_halo, tag=''):
            nsub = W // 128
            xbuf, B_x = xbufs[par], B_xs[par]
            if scale_halo:
                op("pool", "tensor_scalar", [B_halo, B_flag], [B_halo], out=halo[:, :, :], in0=halo[:, :, :], scalar1=flag[:, 0:1], scalar2=None, op0=ALU.mult)
            to_featmajor([(xbuf[:, s, :], B_x[s]) for s in range(nsub)], nsub, W, 2, 1)
            yield
            for j in range(11):
                sl, bsl = load_slab(s_up, B_scr["up"], 0, KC, j * 512, k=1)
                for e in range(4):
                    qi = 4 * j + e
                    bk, bb = proj_fm(sl, bsl, e * 128, W, 1)
                    op("act", "activation", [bb, B_cw], [B_acc[e]], out=acc[:, e, 0:W], in_=bk[:, 0:W], func=AF.Identity,
                       scale=cw_col[:, 2 * NUP + qi:2 * NUP + qi + 1], bias=cb_col[:, qi:qi + 1])
                    op("act", "activation", [bb], [B_halon], out=halo_new[:, qi, :], in_=bk[:, W - 2:W], func=AF.Copy)
                    op("dve", "scalar_tensor_tensor", [bb, B_cw, B_acc[e]], [B_acc[e]], out=acc[:, e, 1:W], in0=bk[:, 0:W - 1], scalar=cw_col[:, NUP + qi:NUP + qi + 1],
                       in1=acc[:, e, 1:W], op0=ALU.mult, op1=ALU.add)
                    op("dve", "scalar_tensor_tensor", [bb, B_cw, B_acc[e]], [B_acc[e]], out=acc[:, e, 2:W], in0=bk[:, 0:W - 2], scalar=cw_col[:, qi:qi + 1],
                       in1=acc[:, e, 2:W], op0=ALU.mult, op1=ALU.add)
                    op("dve", "scalar_tensor_tensor", [B_halo, B_cw, B_acc[e]], [B_acc[e]], out=acc[:, e, 0:2], in0=halo[:, qi, :], scalar=cw_col[:, qi:qi + 1],
                       in1=acc[:, e, 0:2], op0=ALU.mult, op1=ALU.add)
                    op("dve", "scalar_tensor_tensor", [B_halo, B_cw, B_acc[e]], [B_acc[e]], out=acc[:, e, 0:1], in0=halo[:, qi, 1:2], scalar=cw_col[:, NUP + qi:NUP + qi + 1],
                       in1=acc[:, e, 0:1], op0=ALU.mult, op1=ALU.add)
                for e in range(2):
                    op("act", "activation", [B_acc[e]], [B_gt[e]], out=gtmp[:, e, 0:W], in_=acc[:, e, 0:W], func=AF.Gelu)
                    op("pool", "tensor_tensor", [B_gt[e], B_acc[2 + e]], [B_f[j]], out=fbuf[:, 2 * j + e, 0:W], in0=gtmp[:, e, 0:W], in1=acc[:, 2 + e, 0:W], op=ALU.mult)
                yield
            ckpt(tag + ':UP')
            op("pool", "tensor_copy", [B_halon], [B_halo], out=halo[:, :, :], in_=halo_new[:, :, :])
            for hh in range(2):
                dbanks = [bank(1) for _ in range(nsub)]
                for kg in range(3):
                    nk = 8 if kg < 2 else 6
                    sl, bsl = load_slab(s_down, B_scr["down"], kg * 8, nk, hh * 512, k=1)
                    for s in range(nsub):
                        bk, bb = dbanks[s]
                        for kk in range(nk):
                            kc = kg * 8 + kk
                            op("pe", "matmul", [B_f[kc // 2], bsl], [bb], bk[:, :], lhsT=fbuf[:, kc, s * 128:(s + 1) * 128], rhs=sl[:, kk, :],
                               start=(kc == 0), stop=(kc == 21))
                    yield
                for s in range(nsub):
                    bk, bb = dbanks[s]
                    tb_ = B_acc[2 * hh]
                    op("dve", "tensor_tensor", [bb, B_gate, B_acc[2 * hh + 1]], [tb_, B_acc[2 * hh + 1]], out=tres_f[:, hh * 512:(hh + 1) * 512], in0=bk[:, :], in1=gate_bc[:, 1, hh * 512:(hh + 1) * 512], op=ALU.mult)
                    op("dve", "scalar_tensor_tensor", [B_x[s], tb_], [B_x[s]], out=xbuf[:, s, hh * 512:(hh + 1) * 512], in0=xbuf[:, s, hh * 512:(hh + 1) * 512],
                       scalar=float(ALPHA), in1=tres_f[:, hh * 512:(hh + 1) * 512], op0=ALU.mult, op1=ALU.add)
            ckpt(tag + ':DOWN')
            yield
            ln_affine(nsub, 2, xbuf, B_x, 1)
            if out_ap is not None:
                dma("sp", out_ap.rearrange("(s p) d -> p s d", p=128), xbuf[:, 0:nsub, :], B_x[0:nsub], [], "o%d" % par)
            yield

        tres_f = acc[:, :, :].rearrange("p e w -> p (e w)") if WM == 256 else sb("tres_f", [128, D])

        def ln_affine(nsub, gi, xbuf, B_x, k):
            ln_stats([(xbuf[:, s, :], B_x[s]) for s in range(nsub)], nsub, k)
            rstd, nbias, B_ln = rstds[k], nbiass[k], B_lns[k]
            for s in range(nsub):
                op("act", "activation", [B_x[s], B_ln], [B_x[s]], out=xbuf[:, s, :], in_=xbuf[:, s, :], func=AF.Identity,
                   scale=rstd[:, s:s + 1], bias=nbias[:, s:s + 1])
                op("pool", "tensor_tensor", [B_x[s], B_lnbc], [B_x[s]], out=xbuf[:, s, :], in0=xbuf[:, s, :], in1=ln_bc[:, gi, :], op=ALU.mult)
                op("pool", "tensor_tensor", [B_x[s], B_lnbc], [B_x[s]], out=xbuf[:, s, :], in0=xbuf[:, s, :], in1=ln_bc[:, gi + 1, :], op=ALU.add)

        def interleave(*gens):
            gens = [g for g in gens if g is not None]
            while gens:
                for g in list(gens):
                    try:
                        next(g)
                    except StopIteration:
                        gens.remove(g)

        prev_w = [128]
        ckpt("setup")
        r = 0
        pgens = []
        for i, W in enumerate(cfg.pre):
            pgens.append(mixer_gen(False, x_pre[r:r + W, :], pos_pre[:, r:r + W], W, i % 2, False, i == len(cfg.pre) - 1, 'pre%d' % i, k=i % 2))
            r += W
        for i in range(0, len(pgens), 2):
            if INTERLEAVE:
                interleave(*pgens[i:i + 2])
            else:
                for g in pgens[i:i + 2]:
                    interleave(g)
            ckpt("pre%d" % i)
        r = 0
        ro = 0
        pending = None
        for i, W in enumerate(cfg.main):
            oap = None
            if i > 0:
                oap = out_d[ro:ro + W, :]
                ro += W
            g1 = mixer_gen(True, x_main[r:r + W, :], pos_main[:, r:r + W], W, i % 2, i == 1, False, 'main%d' % i)
            if INTERLEAVE:
                interleave(g1, pending)
            else:
                interleave(pending)
                interleave(g1)
            pending = ffn_gen(W, i % 2, oap, i == 1, 'main%d' % i)
            r += W
            ckpt("main%d" % i)
        interleave(pending)

        with nc.Block() as block:
            pg.emit(nc, block, st)
    return nc


def make_consts():
    c = np.zeros((128, 5 * 128 + 2), np.float32)
    c[:, 0:128] = np.eye(128, dtype=np.float32)
    perm = np.zeros((128, 128), np.float32)
    for m in range(128):
        d = m % 64
        if d < 8:
            perm[m + 8, m] = 1.0
        elif d < 16:
            perm[m - 8, m] = 1.0
    c[:, 128:256] = perm
    j = np.arange(128)[:, None]
    i = np.arange(128)[None, :]
    c[:, 256:384] = (j <= i).astype(np.float32)
    c[:, 384:512] = (j > i).astype(np.float32)
    rm = np.ones((128, 128), np.float32)
    rm[:, 0] = 0.0
    c[:, 512:640] = rm
    inv_freq = (500000.0 ** (-(np.arange(0, 16, 2, dtype=np.float32) / 16.0))).astype(np.float32)
    for p in range(128):
        d = p % 64
        if d < 16:
            c[p, 640] = inv_freq[d % 8]
            c[p, 641] = -1.0 if d < 8 else 1.0
    return c


def host_inputs(cfg, x, c, positions, w_ada, b_ada, w_in, gla_w_lr, gla_b_lr, gla_norm_g, swa_sinks,
                w_o, ln1_g, ln1_b, w_up, conv_w, conv_b, w_down, ln2_g, ln2_b, core_map):
    f = np.float32
    w_in = np.asarray(w_in[0], f)
    sp = np.cumsum([512, 512, 1024, 1024, 16, 1024, 128, 128, 2048])[:-1]
    qa, ka, va_, ra, lra, qb, kb, vb, gates = np.split(w_in, sp, axis=1)
    qb_perm = np.concatenate([np.concatenate([qb[:, 64 * cc:64 * cc + 64], qb[:, 64 * (cc + 8):64 * (cc + 8) + 64]], axis=1) for cc in range(8)], axis=1)
    w_fm = np.ascontiguousarray(np.concatenate([qa, ka, qb_perm], axis=1))
    w_tm = np.ascontiguousarray(np.concatenate([va_, ra, gates], axis=1))
    w_sm = np.ascontiguousarray(np.concatenate([lra, kb, vb], axis=1))
    wu = np.asarray(w_up[0], f)
    cwv = np.asarray(conv_w[0], f)
    cbv = np.asarray(conv_b[0], f)
    order = []
    for j in range(11):
        order += list(range(256 * j, 256 * j + 256)) + list(range(DFF + 256 * j, DFF + 256 * j + 256))
    order = np.asarray(order)
    w_up_p = np.ascontiguousarray(wu[:, order])
    cw_p = cwv[:, order]
    cb_p = cbv[order]
    cw_col = np.ascontiguousarray(cw_p.reshape(3, NUP, 128).transpose(2, 0, 1).reshape(128, 3 * NUP))
    cb_col = np.ascontiguousarray(cb_p.reshape(NUP, 128).T)
    b_ada_v = np.asarray(b_ada[0], f)
    b_ada_col = np.ascontiguousarray(b_ada_v.reshape(6, KC, 128).transpose(2, 0, 1).reshape(128, 48))
    ln_rows = np.stack([np.asarray(a[0], f) for a in (ln1_g, ln1_b, ln2_g, ln2_b)])
    consts = make_consts()
    shared = {
        "w_ada": np.ascontiguousarray(np.asarray(w_ada[0], f)), "b_ada_col": b_ada_col, "b_ada_row": b_ada_v[None, :].copy(),
        "w_fm": w_fm, "w_tm": w_tm, "w_sm": w_sm, "w_lr": np.ascontiguousarray(np.asarray(gla_w_lr[0], f)),
        "b_lr_col": np.ascontiguousarray(np.asarray(gla_b_lr[0], f).reshape(4, 128).T),
        "gnorm": np.asarray(gla_norm_g[0], f)[None, :].copy(), "sinks": np.asarray(swa_sinks[0], f)[None, :].copy(),
        "w_o": np.ascontiguousarray(np.asarray(w_o[0], f)), "ln_rows": ln_rows, "w_up": w_up_p, "cw_col": cw_col, "cb_col": cb_col,
        "w_down": np.ascontiguousarray(np.asarray(w_down[0], f)), "consts": consts,
    }
    x = np.asarray(x, f)
    positions = np.asarray(positions, np.int32)
    c = np.asarray(c, f)
    maps = []
    for (b, start) in core_map:
        m = dict(shared)
        warm = cfg.main[0]
        if start > 0:
            assert start - warm == cfg.npre
            m["x_pre"] = np.ascontiguousarray(x[b, 0:cfg.npre])
            m["pos_pre"] = np.ascontiguousarray(positions[b, 0:cfg.npre][None, :])
            m["x_main"] = np.ascontiguousarray(x[b, start - warm:start + cfg.nout])
            m["pos_main"] = np.ascontiguousarray(positions[b, start - warm:start + cfg.nout][None, :])
            m["flag"] = np.ones((128, 1), f)
        else:
            m["x_pre"] = np.ascontiguousarray(x[b, 0:cfg.npre])
            m["pos_pre"] = np.ascontiguousarray(positions[b, 0:cfg.npre][None, :])
            m["x_main"] = np.ascontiguousarray(np.concatenate([x[b, 0:warm], x[b, 0:cfg.nout]], axis=0))
            m["pos_main"] = np.ascontiguousarray(np.concatenate([positions[b, 0:warm], positions[b, 0:cfg.nout]])[None, :])
            m["flag"] = np.zeros((128, 1), f)
        m["c_col"] = np.ascontiguousarray(c[b].reshape(KC, 128).T)
        maps.append(m)
    return maps


_NC_CACHE = {}


def kernel(x, c, positions, w_ada, b_ada, w_in, gla_w_lr, gla_b_lr, gla_norm_g, swa_sinks,
           w_o, ln1_g, ln1_b, w_up, conv_w, conv_b, w_down, ln2_g, ln2_b):
    cfg = FULL
    B, S, _ = x.shape
    half = S // 2
    core_map = [(b, hf * half) for b in range(B) for hf in range(2)]
    maps = host_inputs(cfg, x, c, positions, w_ada, b_ada, w_in, gla_w_lr, gla_b_lr, gla_norm_g, swa_sinks,
                       w_o, ln1_g, ln1_b, w_up, conv_w, conv_b, w_down, ln2_g, ln2_b, core_map)
    if "nc" not in _NC_CACHE:
        _NC_CACHE["nc"] = build(cfg)
    res = run_bass_kernel_spmd(_NC_CACHE["nc"], maps, core_ids=list(range(len(maps))))
    out = np.zeros((B, S, D), np.float32)
    for i, (b, start) in enumerate(core_map):
        out[b, start:start + cfg.nout] = np.asarray(res.results[i]["out"], np.float32)
    return out
```
